# Optimizing a Trainium2 kernel written in Bass

```python
import jax, jax.numpy as jnp
from jax import lax
import numpy as np

D_MODEL = 2048
BATCH = 4
SEQ = 2048
DEPTH = 1

D_MIX = D_MODEL
D_LRU = D_MIX // 2
D_RWKV = D_MIX - D_LRU
LRU_HEADS = 4
LRU_HEAD_DIM = D_LRU // LRU_HEADS
CONV_WIDTH = 4
LRU_C = 8.0
RWKV_HEAD_DIM = 64
RWKV_HEADS = D_RWKV // RWKV_HEAD_DIM
W_LORA = 64
A_LORA = 64
G_LORA = 160
RWKV_PROJ_W = 3 * D_RWKV + W_LORA + A_LORA + G_LORA
IN_W = 2 * D_LRU + RWKV_PROJ_W
D_FF = -(-8 * D_MODEL // (3 * 256)) * 256
N_MOD = 6
RMS_EPS = 1e-6
GN_EPS = 64e-5
L2_EPS = 1e-12

kernel_name = "hybrid_rglru_rwkv7_adaln_layer"

_IN_SPLITS = [D_LRU, 2 * D_LRU]
_RWKV_SPLITS = [int(s) for s in np.cumsum([D_RWKV, D_RWKV, D_RWKV, W_LORA, A_LORA])]


def rms_norm(x, g):
    xf = x.astype(jnp.float32)
    y = xf * lax.rsqrt(jnp.mean(xf * xf, axis=-1, keepdims=True) + RMS_EPS)
    return (y * g.astype(jnp.float32)).astype(x.dtype)


def modulate(h, shift, scale):
    return h * (1.0 + scale[:, None, :]) + shift[:, None, :]


def shift_prev(p):
    return jnp.pad(p[:, :-1], ((0, 0), (1, 0), (0, 0)))


def causal_depthwise_conv(u, w, b):
    S = u.shape[1]
    up = jnp.pad(u, ((0, 0), (CONV_WIDTH - 1, 0), (0, 0)))
    y = b
    for k in range(CONV_WIDTH):
        y = y + up[:, k:k + S] * w[k]
    return y


def _linear_scan_combine(c1, c2):
    a1, b1 = c1
    a2, b2 = c2
    return a1 * a2, a2 * b1 + b2


def rg_lru(u, wa, ba, wx, bx, lam):
    B, S, _ = u.shape
    uf = u.astype(jnp.float32)
    uh = uf.reshape(B, S, LRU_HEADS, LRU_HEAD_DIM)
    r = jax.nn.sigmoid(jnp.einsum('bshi,hij->bshj', uh, wa.astype(jnp.float32)).reshape(B, S, D_LRU) + ba)
    i = jax.nn.sigmoid(jnp.einsum('bshi,hij->bshj', uh, wx.astype(jnp.float32)).reshape(B, S, D_LRU) + bx)
    log_a = -LRU_C * r * jax.nn.softplus(-lam.astype(jnp.float32))
    a = jnp.exp(log_a)
    mult = jnp.sqrt(1.0 - jnp.exp(2.0 * log_a))
    is_first = (jnp.arange(S) == 0)[None, :, None]
    mult = jnp.where(is_first, 1.0, mult)
    b = mult * (i * uf)
    _, h = lax.associative_scan(_linear_scan_combine, (a, b), axis=1)
    return h.astype(u.dtype)


def rwkv7_recurrence(r, log_decay, k, v, kk, a):
    B, S, H, N = r.shape
    decay = jnp.exp(log_decay)

    def step(state, inp):
        r_t, d_t, k_t, v_t, kk_t, a_t = inp
        sa = jnp.einsum('bhvk,bhk->bhv', state, -kk_t)
        state = (state * d_t[:, :, None, :]
                 + sa[..., None] * (kk_t * a_t)[:, :, None, :]
                 + v_t[..., None] * k_t[:, :, None, :])
        y_t = jnp.einsum('bhvk,bhk->bhv', state, r_t)
        return state, y_t

    xs = tuple(jnp.moveaxis(t, 1, 0) for t in (r, decay, k, v, kk, a))
    s0 = jnp.zeros((B, H, N, N), jnp.float32)
    _, y = lax.scan(step, s0, xs)
    return jnp.moveaxis(y, 0, 1)


def rwkv7_mix(p, mu, w0, w2, a0, a2, g2, k_k, k_a, r_k, ln_g, ln_b):
    B, S, _ = p.shape
    dt = p.dtype
    p = p + (shift_prev(p) - p) * mu
    r, k, v, wl, al, gl = jnp.split(p, _RWKV_SPLITS, axis=-1)
    w = -jax.nn.softplus(-(w0 + jnp.tanh(wl) @ w2)) - 0.5
    log_decay = -jnp.exp(w.astype(jnp.float32))
    a = jax.nn.sigmoid(a0 + al @ a2)
    g = jax.nn.sigmoid(gl) @ g2
    hs = (B, S, RWKV_HEADS, RWKV_HEAD_DIM)
    kk = (k * k_k).astype(jnp.float32).reshape(hs)
    kk = kk / jnp.maximum(jnp.linalg.norm(kk, axis=-1, keepdims=True), L2_EPS)
    k = k * (1.0 + (a - 1.0) * k_a)
    rf = r.astype(jnp.float32).reshape(hs)
    kf = k.astype(jnp.float32).reshape(hs)
    vf = v.astype(jnp.float32).reshape(hs)
    af = a.astype(jnp.float32).reshape(hs)
    y = rwkv7_recurrence(rf, log_decay.reshape(hs), kf, vf, kk, af)
    mean = jnp.mean(y, axis=-1, keepdims=True)
    var = jnp.mean(jnp.square(y - mean), axis=-1, keepdims=True)
    y = ((y - mean) * lax.rsqrt(var + GN_EPS)).reshape(B, S, D_RWKV)
    y = y * ln_g.astype(jnp.float32) + ln_b.astype(jnp.float32)
    bonus = jnp.sum(rf * kf * r_k.astype(jnp.float32), axis=-1, keepdims=True) * vf
    y = y + bonus.reshape(B, S, D_RWKV)
    return (y * g.astype(jnp.float32)).astype(dt)


def setup_inputs(seed: int = 0) -> dict:
    key = jax.random.key(seed)
    ks = iter(jax.random.split(key, 32))
    nrm = lambda shape, s: jax.random.normal(next(ks), shape, jnp.float32) * s
    L = DEPTH
    u = jax.random.uniform(next(ks), (L, D_LRU), jnp.float32, 0.9, 0.999)
    base = u ** (1.0 / LRU_C)
    lru_lambda = jnp.log(base) - jnp.log1p(-base)
    return {
        "x": nrm((BATCH, SEQ, D_MODEL), 1.0),
        "c": nrm((BATCH, D_MODEL), 1.0),
        "w_ada": nrm((L, D_MODEL, N_MOD * D_MODEL), 0.5 * D_MODEL ** -0.5),
        "b_ada": nrm((L, N_MOD * D_MODEL), 0.02),
        "norm_mix_g": 1.0 + nrm((L, D_MODEL), 0.02),
        "w_in": nrm((L, D_MODEL, IN_W), D_MODEL ** -0.5),
        "conv_w": nrm((L, CONV_WIDTH, D_LRU), CONV_WIDTH ** -0.5),
        "conv_b": nrm((L, D_LRU), 0.01),
        "lru_wa": nrm((L, LRU_HEADS, LRU_HEAD_DIM, LRU_HEAD_DIM), LRU_HEAD_DIM ** -0.5),
        "lru_ba": nrm((L, D_LRU), 0.01),
        "lru_wx": nrm((L, LRU_HEADS, LRU_HEAD_DIM, LRU_HEAD_DIM), LRU_HEAD_DIM ** -0.5),
        "lru_bx": nrm((L, D_LRU), 0.01),
        "lru_lambda": lru_lambda,
        "rwkv_mu": jax.random.uniform(next(ks), (L, RWKV_PROJ_W), jnp.float32),
        "rwkv_w0": jax.random.uniform(next(ks), (L, D_RWKV), jnp.float32, -6.5, -1.5),
        "rwkv_w2": nrm((L, W_LORA, D_RWKV), W_LORA ** -0.5),
        "rwkv_a0": nrm((L, D_RWKV), 0.5),
        "rwkv_a2": nrm((L, A_LORA, D_RWKV), A_LORA ** -0.5),
        "rwkv_g2": nrm((L, G_LORA, D_RWKV), G_LORA ** -0.5),
        "rwkv_k_k": 0.85 + nrm((L, D_RWKV), 0.05),
        "rwkv_k_a": 1.0 + nrm((L, D_RWKV), 0.05),
        "rwkv_r_k": nrm((L, RWKV_HEADS, RWKV_HEAD_DIM), 0.1),
        "rwkv_ln_g": 1.0 + nrm((L, D_RWKV), 0.02),
        "rwkv_ln_b": nrm((L, D_RWKV), 0.01),
        "w_out": nrm((L, D_MIX, D_MODEL), D_MIX ** -0.5),
        "norm_ffn_g": 1.0 + nrm((L, D_MODEL), 0.02),
        "w_gu": nrm((L, D_MODEL, 2 * D_FF), D_MODEL ** -0.5),
        "w_down": nrm((L, D_FF, D_MODEL), D_FF ** -0.5),
        "final_norm_g": 1.0 + nrm((D_MODEL,), 0.02),
    }


def reference(x, c, w_ada, b_ada, norm_mix_g, w_in, conv_w, conv_b, lru_wa, lru_ba,
              lru_wx, lru_bx, lru_lambda, rwkv_mu, rwkv_w0, rwkv_w2, rwkv_a0, rwkv_a2,
              rwkv_g2, rwkv_k_k, rwkv_k_a, rwkv_r_k, rwkv_ln_g, rwkv_ln_b, w_out,
              norm_ffn_g, w_gu, w_down, final_norm_g):
    c_act = jax.nn.silu(c)
    for l in range(DEPTH):
        mod = c_act @ w_ada[l] + b_ada[l]
        sh_m, sc_m, g_m, sh_f, sc_f, g_f = jnp.split(mod, N_MOD, axis=-1)

        h = modulate(rms_norm(x, norm_mix_g[l]), sh_m, sc_m)
        p = h @ w_in[l]
        p_lru, p_gate, p_rwkv = jnp.split(p, _IN_SPLITS, axis=-1)
        u = causal_depthwise_conv(p_lru, conv_w[l], conv_b[l])
        y_a = rg_lru(u, lru_wa[l], lru_ba[l], lru_wx[l], lru_bx[l], lru_lambda[l]) * jax.nn.gelu(p_gate)
        y_b = rwkv7_mix(p_rwkv, rwkv_mu[l], rwkv_w0[l], rwkv_w2[l], rwkv_a0[l], rwkv_a2[l],
                        rwkv_g2[l], rwkv_k_k[l], rwkv_k_a[l], rwkv_r_k[l], rwkv_ln_g[l], rwkv_ln_b[l])
        mix = jnp.concatenate([y_a, y_b], axis=-1) @ w_out[l]
        x = x + g_m[:, None, :] * mix

        h = modulate(rms_norm(x, norm_ffn_g[l]), sh_f, sc_f)
        gate, up = jnp.split(h @ w_gu[l], 2, axis=-1)
        x = x + g_f[:, None, :] * ((jax.nn.silu(gate) * up) @ w_down[l])
    return rms_norm(x, final_norm_g)
```

```python
import numpy as np
from contextlib import ExitStack
import concourse.bass as bass
import concourse.mybir as mybir
from concourse.bass_utils import run_bass_kernel_spmd

F32 = mybir.dt.float32
BF16 = mybir.dt.bfloat16
AF = mybir.ActivationFunctionType
ALU = mybir.AluOpType
AX = mybir.AxisListType

D = 2048
T = 2048
TO = 1024
DFF = 5632
NWIN = 2848
C0 = float(np.exp(-0.5))
RMS_EPS = 1e-6
GN_EPS = 64e-5
L2_EPS = 1e-12
NDMASEM = 6


class Tr:
    def __init__(self):
        self.ops = []
        self.lw = {}
        self.rd = {}
        self.fenced = 0
        self.dbg_ops = []

    def op(self, eng, fn, r=(), w=(), dma=False, cc=False):
        i = len(self.ops)
        deps = set()
        psr = [t for t in r if isinstance(t, tuple) and t and t[0] == 'ps']
        if psr:
            r = [t for t in r if t not in psr]
            w = list(w) + psr
        for t in r:
            x = self.lw.get(t)
            if x is not None:
                deps.add(x)
        for t in w:
            x = self.lw.get(t)
            if x is not None:
                deps.add(x)
            for y in self.rd.get(t, ()):
                deps.add(y)
        deps.discard(i)
        for t in r:
            self.rd.setdefault(t, []).append(i)
        for t in w:
            self.lw[t] = i
            self.rd[t] = []
        self.ops.append(dict(eng=eng, fn=fn, deps=deps, dma=dma, cc=cc))
        return i

    def fence(self):
        last = {}
        dmas = []
        for i, o in enumerate(self.ops):
            if o['dma'] or o['cc']:
                dmas.append(i)
            elif o['fn'] is not None:
                last[o['eng']] = i
        deps = set(last.values()) | set(dmas[self.fenced:])
        self.fenced = len(dmas)
        for e in ['pe', 'act', 'dve', 'pool', 'sp']:
            i = self.op(e, None, (), ())
            self.ops[i]['deps'] = set(deps)

    def emit(self, nc, es, block, final_deps):
        ops = self.ops
        engs = ['pe', 'act', 'dve', 'pool', 'sp']
        csem = {e: es.enter_context(nc.semaphore("c_" + e)) for e in engs}
        dsem = {e: [es.enter_context(nc.semaphore("d_%s%d" % (e, k))) for k in range(NDMASEM)]
                for e in ['act', 'pool', 'sp']}
        ccsem = es.enter_context(nc.semaphore("ccs"))
        fin = self.op('sp', None, (), ())
        ops[fin]['deps'] = set(final_deps)
        used = set()
        for o in ops:
            for d in o['deps']:
                used.add(d)
        ccnt = {e: 0 for e in engs}
        dcnt = {e: 0 for e in dsem}
        dval = {}
        cccount = 0
        for i, o in enumerate(ops):
            e = o['eng']
            if o['cc']:
                cccount += 1
                o['sem'] = ccsem
                o['val'] = cccount
            elif o['dma']:
                k = dcnt[e] % NDMASEM
                dcnt[e] += 1
                s = dsem[e][k]
                prev = dval.get((e, k), 0)
                o['sem'] = s
                o['prev'] = prev
                o['val'] = prev + 16
                dval[(e, k)] = prev + 16
            elif o['fn'] is not None and i in used:
                ccnt[e] += 1
                o['sem'] = csem[e]
                o['val'] = ccnt[e]
            else:
                o['sem'] = None
        per = {e: [] for e in engs}
        for i, o in enumerate(ops):
            per[o['eng']].append(i)

        def run(e, eng):
            waited = {}

            def wait(sem, val):
                key = sem.num if hasattr(sem, 'num') else id(sem)
                if waited.get(key, 0) >= val:
                    return
                eng.wait_ge(sem, val)
                waited[key] = val

            for i in per[e]:
                o = ops[i]
                for d in sorted(o['deps']):
                    p = ops[d]
                    if p['sem'] is None:
                        continue
                    if p['eng'] == e and e == 'pe' and not p['dma'] and not p['cc']:
                        continue
                    wait(p['sem'], p['val'])
                if o['dma'] and not o['cc'] and o['prev'] > 0:
                    wait(o['sem'], o['prev'])
                if o['fn'] is None:
                    continue
                ins = o['fn'](eng)
                if o['sem'] is not None:
                    if o['cc']:
                        ins.then_inc(o['sem'], 1)
                    elif o['dma']:
                        ins.then_inc(o['sem'], 16)
                    else:
                        ins.then_inc(o['sem'], 1)

        @block.tensor
        def _(eng):
            run('pe', eng)

        @block.scalar
        def _(eng):
            run('act', eng)

        @block.vector
        def _(eng):
            run('dve', eng)

        @block.gpsimd
        def _(eng):
            run('pool', eng)

        @block.sync
        def _(eng):
            run('sp', eng)


class Arena:
    def __init__(self, ap_f32, nwords):
        self.ap = ap_f32
        self.n = nwords
        self.top = 0
        self.hi = 0

    def mark(self):
        return self.top

    def release(self, m):
        self.top = m

    def f32(self, cols, parts=128):
        a = self.top
        self.top += cols
        self.hi = max(self.hi, self.top)
        assert self.top <= self.n, ("SBUF arena overflow", self.top, self.n)
        return self.ap[0:parts, a:a + cols]

    def bf16(self, cols, parts=128):
        w = (cols + 1) // 2
        a = self.top
        self.top += w
        self.hi = max(self.hi, self.top)
        assert self.top <= self.n, ("SBUF arena overflow", self.top, self.n)
        return self.ap[0:parts, a:a + w].bitcast(BF16)[:, 0:cols]


class _Done(Exception):
    def __init__(self, nc):
        self.nc = nc


def build(dbg=None, stage=None):
    try:
        return build_nc(dbg, stage)
    except _Done as d:
        return d.nc


def build_nc(dbg=None, stage=None):
    nc = bass.Bass("TRN2", target_bir_lowering=False)
    es = ExitStack()
    tr = Tr()

    def stage_end(name):
        if stage == name:
            blk_ = es.enter_context(nc.Block())
            tr.emit(nc, es, blk_, list(tr.dbg_ops))
            es.close()
            raise _Done(nc)

    def din(name, shape, dt=F32):
        return nc.dram_tensor(name, list(shape), dt, kind="ExternalInput").ap()

    x_d = din("x", [T, D])
    xown_d = din("xown", [TO, D])
    cb_d = din("cb", [128, 16])
    wada_d = din("wada", [D, 6 * D])
    bada_d = din("bada", [6 * D])
    gvec_d = din("gvec", [3, D])
    win_d = din("win", [D, NWIN])
    pcol_d = din("pcol", [128, 80])
    rowv_d = din("rowv", [2, 512])
    rk_d = din("rkm", [128, 8])
    wa2_d = din("wa2", [128, 512])
    g2_d = din("g2", [160, 512])
    lruw_d = din("lruw", [2, 2, 256, 256])
    wout_d = din("wout", [D, D])
    wgu_d = din("wgu", [D, 2 * DFF])
    wdown_d = din("wdown", [DFF, D])
    ident_d = din("ident", [128, 128])
    masks_d = din("masks", [128, 4, 128])
    hsel_d = din("hsel", [128, 2])
    out_d = nc.dram_tensor("out", [TO, D], F32, kind="ExternalOutput").ap()
    ysrc_l = [nc.dram_tensor("ysrc%d" % i, [1024, 512], BF16, kind="Internal").ap() for i in range(4)]
    ydst_l = [nc.dram_tensor("ydst%d" % i, [2048, 512], BF16, kind="Internal").ap() for i in range(4)]
    dbg_d = {}
    if dbg:
        for nm, shp in dbg.items():
            dbg_d[nm] = nc.dram_tensor("dbg_" + nm, list(shp), F32, kind="ExternalOutput").ap()

    NW = 53000
    arena_t = es.enter_context(nc.sbuf_tensor("arena", [128, NW], F32))
    ar = Arena(arena_t, NW)
    ps = [es.enter_context(nc.psum_tensor("ps%d" % i, [128, 512], F32)) for i in range(8)]

    dmaq = ['sp', 'act']
    dctr = [0]

    def dma(out, in_, r=(), w=(), q=None, **kw):
        if q is None:
            q = dmaq[dctr[0] % len(dmaq)]
            dctr[0] += 1
        return tr.op(q, lambda e: e.dma_start(out=out, in_=in_, **kw), r, w, dma=True)

    def mm(out, lhsT, rhs, start, stop, r=(), w=()):
        return tr.op('pe', lambda e: e.matmul(out, lhsT, rhs, start=start, stop=stop), r, w)

    def tp(out, in_, idn, r=(), w=()):
        return tr.op('pe', lambda e: e.transpose(out, in_, idn), r, w)

    def act(out, in_, func, r=(), w=(), bias=None, scale=None, accum_out=None):
        kw = {}
        if bias is not None:
            kw['bias'] = bias
        if scale is not None:
            kw['scale'] = scale
        if accum_out is not None:
            kw['accum_out'] = accum_out
        return tr.op('act', lambda e: e.activation(out, in_, func, **kw), r, w)

    def ts(eng, out, in0, s1, s2, op0, op1, r=(), w=()):
        if op1 is None:
            return tr.op(eng, lambda e: e.tensor_scalar(out, in0, s1, None, op0), r, w)
        return tr.op(eng, lambda e: e.tensor_scalar(out, in0, s1, s2, op0, op1), r, w)

    def tt(eng, out, in0, in1, op, r=(), w=()):
        return tr.op(eng, lambda e: e.tensor_tensor(out, in0, in1, op), r, w)

    def stt(out, in0, sc, in1, op0, op1, r=(), w=()):
        return tr.op('dve', lambda e: e.scalar_tensor_tensor(out, in0, sc, in1, op0, op1), r, w)

    def cp(eng, out, in_, r=(), w=()):
        if eng == 'act':
            return tr.op('act', lambda e: e.activation(out, in_, AF.Copy), r, w)
        return tr.op(eng, lambda e: e.tensor_copy(out, in_), r, w)

    def memset(eng, ap, val, w=()):
        return tr.op(eng, lambda e: e.memset(ap, val), (), w)

    def dump(name, ap, r):
        if name in dbg_d:
            i_ = dma(dbg_d[name], ap, r=r, w=[('dbgout', name)], q='sp')
            tr.dbg_ops.append(i_)
            return i_
        return None

    ident = ar.f32(128)
    masks = ar.f32(512)
    pcol = ar.f32(80)
    rkm = ar.f32(8)
    hsel = ar.f32(2)
    ones = ar.f32(128)
    epsc = ar.f32(4)
    wa2 = ar.f32(512)
    g2a = ar.f32(512)
    g2b = ar.f32(512, 32)
    rowv = ar.f32(1024)
    cb = ar.f32(16)
    identb = ar.bf16(128)
    identb2 = identb
    dma(ident, ident_d, w=['ident'])
    dma(masks, masks_d.rearrange("p a b -> p (a b)"), w=['masks'])
    dma(pcol, pcol_d, w=['pcol'])
    dma(rkm, rk_d, w=['rkm'])
    dma(hsel, hsel_d, w=['hsel'])
    dma(wa2, wa2_d, w=['wa2'])
    dma(g2a, g2_d[0:128, :], w=['g2'])
    dma(g2b, g2_d[128:160, :], w=['g2'])
    dma(rowv, rowv_d.rearrange("a n -> (a n)").partition_broadcast(128), w=['rowv'])
    dma(cb, cb_d, w=['cb'])
    memset('pool', ones, 1.0, w=['ones'])
    memset('pool', epsc[:, 0:1], RMS_EPS, w=['epsc'])
    memset('pool', epsc[:, 1:2], GN_EPS, w=['epsc'])
    memset('pool', epsc[:, 2:3], 0.0, w=['epsc'])
    MUS = masks[:, 0:128]
    MUI = masks[:, 128:256]
    MLS = masks[:, 256:384]
    BLK = masks[:, 384:512]
    lng = rowv[:, 0:512]
    lnb = rowv[:, 512:1024]

    def pc(i):
        return pcol[:, i:i + 1]

    dcol = ar.f32(64)
    mu_idx = [8 * j + q for j in range(4) for q in range(3)] + [32, 33, 34]
    for n, mi in enumerate(mu_idx):
        ts('pool', dcol[:, n:n + 1], pc(mi), -1.0, 1.0, ALU.mult, ALU.add, r=['pcol'], w=['dcol'])

    def omm(mi):
        return dcol[:, mu_idx.index(mi):mu_idx.index(mi) + 1]

    for j in range(4):
        ts('pool', dcol[:, 16 + j:17 + j], pc(8 * j + 4), -1.0, 1.0, ALU.mult, ALU.add, r=['pcol'], w=['dcol'])
        act(dcol[:, 32 + j:33 + j], pc(40 + 8 * j + 7), AF.Exp, r=['pcol'], w=['dcol'], scale=-1.0)
        act(dcol[:, 36 + j:37 + j], dcol[:, 32 + j:33 + j], AF.Ln, r=['dcol'], w=['dcol'], bias=1.0)
        ts('pool', dcol[:, 24 + j:25 + j], dcol[:, 36 + j:37 + j], -8.0, None, ALU.mult, None, r=['dcol'], w=['dcol'])
        ts('pool', dcol[:, 28 + j:29 + j], dcol[:, 36 + j:37 + j], -16.0, None, ALU.mult, None, r=['dcol'], w=['dcol'])

    cact = ar.f32(16)
    act(cact, cb, AF.Silu, r=['cb'], w=['cact'])
    cbc = ar.f32(16 * 128)
    cp('dve', cbc.rearrange("p (k m) -> p k m", k=16), cact.unsqueeze(2).to_broadcast([128, 16, 128]),
       r=['cact'], w=['cbc'])
    m_phase = ar.mark()
    MB = {}

    def alloc_mb():
        MB['bad'] = ar.f32(2048)
        MB['gv'] = ar.f32(2048)
        MB['wst'] = [ar.f32(2 * 2048), ar.f32(2 * 2048)]
    wada_v = wada_d.rearrange("(p k) n -> p k n", k=16)

    def mod_chunk(jm, evac):
        nst = 0
        for kq in range(8):
            b = MB['wst'][nst % 2]
            nst += 1
            dma(b.rearrange("p (k n) -> p k n", k=2), wada_v[:, kq * 2:(kq + 1) * 2, jm * 2048:(jm + 1) * 2048],
                w=[('wst', id(b))])
            for kk in range(2):
                k = kq * 2 + kk
                for q in range(4):
                    mm(ps[q][:, :], cbc[:, k * 128:(k + 1) * 128], b[:, kk * 2048 + q * 512: kk * 2048 + (q + 1) * 512],
                       k == 0, k == 15, r=['cbc', ('wst', id(b))], w=[('ps', q)])
        for q in range(4):
            evac(q, ps[q][:, :])

    def mod_vec(jm, dst, gidx=None):
        bad = MB['bad']
        gv = MB['gv']
        dma(bad, bada_d[jm * 2048:(jm + 1) * 2048].partition_broadcast(128), w=['bad'])
        if gidx is not None:
            dma(gv, gvec_d[gidx, :].partition_broadcast(128), w=['gv'])

        def ev(q, p):
            sl = slice(q * 512, (q + 1) * 512)
            tt('dve', dst[:, sl], p, bad[:, sl], ALU.add, r=[('ps', q), 'bad'], w=[('vec', id(dst))])
            if gidx is not None:
                stt(dst[:, sl], dst[:, sl], 1.0, gv[:, sl], ALU.add, ALU.mult, r=[('vec', id(dst)), 'gv'],
                    w=[('vec', id(dst))])
        mod_chunk(jm, ev)

    hT = ar.f32(16 * 2048 // 2)
    hTb = hT.bitcast(BF16).rearrange("p (k t) -> p k t", k=16)
    p1 = ar.mark()
    alloc_mb()
    gm1 = ar.f32(2048)
    shm = ar.f32(2048)
    mod_vec(1, gm1, gidx=0)
    mod_vec(0, shm)
    xs = [ar.f32(2048), ar.f32(2048)]
    hf = ar.f32(2048)
    hb = ar.bf16(2048)
    junk = ar.bf16(2048)
    ssq = ar.f32(16)
    rst = ar.f32(16)
    cp('dve', identb, ident, r=['ident'], w=['identb', 'identb2'])
    for tt_i in range(16):
        xb = xs[tt_i % 2]
        xt = ('xs', tt_i % 2)
        dma(xb, x_d[tt_i * 128:(tt_i + 1) * 128, :], w=[xt], q='sp')
        act(junk, xb, AF.Square, r=[xt], w=['junk', ('ssq', tt_i)], accum_out=ssq[:, tt_i:tt_i + 1])
        act(rst[:, tt_i:tt_i + 1], ssq[:, tt_i:tt_i + 1], AF.Sqrt, r=[('ssq', tt_i), 'epsc'], w=[('rst', tt_i)],
            bias=epsc[:, 0:1], scale=1.0 / D)
        tr.op('dve', lambda e, a=rst[:, tt_i:tt_i + 1]: e.reciprocal(a, a), [('rst', tt_i)], [('rst', tt_i)])
        stt(hf, xb, rst[:, tt_i:tt_i + 1], gm1, ALU.mult, ALU.mult, r=[xt, ('rst', tt_i), ('vec', id(gm1))], w=['hf'])
        tt('pool', hb, hf, shm, ALU.add, r=['hf', ('vec', id(shm))], w=['hb'])
        for half in range(2):
            pb = ps[4 + (tt_i * 2 + half) % 4]
            pt = ('ps', 4 + (tt_i * 2 + half) % 4)
            pbv = pb[:, :].bitcast(BF16)
            for q in range(8):
                dc = half * 8 + q
                tp(pbv[:, q * 128:(q + 1) * 128], hb[:, dc * 128:(dc + 1) * 128], identb, r=['hb', 'identb'], w=[pt])
            cp('act' if half == 0 else 'dve', hTb[:, half * 8:(half + 1) * 8, tt_i * 128:(tt_i + 1) * 128],
               pbv.rearrange("p (q t) -> p q t", q=8), r=[pt], w=[('hT', tt_i)])
    hT_all = [('hT', i) for i in range(16)]
    if 'hT' in dbg_d:
        tmpd = ar.f32(2048)
        cp('dve', tmpd, hTb[:, 3, :], r=hT_all, w=['tmpd'])
        dump('hT', tmpd, ['tmpd'])
    dump('gm1', gm1, [('vec', id(gm1))])
    stage_end('p1')
    tr.fence()
    ar.release(p1)

    wbufs = [ar.bf16(16 * 512), ar.bf16(16 * 512)]
    wctr = [0]

    def load_w(c0, ncols):
        b = wbufs[wctr[0] % 2]
        wctr[0] += 1
        bv = b[:, 0:16 * ncols].rearrange("p (k n) -> p k n", k=16)
        tok = ('wbuf', id(b))
        for kh in range(4):
            dma(bv[:, kh * 4:(kh + 1) * 4, :],
                win_d[kh * 512:(kh + 1) * 512, c0:c0 + ncols].rearrange("(k p) n -> p k n", p=128),
                w=[tok], q='pool')
        return bv, tok

    psrr = [0]

    def proj(bv, tok, cofs, ncol, tg, tlen=512):
        bi = psrr[0] % 4
        psrr[0] += 1
        p = ps[bi]
        for k in range(16):
            mm(p[0:ncol, 0:tlen], bv[:, k, cofs:cofs + ncol], hTb[:, k, tg * 512: tg * 512 + tlen], k == 0, k == 15,
               r=[tok] + hT_all, w=[('ps', bi)])
        return p[0:ncol, 0:tlen], ('ps', bi)

    pmu = ar.f32(516)
    memset('pool', pmu[:, 0:1], 0.0, w=['pmu'])

    def lerp_evac(p, ptok, nrow, mui, dst, dtok, tg):
        if tg > 0:
            cp('pool', pmu[0:nrow, 0:1], pmu[0:nrow, 512:513], r=['pmu'], w=['pmu'])
        else:
            memset('pool', pmu[0:nrow, 0:1], 0.0, w=['pmu'])
        act(pmu[0:nrow, 1:513], p, AF.Copy, r=[ptok, 'pcol'], w=['pmu'], scale=pc(mui)[0:nrow, :])
        stt(dst[0:nrow, tg * 512:(tg + 1) * 512], p, omm(mui)[0:nrow, :], pmu[0:nrow, 0:512], ALU.mult, ALU.add,
            r=[ptok, 'pmu', 'dcol'], w=[dtok])

    lwin = ar.f32(2048)
    glB = ar.f32(2048)
    glC = ar.f32(2048, 32)
    bv, tok = load_w(0, 288)
    for tg in range(4):
        p, ptk = proj(bv, tok, 0, 128, tg)
        lerp_evac(p, ptk, 128, 32, lwin, ('lwin', tg), tg)
        act(lwin[0:64, tg * 512:(tg + 1) * 512], lwin[0:64, tg * 512:(tg + 1) * 512], AF.Tanh, r=[('lwin', tg)],
            w=[('lwin', tg)])
    for tg in range(4):
        p, ptk = proj(bv, tok, 128, 128, tg)
        lerp_evac(p, ptk, 128, 33, glB, ('glB', tg), tg)
        act(glB[:, tg * 512:(tg + 1) * 512], glB[:, tg * 512:(tg + 1) * 512], AF.Sigmoid, r=[('glB', tg)],
            w=[('glB', tg)])
    for tg in range(4):
        p, ptk = proj(bv, tok, 256, 32, tg)
        lerp_evac(p, ptk, 32, 34, glC, ('glC', tg), tg)
        act(glC[:, tg * 512:(tg + 1) * 512], glC[:, tg * 512:(tg + 1) * 512], AF.Sigmoid, r=[('glC', tg)],
            w=[('glC', tg)])
    dump('lwin', lwin, [('lwin', g) for g in range(4)])
    dump('glB', glB, [('glB', g) for g in range(4)])
    stage_end('lora')

    def pq(bank, q0, nq=1, rows=128):
        return ps[bank][0:rows, q0 * 128:(q0 + nq) * 128], [('ps', bank)]

    def proj_prev(bv, tok, cofs, ncol, tcol, width, bank, col):
        for k in range(16):
            mm(ps[bank][0:ncol, col:col + width], bv[:, k, cofs:cofs + ncol], hTb[:, k, tcol:tcol + width], k == 0,
               k == 15, r=[tok] + hT_all, w=[('ps', bank)])
        return ps[bank][0:ncol, col:col + width], ('ps', bank)

    mask4 = ar.f32(512)
    cp('pool', mask4[:, 0:128], MUS, r=['masks'], w=['mask4'])
    cp('pool', mask4[:, 128:256], MUI, r=['masks'], w=['mask4'])
    cp('pool', mask4[:, 256:384], MUS, r=['masks'], w=['mask4'])
    cp('pool', mask4[:, 384:512], MUI, r=['masks'], w=['mask4'])

    ydma = []
    mrw = ar.mark()
    Hst = [[ar.f32(64), ar.f32(64)], [ar.f32(64), ar.f32(64)]]
    btp = [ar.f32(512), ar.f32(512)]
    ktp = [ar.f32(512), ar.f32(512)]
    for z_ in btp + ktp:
        memset('pool', z_, 0.0, w=[('pad', id(z_))])
    hcnt = [0]
    pqs = [[ar.f32(384), ar.f32(384)] for _ in range(2)]
    amat = [ar.f32(512) for _ in range(2)]
    trn = ar.f32(384)
    Xs = ar.f32(128)
    Us = ar.f32(128)
    gtok = ar.f32(512)
    s_sb = ar.f32(8)
    ybT = ar.bf16(512)
    yb_ = ar.f32(128)
    yc_ = ar.f32(128)
    sq_ = ar.f32(128)
    st_ = ar.f32(8)
    yfb = ar.bf16(128)
    dec = ar.f32(4)
    A = {}
    for nm in ['r', 'k', 'v', 'sg', 'L', 'a', 'kk', 'EL', 'tmp']:
        A[nm] = ar.f32(512)
    tk = lambda nm: ('rw', nm)
    for j in range(4):
        bv, tok = load_w(288 + 384 * j, 384)
        for par_ in range(2):
            for hd_ in range(2):
                memset('pool', Hst[par_][hd_], 0.0, w=[('H', par_)])
        for tb in range(4):
            tsl = slice(tb * 512, (tb + 1) * 512)
            for q, nm in enumerate(['r', 'k', 'v']):
                p, ptk = proj(bv, tok, 128 * q, 128, tb)
                if tb > 0:
                    pp, pptk = proj_prev(bv, tok, 128 * q, 128, tb * 512 - 1, 1, 7, 0)
                    act(pmu[:, 0:1], pp, AF.Copy, r=[pptk, 'pcol'], w=['pmu'], scale=pc(8 * j + q))
                else:
                    memset('pool', pmu[:, 0:1], 0.0, w=['pmu'])
                act(pmu[:, 1:513], p, AF.Copy, r=[ptk, 'pcol'], w=['pmu'], scale=pc(8 * j + q))
                stt(A[nm], p, omm(8 * j + q), pmu[:, 0:512], ALU.mult, ALU.add, r=[ptk, 'pmu', 'dcol'], w=[tk(nm)])
            pw, pwt = pq(4, 0, 4)
            mm(pw, wa2[0:64, j * 128:(j + 1) * 128], lwin[0:64, tsl], True, True, r=['wa2', ('lwin', tb)], w=pwt)
            act(A['sg'], pw, AF.Sigmoid, r=pwt + ['pcol'], w=[tk('sg')], bias=pc(8 * j + 5))
            pa, pat = pq(5, 0, 4)
            mm(pa, wa2[64:128, j * 128:(j + 1) * 128], lwin[64:128, tsl], True, True, r=['wa2', ('lwin', tb)], w=pat)
            act(A['a'], pa, AF.Sigmoid, r=pat + ['pcol'], w=[tk('a')], bias=pc(8 * j + 6))
            for c in range(4):
                cs = slice(c * 128, (c + 1) * 128)
                tr.op('dve', lambda e, o=A['L'][:, cs], d1=A['sg'][:, cs]: e.tensor_tensor_scan(
                    o, ones, d1, 0.0, ALU.mult, ALU.add), [tk('sg'), 'ones'], [tk('L')])
            tt('pool', A['tmp'], A['L'], A['sg'], ALU.subtract, r=[tk('L'), tk('sg')], w=[tk('tmp')])
            act(A['tmp'], A['tmp'], AF.Exp, r=[tk('tmp')], w=[tk('tmp')], scale=-C0)
            act(A['EL'], A['L'], AF.Exp, r=[tk('L')], w=[tk('EL')], scale=-C0)
            act(A['L'], A['L'], AF.Exp, r=[tk('L')], w=[tk('L')], scale=C0)
            ts('pool', A['kk'], A['k'], pc(8 * j + 3), None, ALU.mult, None, r=[tk('k'), 'pcol'], w=[tk('kk')])
            act(A['sg'], A['kk'], AF.Square, r=[tk('kk')], w=[tk('sg')])
            pn, pnt = pq(6, 0, 4)
            mm(pn, BLK, A['sg'], True, True, r=['masks', tk('sg')], w=pnt)
            act(A['sg'], pn, AF.Sqrt, r=pnt, w=[tk('sg')])
            ts('dve', A['sg'], A['sg'], L2_EPS, None, ALU.max, None, r=[tk('sg')], w=[tk('sg')])
            tr.op('dve', lambda e, a_=A['sg']: e.reciprocal(a_, a_), [tk('sg')], [tk('sg')])
            tt('dve', A['kk'], A['kk'], A['sg'], ALU.mult, r=[tk('kk'), tk('sg')], w=[tk('kk')])
            ts('dve', A['sg'], A['a'], pc(8 * j + 4), dcol[:, 16 + j:17 + j], ALU.mult, ALU.add,
               r=[tk('a'), 'pcol', 'dcol'], w=[tk('sg')])
            tt('dve', A['k'], A['k'], A['sg'], ALU.mult, r=[tk('k'), tk('sg')], w=[tk('k')])
            tt('pool', A['sg'], A['r'], A['k'], ALU.mult, r=[tk('r'), tk('k')], w=[tk('sg')])
            psS, psSt = pq(7, 1, 1)
            for c in range(4):
                mm(psS[:, 2 * c:2 * c + 2], A['sg'][:, c * 128:(c + 1) * 128], rkm[:, 2 * j:2 * j + 2], True, True,
                   r=[tk('sg'), 'rkm'], w=psSt)
            cp('act', s_sb, psS[:, 0:8], r=psSt, w=['s_sb'])
            tt('pool', A['a'], A['kk'], A['a'], ALU.mult, r=[tk('kk'), tk('a')], w=[tk('a')])
            tt('dve', A['r'], A['r'], A['EL'], ALU.mult, r=[tk('r'), tk('EL')], w=[tk('r')])
            tt('pool', A['k'], A['k'], A['L'], ALU.mult, r=[tk('k'), tk('L')], w=[tk('k')])
            tt('dve', A['a'], A['a'], A['L'], ALU.mult, r=[tk('a'), tk('L')], w=[tk('a')])
            stt(A['kk'], A['kk'], -1.0, A['tmp'], ALU.mult, ALU.mult, r=[tk('kk'), tk('tmp')], w=[tk('kk')])
            cp('pool', dec, A['EL'].rearrange("p (c t) -> p c t", c=4)[:, :, 127], r=[tk('EL')], w=['dec'])
            decb = dec.unsqueeze(2).to_broadcast([128, 4, 128])
            tt('dve', A['L'].rearrange("p (c t) -> p c t", c=4), A['a'].rearrange("p (c t) -> p c t", c=4), decb,
               ALU.mult, r=[tk('a'), 'dec', tk('L')], w=[tk('L')])
            tt('pool', A['tmp'].rearrange("p (c t) -> p c t", c=4), A['k'].rearrange("p (c t) -> p c t", c=4), decb,
               ALU.mult, r=[tk('k'), 'dec', tk('tmp')], w=[tk('tmp')])
            rt_, kt_, bt_, at_, Bd_, Kd_ = A['r'], A['k'], A['a'], A['kk'], A['L'], A['tmp']
            for hd_ in range(2):
                hs_ = slice(64 * hd_, 64 * hd_ + 64)
                cp('pool', btp[hd_][hs_, :], bt_[hs_, :], r=[tk('a')], w=[('pad', id(btp[hd_]))])
                cp('act', ktp[hd_][hs_, :], kt_[hs_, :], r=[tk('k')], w=[('pad', id(ktp[hd_]))])
            if j == 0 and tb < 2:
                for nm_, ap_, t_ in [('rt', rt_, 'r'), ('kt', kt_, 'k'), ('bt', bt_, 'a'), ('at', at_, 'kk'),
                                     ('Bd', Bd_, 'L'), ('Kd', Kd_, 'tmp'), ('vv', A['v'], 'v')]:
                    dump('%s%d' % (nm_, tb), ap_, [tk(t_)])
            if j == 0 and tb == 0:
                stage_end('rwA')
            for c in range(4):
                cs = slice(tb * 512 + c * 128, tb * 512 + (c + 1) * 128)
                pg, pgt = pq(4, c, 1)
                mm(pg, glB[:, cs], g2a[:, j * 128:(j + 1) * 128], True, False, r=[('glB', tb), 'g2'], w=pgt)
                mm(pg, glC[0:32, cs], g2b[0:32, j * 128:(j + 1) * 128], False, True, r=[('glC', tb), 'g2'], w=pgt)
            pg4, pg4t = pq(4, 0, 4)
            cp('act', gtok, pg4, r=pg4t, w=['gtok'])
            if j == 0 and tb == 0:
                stage_end('rwA1')
            for c in range(4):
                cs = slice(c * 128, (c + 1) * 128)
                ptr_, ptrt = pq(4, 0, 3)
                tp(ptr_[:, 0:128], A['v'][:, cs], ident, r=[tk('v'), 'ident'], w=ptrt)
                tp(ptr_[:, 128:256], Bd_[:, cs], ident, r=[tk('L'), 'ident'], w=ptrt)
                tp(ptr_[:, 256:384], Kd_[:, cs], ident, r=[tk('tmp'), 'ident'], w=ptrt)
                cp('act', trn, ptr_, r=ptrt, w=['trn'])
                vtok, BdT, KdT = trn[:, 0:128], trn[:, 128:256], trn[:, 256:384]
                if j == 0 and tb == 0 and c == 0:
                    dump('trn', trn, ['trn'])
                    stage_end('rwA2')
                import os
                for hd in range(int(os.environ.get('KDBG_NHD', '2'))):
                    hs = slice(64 * hd, 64 * hd + 64)
                    pA, pAt = pq(5 + hd, 0, 4)
                    bpt, kpt = ('pad', id(btp[hd])), ('pad', id(ktp[hd]))
                    mm(pA[:, 0:128], btp[hd][:, cs], at_[:, cs], True, True, r=[bpt, tk('kk')], w=pAt)
                    mm(pA[:, 128:256], btp[hd][:, cs], rt_[:, cs], True, True, r=[bpt, tk('r')], w=pAt)
                    mm(pA[:, 256:384], ktp[hd][:, cs], at_[:, cs], True, True, r=[kpt, tk('kk')], w=pAt)
                    mm(pA[:, 384:512], ktp[hd][:, cs], rt_[:, cs], True, True, r=[kpt, tk('r')], w=pAt)
                    pQ, pQt = pq(7, 2 + hd, 1)
                    mm(pQ, at_[:, cs], btp[hd][:, cs], True, True, r=[bpt, tk('kk')], w=pQt)
                    tt('dve', amat[hd], pA, mask4, ALU.mult, r=pAt + ['mask4'], w=[('amat', hd)])
                    b0 = pqs[hd][0]
                    cp('act', b0[:, 0:128], amat[hd][:, 0:128], r=[('amat', hd)], w=[('pqs', hd, 0)])
                    tt('dve', b0[:, 128:256], pQ, MLS, ALU.mult, r=pQt + ['masks'], w=[('pqs', hd, 0)])
                    tt('pool', b0[:, 256:384], amat[hd][:, 0:128], ident, ALU.add, r=[('amat', hd), 'ident'],
                       w=[('pqs', hd, 0)])
                if j == 0 and tb == 0 and c == 0:
                    dump('amat0', amat[0], [('amat', 0)])
                    stage_end('rwA3')
                for it in range(int(os.environ.get('KDBG_NIT', '6'))):
                    for hd in range(2):
                        cur, nxt = pqs[hd][it % 2], pqs[hd][(it + 1) % 2]
                        ct, nt = ('pqs', hd, it % 2), ('pqs', hd, (it + 1) % 2)
                        P_, Q_, S_ = cur[:, 0:128], cur[:, 128:256], cur[:, 256:384]
                        pI, pIt = pq(5 + hd, 0, 3)
                        mm(pI[:, 0:128], P_, Q_, True, True, r=[ct], w=pIt)
                        cp('act', nxt[:, 128:256], pI[:, 0:128], r=pIt, w=[nt])
                        if it < 5:
                            mm(pI[:, 128:256], Q_, P_, True, True, r=[ct], w=pIt)
                            cp('act', nxt[:, 0:128], pI[:, 128:256], r=pIt, w=[nt])
                        mm(pI[:, 256:384], nxt[:, 128:256], S_, True, True, r=[ct, nt], w=pIt)
                        tt('dve', nxt[:, 256:384], pI[:, 256:384], S_, ALU.add, r=pIt + [ct], w=[nt])
                if j == 0 and tb == 0 and c < 2:
                    dump('amat%d' % c, amat[0], [('amat', 0)])
                    dump('Tt%d' % c, pqs[0][0][:, 256:384], [('pqs', 0, 0)])
                if j == 0 and tb == 0 and c == 0:
                    stage_end('rwB')
                Tt = [pqs[hd][0][:, 256:384] for hd in range(2)]
                Ttt = [('pqs', hd, 0) for hd in range(2)]
                Hc = Hst[hcnt[0] % 2]
                Hn = Hst[(hcnt[0] + 1) % 2]
                Hcp = Hc
                Hct, Hnt = ('H', hcnt[0] % 2), ('H', (hcnt[0] + 1) % 2)
                hcnt[0] += 1
                pX, pXt = pq(4, 3, 1)
                for hd in range(2):
                    hs = slice(64 * hd, 64 * hd + 64)
                    vs = slice(64 * hd, 64 * hd + 64)
                    mm(pX[:, vs], at_[:, cs], Hc[hd], True, False, r=[tk('kk'), Hct], w=pXt)
                    mm(pX[:, vs], amat[hd][:, 256:384], vtok[:, vs], False, True, r=[('amat', hd), 'trn'], w=pXt)
                cp('act', Xs, pX, r=pXt, w=['Xs'])
                pU, pUt = pq(7, 0, 1)
                for hd in range(2):
                    vs = slice(64 * hd, 64 * hd + 64)
                    mm(pU[:, vs], Tt[hd], Xs[:, vs], True, True, r=[Ttt[hd], 'Xs'], w=pUt)
                cp('act', Us, pU, r=pUt, w=['Us'])
                if j == 0 and tb == 0 and c == 0:
                    dump('Xs0', Xs, ['Xs'])
                    dump('Us0', Us, ['Us'])
                    stage_end('rwB1')
                pY, pYt = pq(4, 0, 1)
                for hd in range(2):
                    hs = slice(64 * hd, 64 * hd + 64)
                    vs = slice(64 * hd, 64 * hd + 64)
                    mm(pY[:, vs], rt_[:, cs], Hc[hd], True, False, r=[tk('r'), Hct], w=pYt)
                    mm(pY[:, vs], amat[hd][:, 128:256], Us[:, vs], False, False, r=[('amat', hd), 'Us'], w=pYt)
                    mm(pY[:, vs], amat[hd][:, 384:512], vtok[:, vs], False, True, r=[('amat', hd), 'trn'], w=pYt)
                pH, pHt = pq(4, 1, 1)
                mm(pH, BdT, Us, True, False, r=['trn', 'Us'], w=pHt)
                mm(pH, KdT, vtok, False, True, r=['trn'], w=pHt)
                for hd in range(2):
                    hs = slice(64 * hd, 64 * hd + 64)
                    stt(Hn[hd][hs, :], Hc[hd][hs, :], dec[hs, c:c + 1], pH[hs, 64 * hd:64 * hd + 64], ALU.mult, ALU.add,
                        r=[Hct, 'dec'] + pHt, w=[Hnt])
                if j == 0 and tb == 0 and c == 0:
                    dump('Hn0', Hn[0], [Hnt])
                    dump('Hm0', Hn[1], [Hnt])
                    stage_end('rwB2')
                cp('act', yb_, pY, r=pYt, w=['yb'])
                if j == 0 and tb == 0 and c < 2:
                    dump('Xs%d' % c, Xs, ['Xs'])
                    dump('Us%d' % c, Us, ['Us'])
                    dump('yb%d' % c, yb_, ['yb'])
                    dump('Hn%d' % c, Hn[0], [Hnt])
                    dump('Hm%d' % c, Hn[1], [Hnt])
                y3 = yb_.rearrange("p (h v) -> p h v", h=2)
                if os.environ.get('KDBG_RED'):
                    tr.op('dve', lambda e, o=st_[:, 0:2], i_=y3: e.tensor_reduce(o, i_, AX.X, ALU.add), ['yb'], ['st'])
                else:
                    for hd_ in range(2):
                        act(sq_[:, 64 * hd_:64 * hd_ + 64], yb_[:, 64 * hd_:64 * hd_ + 64], AF.Copy, r=['yb'],
                            w=['sq', 'st'], accum_out=st_[:, hd_:hd_ + 1])
                ts('dve', st_[:, 0:2], st_[:, 0:2], -1.0 / 64, None, ALU.mult, None, r=['st'], w=['st'])
                tt('dve', yc_.rearrange("p (h v) -> p h v", h=2), y3, st_[:, 0:2].unsqueeze(2).to_broadcast([128, 2, 64]),
                   ALU.add, r=['yb', 'st'], w=['yc'])
                if os.environ.get('KDBG_RED'):
                    act(sq_, yc_, AF.Square, r=['yc'], w=['sq'])
                else:
                    for hd_ in range(2):
                        act(sq_[:, 64 * hd_:64 * hd_ + 64], yc_[:, 64 * hd_:64 * hd_ + 64], AF.Square, r=['yc'],
                            w=['sq', 'st'], accum_out=st_[:, 2 + hd_:3 + hd_])
                if os.environ.get('KDBG_RED'):
                    tr.op('dve', lambda e, o=st_[:, 2:4], i_=sq_.rearrange("p (h v) -> p h v", h=2): e.tensor_reduce(
                        o, i_, AX.X, ALU.add), ['sq'], ['st'])
                act(st_[:, 2:4], st_[:, 2:4], AF.Sqrt, r=['st', 'epsc'], w=['st'], bias=epsc[:, 1:2], scale=1.0 / 64)
                tr.op('dve', lambda e, a_=st_[:, 2:4]: e.reciprocal(a_, a_), ['st'], ['st'])
                tt('dve', yc_.rearrange("p (h v) -> p h v", h=2), yc_.rearrange("p (h v) -> p h v", h=2),
                   st_[:, 2:4].unsqueeze(2).to_broadcast([128, 2, 64]), ALU.mult, r=['yc', 'st'], w=['yc'])
                if j == 0 and tb == 0 and c == 0:
                    stage_end('rwB3')
                SK = os.environ.get('KDBG_SKIP', '')
                if '1' not in SK:
                    tt('pool', yc_, yc_, lng[:, j * 128:(j + 1) * 128], ALU.mult, r=['yc', 'rowv'], w=['yc'])
                if '2' not in SK:
                    tt('pool', yc_, yc_, lnb[:, j * 128:(j + 1) * 128], ALU.add, r=['yc', 'rowv'], w=['yc'])
                if '3' not in SK:
                    tt('dve', sq_.rearrange("p (h v) -> p h v", h=2), vtok.rearrange("p (h v) -> p h v", h=2),
                       s_sb[:, 2 * c:2 * c + 2].unsqueeze(2).to_broadcast([128, 2, 64]), ALU.mult, r=['trn', 's_sb'],
                       w=['sq'])
                if '4' not in SK:
                    tt('dve', yc_, yc_, sq_, ALU.add, r=['yc', 'sq'], w=['yc'])
                if j == 0 and tb == 0 and c < 2:
                    dump('ypre%d' % c, yc_, ['yc'])
                if j == 0 and tb == 0 and c == 0:
                    stage_end('rwC')
                tt('dve', yfb, yc_, gtok[:, cs], ALU.mult, r=['yc', 'gtok'], w=['yfb'])
                pT, pTt = pq(7, 1, 1)
                pTb = pT.bitcast(BF16)[:, 0:128]
                tp(pTb, yfb, identb2, r=['yfb', 'identb2'], w=pTt)
                cp('act', ybT[:, cs], pTb, r=pTt, w=['ybT'])
            ydma.append(dma(ysrc_l[tb][512 + j * 128: 512 + (j + 1) * 128, :], ybT, r=['ybT'], w=[('ysrc', 4 + j, tb)]))
            if 'ybT%d' % tb in dbg_d and j == 0:
                tmpd2 = A['sg']
                cp('dve', tmpd2, ybT, r=['ybT'], w=[tk('sg')])
                dump('ybT%d' % tb, tmpd2, [tk('sg')])
            if j == 0 and tb == 1:
                stage_end('rwkv1')
    tr.fence()
    ar.release(mrw)

    mlr = ar.mark()
    lw = ar.f32(1024)
    lwv = lw.rearrange("p (g q n) -> p g q n", g=2, q=2)
    xp = [ar.f32(516), ar.f32(516)]
    uu = [ar.f32(512), ar.f32(512)]
    hh_ = [ar.f32(512), ar.f32(512)]
    hprev = ar.f32(2)
    R_ = ar.f32(512)
    I_ = ar.f32(512)
    A_ = ar.f32(512)
    M_ = ar.f32(512)
    G1 = ar.f32(512)
    G2 = ar.f32(512)
    yaT = ar.bf16(512)
    for h2 in range(2):
        bv, tok = load_w(288 + 1536 + 512 * h2, 512)
        for g in range(2):
            dma(lwv[:, g, :, :], lruw_d[g, h2, :, :].rearrange("(q p) n -> p q n", p=128), w=['lw'])
        for tb in range(4):
            tsl = slice(tb * 512, (tb + 1) * 512)
            for q in range(2):
                jt = 2 * h2 + q
                pb_ = 40 + 8 * jt
                p, ptk = proj(bv, tok, 128 * q, 128, tb)
                cp('act', xp[q][:, 3:515], p, r=[ptk], w=[('xp', q)])
                if tb > 0:
                    pp, pptk = proj_prev(bv, tok, 128 * q, 128, tb * 512 - 3, 3, 7, 0)
                    cp('act', xp[q][:, 0:3], pp, r=[pptk], w=[('xp', q)])
                else:
                    memset('pool', xp[q][:, 0:3], 0.0, w=[('xp', q)])
                ts('dve', uu[q], xp[q][:, 3:515], pc(pb_ + 3), pc(pb_ + 4), ALU.mult, ALU.add, r=[('xp', q), 'pcol'],
                   w=[('uu', q)])
                for kx in range(3):
                    stt(uu[q], xp[q][:, kx:kx + 512], pc(pb_ + kx), uu[q], ALU.mult, ALU.add,
                        r=[('xp', q), 'pcol', ('uu', q)], w=[('uu', q)])
            for qo in range(2):
                jt = 2 * h2 + qo
                pb_ = 40 + 8 * jt
                pr, prt = pq(4, 0, 4)
                pi, pit = pq(5, 0, 4)
                for q in range(2):
                    mm(pr, lwv[:, 0, q, qo * 128:(qo + 1) * 128], uu[q], q == 0, q == 1, r=['lw', ('uu', q)], w=prt)
                for q in range(2):
                    mm(pi, lwv[:, 1, q, qo * 128:(qo + 1) * 128], uu[q], q == 0, q == 1, r=['lw', ('uu', q)], w=pit)
                act(R_, pr, AF.Sigmoid, r=prt + ['pcol'], w=['R_'], bias=pc(pb_ + 5))
                act(I_, pi, AF.Sigmoid, r=pit + ['pcol'], w=['I_'], bias=pc(pb_ + 6))
                act(A_, R_, AF.Exp, r=['R_', 'dcol'], w=['A_'], scale=dcol[:, 24 + jt:25 + jt])
                act(M_, R_, AF.Exp, r=['R_', 'dcol'], w=['M_'], scale=dcol[:, 28 + jt:29 + jt])
                ts('dve', M_, M_, -1.0, 1.0, ALU.mult, ALU.add, r=['M_'], w=['M_'])
                ts('dve', M_, M_, 0.0, None, ALU.max, None, r=['M_'], w=['M_'])
                act(M_, M_, AF.Sqrt, r=['M_'], w=['M_'])
                if tb == 0:
                    memset('pool', M_[:, 0:1], 1.0, w=['M_'])
                tt('dve', I_, I_, M_, ALU.mult, r=['I_', 'M_'], w=['I_'])
                tt('pool', I_, I_, uu[qo], ALU.mult, r=['I_', ('uu', qo)], w=['I_'])
                if tb == 0:
                    init = 0.0
                    rr = ['A_', 'I_']
                else:
                    cp('pool', hprev[:, qo:qo + 1], hh_[qo][:, 511:512], r=[('hh', qo)], w=[('hprev', qo)])
                    init = hprev[:, qo:qo + 1]
                    rr = ['A_', 'I_', ('hprev', qo)]
                tr.op('dve', lambda e, o=hh_[qo], i0=init: e.tensor_tensor_scan(o, A_, I_, i0, ALU.mult, ALU.add),
                      rr, [('hh', qo)])
                p, ptk = proj(bv, tok, 256 + 128 * qo, 128, tb)
                cp('act', G1, p, r=[ptk], w=['G1'])
                act(G2, p, AF.Square, r=[ptk], w=['G2'])
                ts('dve', G2, G2, 0.044715, 1.0, ALU.mult, ALU.add, r=['G2'], w=['G2'])
                tt('pool', G2, G2, G1, ALU.mult, r=['G2', 'G1'], w=['G2'])
                act(G2, G2, AF.Sigmoid, r=['G2'], w=['G2'], scale=1.5957691216057308)
                tt('dve', G1, G1, G2, ALU.mult, r=['G1', 'G2'], w=['G1'])
                tt('dve', yaT, hh_[qo], G1, ALU.mult, r=[('hh', qo), 'G1'], w=['yaT'])
                ydma.append(dma(ysrc_l[tb][jt * 128:(jt + 1) * 128, :], yaT, r=['yaT'], w=[('ysrc', jt, tb)]))
                if ('yaT%d' % tb) in dbg_d and h2 == 0 and qo == 1:
                    cp('dve', G2, yaT, r=['yaT'], w=['G2'])
                    dump('yaT%d' % tb, G2, ['G2'])
    stage_end('mix')
    tr.fence()
    ar.release(mlr)
    ar.release(m_phase)

    import os
    for tb in range(4):
        toks_ = [('ysrc', a, tb) for a in range(8)]
        if os.environ.get('KDBG_NOCC'):
            dma(ydst_l[tb][0:1024, :], ysrc_l[tb], r=toks_, w=[('ydst', tb)], q='sp')
            dma(ydst_l[tb][1024:2048, :], ysrc_l[tb], r=toks_, w=[('ydst', tb)], q='sp')
        else:
            tr.op('pool', lambda e, a_=ysrc_l[tb], b_=ydst_l[tb]: e.collective_compute(
                "AllGather", ALU.bypass, replica_groups=[[0, 1], [2, 3], [4, 5], [6, 7]], ins=[a_], outs=[b_]),
                toks_, [('ydst', tb)], cc=True)
    x1 = ar.f32(8 * 2048)
    x1v = x1.rearrange("p (a n) -> p a n", a=8)
    vecA = ar.f32(2048)
    vecB = ar.f32(2048)
    vecC = ar.f32(2048)
    vecD = vecA
    tmpf = ar.f32(512)
    mtr = ar.mark()
    alloc_mb()
    mod_vec(2, vecA)
    mod_vec(4, vecB, gidx=1)
    mod_vec(3, vecC)
    tr.fence()
    ar.release(mtr)
    for a in range(8):
        dma(x1v[:, a, :], xown_d[a * 128:(a + 1) * 128, :], w=[('x1', a)])
    mys = ar.mark()
    ysel = ar.bf16(16 * 1024)
    yselv = ysel.rearrange("p (k t) -> p k t", k=16)
    tA = [ar.bf16(1024), ar.bf16(1024)]
    tB = [ar.bf16(1024), ar.bf16(1024)]
    for k in range(16):
        a_, b_ = tA[k % 2], tB[k % 2]
        for h_ in range(2):
            dma(a_[:, h_ * 512:(h_ + 1) * 512], ydst_l[h_][k * 128:(k + 1) * 128, :], r=[('ydst', h_)], w=[('tA', k % 2)])
            dma(b_[:, h_ * 512:(h_ + 1) * 512], ydst_l[2 + h_][k * 128:(k + 1) * 128, :], r=[('ydst', 2 + h_)],
                w=[('tB', k % 2)])
        ts('pool', a_, a_, hsel[:, 0:1], None, ALU.mult, None, r=[('tA', k % 2), 'hsel'], w=[('tA', k % 2)])
        stt(yselv[:, k, :], b_, hsel[:, 1:2], a_, ALU.mult, ALU.add, r=[('tA', k % 2), ('tB', k % 2), 'hsel'],
            w=[('ysel', k)])
    ysel_all = [('ysel', k) for k in range(16)]
    if 'ysel' in dbg_d:
        tmpq = ar.f32(1024)
        cp('dve', tmpq, yselv[:, 5, :], r=ysel_all, w=['tmpq'])
        dump('ysel', tmpq, ['tmpq'])
        cp('dve', tmpq, yselv[:, 13, :], r=ysel_all + [('dbgout', 'ysel')], w=['tmpq'])
        dump('ysel2', tmpq, ['tmpq'])
    stage_end('xchg')
    wb2 = [ar.bf16(16 * 512), ar.bf16(16 * 512)]
    w2c = [0]

    def load_w2(src, r0, nk, c0, ncols, buf=None, bofs=0, bcols=None):
        if buf is None:
            buf = wb2[w2c[0] % 2]
            w2c[0] += 1
        bc_ = bcols if bcols is not None else ncols
        v = buf[:, 0:nk * bc_].rearrange("p (k n) -> p k n", k=nk)
        tok_ = ('wb2', id(buf))
        step = 4
        for k0 in range(0, nk, step):
            k1 = min(nk, k0 + step)
            dma(v[:, k0:k1, bofs:bofs + ncols],
                src[r0 + k0 * 128: r0 + k1 * 128, c0:c0 + ncols].rearrange("(k p) n -> p k n", p=128),
                w=[tok_], q='pool')
        return v, tok_

    for cg in range(4):
        wv, wtok = load_w2(wout_d, 0, 16, cg * 512, 512)
        for a in range(8):
            bi = (cg * 8 + a) % 4
            for k in range(16):
                mm(ps[bi][:, :], yselv[:, k, a * 128:(a + 1) * 128], wv[:, k, :], k == 0, k == 15,
                   r=ysel_all + [wtok], w=[('ps', bi)])
            csl = slice(cg * 512, (cg + 1) * 512)
            tt('dve', tmpf, ps[bi][:, :], vecA[:, csl], ALU.mult, r=[('ps', bi), ('vec', id(vecA))], w=['tmpf'])
            tt('pool', x1v[:, a, csl], x1v[:, a, csl], tmpf, ALU.add, r=['tmpf', ('x1', a)], w=[('x1', a)])
    dump('x1a', x1v[:, 0, :], [('x1', 0)])
    dump('x1b', x1v[:, 7, :], [('x1', 7)])
    stage_end('wout')
    tr.fence()
    ar.release(mys)
    alloc_mb()
    mod_vec(5, vecD)
    tr.fence()
    ar.release(mys)

    h2T = ar.bf16(16 * 1024)
    h2Tv = h2T.rearrange("p (k t) -> p k t", k=16)
    actT = ar.bf16(4 * 1024)
    actv = actT.rearrange("p (f t) -> p f t", f=4)
    wg = [ar.bf16(16 * 256), ar.bf16(16 * 256)]
    wd = [ar.bf16(4 * 2048)] * 2
    hf2 = ar.f32(2048)
    hb2 = ar.bf16(2048)
    junk2 = hb2
    st2 = ar.f32(32)

    def rms_rstd(a, col):
        act(junk2, x1v[:, a, :], AF.Square, r=[('x1', a)], w=['hb2', ('st2', col)], accum_out=st2[:, col:col + 1])
        act(st2[:, col:col + 1], st2[:, col:col + 1], AF.Sqrt, r=[('st2', col), 'epsc'], w=[('st2', col)],
            bias=epsc[:, 0:1], scale=1.0 / D)
        tr.op('dve', lambda e, a_=st2[:, col:col + 1]: e.reciprocal(a_, a_), [('st2', col)], [('st2', col)])

    for a in range(8):
        rms_rstd(a, a)
        stt(hf2, x1v[:, a, :], st2[:, a:a + 1], vecB, ALU.mult, ALU.mult, r=[('x1', a), ('st2', a), ('vec', id(vecB))],
            w=['hf2'])
        tt('pool', hb2, hf2, vecC, ALU.add, r=['hf2', ('vec', id(vecC))], w=['hb2'])
        for half in range(2):
            bi = 4 + (a * 2 + half) % 4
            pbv = ps[bi][:, :].bitcast(BF16)
            for q in range(8):
                dc = half * 8 + q
                tp(pbv[:, q * 128:(q + 1) * 128], hb2[:, dc * 128:(dc + 1) * 128], identb, r=['hb2', 'identb'],
                   w=[('ps', bi)])
            cp('act' if half == 0 else 'dve', h2Tv[:, half * 8:(half + 1) * 8, a * 128:(a + 1) * 128],
               pbv.rearrange("p (q t) -> p q t", q=8), r=[('ps', bi)], w=[('h2T', a)])
    h2T_all = [('h2T', a) for a in range(8)]

    tmpg = ar.f32(512)
    for fg in range(11):
        dv, dtok = load_w2(wdown_d, fg * 512, 4, 0, 2048, buf=wd[fg % 2])
        for ft in range(4):
            f = fg * 4 + ft
            gb = wg[f % 2]
            gv_, gtok_ = load_w2(wgu_d, 0, 16, f * 128, 128, buf=gb, bofs=0, bcols=256)
            load_w2(wgu_d, 0, 16, DFF + f * 128, 128, buf=gb, bofs=128, bcols=256)
            for tg in range(2):
                bg, bu = (tg * 2) % 4, (tg * 2 + 1) % 4
                for k in range(16):
                    mm(ps[bg][:, :], gv_[:, k, 0:128], h2Tv[:, k, tg * 512:(tg + 1) * 512], k == 0, k == 15,
                       r=[gtok_] + h2T_all, w=[('ps', bg)])
                for k in range(16):
                    mm(ps[bu][:, :], gv_[:, k, 128:256], h2Tv[:, k, tg * 512:(tg + 1) * 512], k == 0, k == 15,
                       r=[gtok_] + h2T_all, w=[('ps', bu)])
                act(tmpg, ps[bg][:, :], AF.Silu, r=[('ps', bg)], w=['tmpg'])
                tt('dve', actv[:, ft, tg * 512:(tg + 1) * 512], tmpg, ps[bu][:, :], ALU.mult, r=['tmpg', ('ps', bu)],
                   w=[('actT', ft)])
        for a in range(8):
            for cg in range(4):
                bi = 4 + (a * 4 + cg) % 4
                for ft in range(4):
                    mm(ps[bi][:, :], actv[:, ft, a * 128:(a + 1) * 128], dv[:, ft, cg * 512:(cg + 1) * 512], ft == 0,
                       ft == 3, r=[('actT', ft), dtok], w=[('ps', bi)])
                csl = slice(cg * 512, (cg + 1) * 512)
                tt('dve', tmpf, ps[bi][:, :], vecD[:, csl], ALU.mult, r=[('ps', bi), ('vec', id(vecD))], w=['tmpf'])
                tt('pool', x1v[:, a, csl], x1v[:, a, csl], tmpf, ALU.add, r=['tmpf', ('x1', a)], w=[('x1', a)])
    tr.fence()
    dma(vecB, gvec_d[2, :].partition_broadcast(128), w=[('vec', id(vecB))])
    outs = []
    for a in range(8):
        rms_rstd(a, 8 + a)
        stt(hf2, x1v[:, a, :], st2[:, 8 + a:9 + a], vecB, ALU.mult, ALU.mult,
            r=[('x1', a), ('st2', 8 + a), ('vec', id(vecB))], w=['hf2'])
        outs.append(dma(out_d[a * 128:(a + 1) * 128, :], hf2, r=['hf2'], w=[('out', a)], q='sp'))
    fin_deps = list(outs)
    for nm in dbg_d:
        pass
    blk = es.enter_context(nc.Block())
    tr.emit(nc, es, blk, fin_deps + tr.dbg_ops)
    es.close()
    return nc


def _perm_cols(hh):
    cols = []
    rw = 2048
    wl0, al0, gl0 = rw + 3072, rw + 3072 + 64, rw + 3072 + 128
    cols += list(range(wl0, wl0 + 64)) + list(range(al0, al0 + 64)) + list(range(gl0, gl0 + 160))
    for j in range(4):
        c = hh * 512 + j * 128
        for q in range(3):
            cols += list(range(rw + q * 1024 + c, rw + q * 1024 + c + 128))
    for h2 in range(2):
        c = hh * 512 + h2 * 256
        cols += list(range(c, c + 256)) + list(range(1024 + c, 1024 + c + 256))
    return np.array(cols)


_CACHE = {}


def kernel(x, c, w_ada, b_ada, norm_mix_g, w_in, conv_w, conv_b, lru_wa, lru_ba, lru_wx, lru_bx, lru_lambda,
           rwkv_mu, rwkv_w0, rwkv_w2, rwkv_a0, rwkv_a2, rwkv_g2, rwkv_k_k, rwkv_k_a, rwkv_r_k, rwkv_ln_g,
           rwkv_ln_b, w_out, norm_ffn_g, w_gu, w_down, final_norm_g, _dbg=None, _stage=None, _ncores=8):
    f = lambda a: np.ascontiguousarray(np.asarray(a, dtype=np.float32))
    x, c = f(x), f(c)
    if 'nc' not in _CACHE or _dbg is not None:
        _CACHE['nc'] = build(_dbg, _stage)
    nc = _CACHE['nc']
    ident = np.eye(128, dtype=np.float32)
    masks = np.zeros((128, 4, 128), np.float32)
    masks[:, 0] = np.triu(np.ones((128, 128)), 1)
    masks[:, 1] = np.triu(np.ones((128, 128)), 0)
    masks[:, 2] = np.tril(np.ones((128, 128)), -1)
    masks[0:64, 3, 0:64] = 1
    masks[64:128, 3, 64:128] = 1
    mu = f(rwkv_mu)[0]
    wout_perm = np.concatenate([np.arange(0, 512), np.arange(1024, 1536), np.arange(512, 1024), np.arange(1536, 2048)])
    wout_p = f(w_out)[0][wout_perm]
    wgu = f(w_gu)[0]
    wdown = f(w_down)[0]
    wada = f(w_ada)[0]
    bada = f(b_ada)[0]
    gvec = np.stack([f(norm_mix_g)[0], f(norm_ffn_g)[0], f(final_norm_g)])
    in_maps = []
    for r in range(8):
        b, hh = r // 2, r % 2
        cols = _perm_cols(hh)
        win = np.ascontiguousarray(f(w_in)[0][:, cols])
        pcol = np.zeros((128, 80), np.float32)
        rkm = np.zeros((128, 8), np.float32)
        for j in range(4):
            ch = hh * 512 + j * 128 + np.arange(128)
            pcol[:, 8 * j + 0] = mu[ch]
            pcol[:, 8 * j + 1] = mu[1024 + ch]
            pcol[:, 8 * j + 2] = mu[2048 + ch]
            pcol[:, 8 * j + 3] = f(rwkv_k_k)[0][ch]
            pcol[:, 8 * j + 4] = f(rwkv_k_a)[0][ch]
            pcol[:, 8 * j + 5] = f(rwkv_w0)[0][ch]
            pcol[:, 8 * j + 6] = f(rwkv_a0)[0][ch]
            rkf = f(rwkv_r_k)[0].reshape(-1)[ch]
            rkm[0:64, 2 * j] = rkf[0:64]
            rkm[64:128, 2 * j + 1] = rkf[64:128]
            pcol[:, 40 + 8 * j + 0:40 + 8 * j + 4] = f(conv_w)[0][:, ch].T
            pcol[:, 40 + 8 * j + 4] = f(conv_b)[0][ch]
            pcol[:, 40 + 8 * j + 5] = f(lru_ba)[0][ch]
            pcol[:, 40 + 8 * j + 6] = f(lru_bx)[0][ch]
            pcol[:, 40 + 8 * j + 7] = f(lru_lambda)[0][ch]
        pcol[:, 32] = mu[3072:3200]
        pcol[:, 33] = mu[3200:3328]
        pcol[0:32, 34] = mu[3328:3360]
        chs = hh * 512 + np.arange(512)
        rowv = np.stack([f(rwkv_ln_g)[0][chs], f(rwkv_ln_b)[0][chs]])
        wa2 = np.concatenate([f(rwkv_w2)[0][:, chs], f(rwkv_a2)[0][:, chs]], axis=0)
        g2 = np.ascontiguousarray(f(rwkv_g2)[0][:, chs])
        lruw = np.stack([f(lru_wa)[0][2 * hh:2 * hh + 2], f(lru_wx)[0][2 * hh:2 * hh + 2]])
        hsel = np.zeros((128, 2), np.float32)
        hsel[:, hh] = 1.0
        m = dict(x=x[b], xown=np.ascontiguousarray(x[b, hh * 1024:(hh + 1) * 1024]),
                 cb=np.ascontiguousarray(c[b].reshape(128, 16)), wada=wada, bada=bada, gvec=gvec, win=win, pcol=pcol,
                 rowv=np.ascontiguousarray(rowv), rkm=rkm, wa2=np.ascontiguousarray(wa2), g2=g2,
                 lruw=np.ascontiguousarray(lruw), wout=wout_p, wgu=wgu, wdown=wdown, ident=ident, masks=masks, hsel=hsel)
        in_maps.append(m)
    res = run_bass_kernel_spmd(nc, in_maps[:_ncores], core_ids=list(range(_ncores)))
    if _stage is not None:
        return None, res
    out = np.zeros((4, 2048, 2048), np.float32)
    for r in range(8):
        b, hh = r // 2, r % 2
        out[b, hh * 1024:(hh + 1) * 1024] = res.results[r]["out"]
    if _dbg is not None:
        return out, res
    return out
```

```python
import numpy as np
from contextlib import ExitStack
import concourse.bass as bass
import concourse.mybir as mybir
from concourse.bass_utils import run_bass_kernel_spmd

F32 = mybir.dt.float32
BF16 = mybir.dt.bfloat16
AF = mybir.ActivationFunctionType
ALU = mybir.AluOpType
AX = mybir.AxisListType

D = 2048
T = 2048
TO = 1024
DFF = 5632
NWIN = 2848
C0 = float(np.exp(-0.5))
RMS_EPS = 1e-6
GN_EPS = 64e-5
L2_EPS = 1e-12
NDMASEM = 6


class Tr:
    def __init__(self):
        self.ops = []
        self.lw = {}
        self.rd = {}
        self.fenced = 0
        self.dbg_ops = []

    def op(self, eng, fn, r=(), w=(), dma=False, cc=False):
        i = len(self.ops)
        deps = set()
        psr = [t for t in r if isinstance(t, tuple) and t and t[0] == 'ps']
        if psr:
            r = [t for t in r if t not in psr]
            w = list(w) + psr
        for t in r:
            x = self.lw.get(t)
            if x is not None:
                deps.add(x)
        for t in w:
            x = self.lw.get(t)
            if x is not None:
                deps.add(x)
            for y in self.rd.get(t, ()):
                deps.add(y)
        deps.discard(i)
        for t in r:
            self.rd.setdefault(t, []).append(i)
        for t in w:
            self.lw[t] = i
            self.rd[t] = []
        self.ops.append(dict(eng=eng, fn=fn, deps=deps, dma=dma, cc=cc))
        return i

    def fence(self):
        last = {}
        dmas = []
        for i, o in enumerate(self.ops):
            if o['dma'] or o['cc']:
                dmas.append(i)
            elif o['fn'] is not None:
                last[o['eng']] = i
        deps = set(last.values()) | set(dmas[self.fenced:])
        self.fenced = len(dmas)
        for e in ['pe', 'act', 'dve', 'pool', 'sp']:
            i = self.op(e, None, (), ())
            self.ops[i]['deps'] = set(deps)

    def emit(self, nc, es, block, final_deps):
        ops = self.ops
        engs = ['pe', 'act', 'dve', 'pool', 'sp']
        csem = {e: es.enter_context(nc.semaphore("c_" + e)) for e in engs}
        dsem = {e: [es.enter_context(nc.semaphore("d_%s%d" % (e, k))) for k in range(NDMASEM)]
                for e in ['act', 'pool', 'sp']}
        ccsem = es.enter_context(nc.semaphore("ccs"))
        fin = self.op('sp', None, (), ())
        ops[fin]['deps'] = set(final_deps)
        used = set()
        for o in ops:
            for d in o['deps']:
                used.add(d)
        ccnt = {e: 0 for e in engs}
        dcnt = {e: 0 for e in dsem}
        dval = {}
        cccount = 0
        for i, o in enumerate(ops):
            e = o['eng']
            if o['cc']:
                cccount += 1
                o['sem'] = ccsem
                o['val'] = cccount
            elif o['dma']:
                k = dcnt[e] % NDMASEM
                dcnt[e] += 1
                s = dsem[e][k]
                prev = dval.get((e, k), 0)
                o['sem'] = s
                o['prev'] = prev
                o['val'] = prev + 16
                dval[(e, k)] = prev + 16
            elif o['fn'] is not None and i in used:
                ccnt[e] += 1
                o['sem'] = csem[e]
                o['val'] = ccnt[e]
            else:
                o['sem'] = None
        per = {e: [] for e in engs}
        for i, o in enumerate(ops):
            per[o['eng']].append(i)

        def run(e, eng):
            waited = {}

            def wait(sem, val):
                key = sem.num if hasattr(sem, 'num') else id(sem)
                if waited.get(key, 0) >= val:
                    return
                eng.wait_ge(sem, val)
                waited[key] = val

            for i in per[e]:
                o = ops[i]
                for d in sorted(o['deps']):
                    p = ops[d]
                    if p['sem'] is None:
                        continue
                    if p['eng'] == e and e == 'pe' and not p['dma'] and not p['cc']:
                        continue
                    wait(p['sem'], p['val'])
                if o['dma'] and not o['cc'] and o['prev'] > 0:
                    wait(o['sem'], o['prev'])
                if o['fn'] is None:
                    continue
                ins = o['fn'](eng)
                if o['sem'] is not None:
                    if o['cc']:
                        ins.then_inc(o['sem'], 1)
                    elif o['dma']:
                        ins.then_inc(o['sem'], 16)
                    else:
                        ins.then_inc(o['sem'], 1)

        @block.tensor
        def _(eng):
            run('pe', eng)

        @block.scalar
        def _(eng):
            run('act', eng)

        @block.vector
        def _(eng):
            run('dve', eng)

        @block.gpsimd
        def _(eng):
            run('pool', eng)

        @block.sync
        def _(eng):
            run('sp', eng)


class Arena:
    def __init__(self, ap_f32, nwords):
        self.ap = ap_f32
        self.n = nwords
        self.top = 0
        self.hi = 0

    def mark(self):
        return self.top

    def release(self, m):
        self.top = m

    def f32(self, cols, parts=128):
        a = self.top
        self.top += cols
        self.hi = max(self.hi, self.top)
        assert self.top <= self.n, ("SBUF arena overflow", self.top, self.n)
        return self.ap[0:parts, a:a + cols]

    def bf16(self, cols, parts=128):
        w = (cols + 1) // 2
        a = self.top
        self.top += w
        self.hi = max(self.hi, self.top)
        assert self.top <= self.n, ("SBUF arena overflow", self.top, self.n)
        return self.ap[0:parts, a:a + w].bitcast(BF16)[:, 0:cols]


class _Done(Exception):
    def __init__(self, nc):
        self.nc = nc


def build(dbg=None, stage=None):
    try:
        return build_nc(dbg, stage)
    except _Done as d:
        return d.nc


def build_nc(dbg=None, stage=None):
    nc = bass.Bass("TRN2", target_bir_lowering=False)
    es = ExitStack()
    tr = Tr()

    def stage_end(name):
        if stage == name:
            blk_ = es.enter_context(nc.Block())
            tr.emit(nc, es, blk_, list(tr.dbg_ops))
            es.close()
            raise _Done(nc)

    def din(name, shape, dt=F32):
        return nc.dram_tensor(name, list(shape), dt, kind="ExternalInput").ap()

    x_d = din("x", [T, D])
    xown_d = din("xown", [TO, D])
    cb_d = din("cb", [128, 16])
    wada_d = din("wada", [D, 6 * D])
    bada_d = din("bada", [6 * D])
    gvec_d = din("gvec", [3, D])
    win_d = din("win", [128, 16 * NWIN])
    pcol_d = din("pcol", [128, 80])
    rowv_d = din("rowv", [2, 512])
    rk_d = din("rkm", [128, 8])
    wa2_d = din("wa2", [128, 512])
    g2_d = din("g2", [160, 512])
    lruw_d = din("lruw", [2, 2, 256, 256])
    wout_d = din("wout", [128, 4 * 8192])
    wgu_d = din("wgu", [128, 44 * 4096])
    wdown_d = din("wdown", [128, 11 * 8192])
    ident_d = din("ident", [128, 128])
    masks_d = din("masks", [128, 4, 128])
    hsel_d = din("hsel", [128, 2])
    out_d = nc.dram_tensor("out", [TO, D], F32, kind="ExternalOutput").ap()
    ysrc_l = [nc.dram_tensor("ysrc%d" % i, [1024, 512], BF16, kind="Internal").ap() for i in range(4)]
    ydst_l = [nc.dram_tensor("ydst%d" % i, [2048, 512], BF16, kind="Internal").ap() for i in range(4)]
    dbg_d = {}
    if dbg:
        for nm, shp in dbg.items():
            dbg_d[nm] = nc.dram_tensor("dbg_" + nm, list(shp), F32, kind="ExternalOutput").ap()

    NW = 53000
    arena_t = es.enter_context(nc.sbuf_tensor("arena", [128, NW], F32))
    ar = Arena(arena_t, NW)
    ps = [es.enter_context(nc.psum_tensor("ps%d" % i, [128, 512], F32)) for i in range(8)]

    dmaq = ['sp', 'act']
    dctr = [0]

    def dma(out, in_, r=(), w=(), q=None, **kw):
        if q is None:
            q = dmaq[dctr[0] % len(dmaq)]
            dctr[0] += 1
        return tr.op(q, lambda e: e.dma_start(out=out, in_=in_, **kw), r, w, dma=True)

    def mm(out, lhsT, rhs, start, stop, r=(), w=()):
        return tr.op('pe', lambda e: e.matmul(out, lhsT, rhs, start=start, stop=stop), r, w)

    def tp(out, in_, idn, r=(), w=()):
        return tr.op('pe', lambda e: e.transpose(out, in_, idn), r, w)

    def act(out, in_, func, r=(), w=(), bias=None, scale=None, accum_out=None):
        kw = {}
        if bias is not None:
            kw['bias'] = bias
        if scale is not None:
            kw['scale'] = scale
        if accum_out is not None:
            kw['accum_out'] = accum_out
        return tr.op('act', lambda e: e.activation(out, in_, func, **kw), r, w)

    def ts(eng, out, in0, s1, s2, op0, op1, r=(), w=()):
        if op1 is None:
            return tr.op(eng, lambda e: e.tensor_scalar(out, in0, s1, None, op0), r, w)
        return tr.op(eng, lambda e: e.tensor_scalar(out, in0, s1, s2, op0, op1), r, w)

    def tt(eng, out, in0, in1, op, r=(), w=()):
        return tr.op(eng, lambda e: e.tensor_tensor(out, in0, in1, op), r, w)

    def stt(out, in0, sc, in1, op0, op1, r=(), w=()):
        return tr.op('dve', lambda e: e.scalar_tensor_tensor(out, in0, sc, in1, op0, op1), r, w)

    def cp(eng, out, in_, r=(), w=()):
        if eng == 'act':
            return tr.op('act', lambda e: e.activation(out, in_, AF.Copy), r, w)
        return tr.op(eng, lambda e: e.tensor_copy(out, in_), r, w)

    def memset(eng, ap, val, w=()):
        return tr.op(eng, lambda e: e.memset(ap, val), (), w)

    def dump(name, ap, r):
        if name in dbg_d:
            i_ = dma(dbg_d[name], ap, r=r, w=[('dbgout', name)], q='sp')
            tr.dbg_ops.append(i_)
            return i_
        return None

    ident = ar.f32(128)
    masks = ar.f32(512)
    pcol = ar.f32(80)
    rkm = ar.f32(8)
    hsel = ar.f32(2)
    ones = ar.f32(128)
    epsc = ar.f32(4)
    wa2 = ar.f32(512)
    g2a = ar.f32(512)
    g2b = ar.f32(512, 32)
    rowv = ar.f32(1024)
    cb = ar.f32(16)
    identb = ar.bf16(128)
    identb2 = identb
    dma(ident, ident_d, w=['ident'])
    dma(masks, masks_d.rearrange("p a b -> p (a b)"), w=['masks'])
    dma(pcol, pcol_d, w=['pcol'])
    dma(rkm, rk_d, w=['rkm'])
    dma(hsel, hsel_d, w=['hsel'])
    dma(wa2, wa2_d, w=['wa2'])
    dma(g2a, g2_d[0:128, :], w=['g2'])
    dma(g2b, g2_d[128:160, :], w=['g2'])
    dma(rowv, rowv_d.rearrange("a n -> (a n)").partition_broadcast(128), w=['rowv'])
    dma(cb, cb_d, w=['cb'])
    memset('pool', ones, 1.0, w=['ones'])
    memset('pool', epsc[:, 0:1], RMS_EPS, w=['epsc'])
    memset('pool', epsc[:, 1:2], GN_EPS, w=['epsc'])
    memset('pool', epsc[:, 2:3], 0.0, w=['epsc'])
    MUS = masks[:, 0:128]
    MUI = masks[:, 128:256]
    MLS = masks[:, 256:384]
    BLK = masks[:, 384:512]
    lng = rowv[:, 0:512]
    lnb = rowv[:, 512:1024]

    def pc(i):
        return pcol[:, i:i + 1]

    dcol = ar.f32(64)
    mu_idx = [8 * j + q for j in range(4) for q in range(3)] + [32, 33, 34]
    for n, mi in enumerate(mu_idx):
        ts('pool', dcol[:, n:n + 1], pc(mi), -1.0, 1.0, ALU.mult, ALU.add, r=['pcol'], w=['dcol'])

    def omm(mi):
        return dcol[:, mu_idx.index(mi):mu_idx.index(mi) + 1]

    for j in range(4):
        ts('pool', dcol[:, 16 + j:17 + j], pc(8 * j + 4), -1.0, 1.0, ALU.mult, ALU.add, r=['pcol'], w=['dcol'])
        act(dcol[:, 32 + j:33 + j], pc(40 + 8 * j + 7), AF.Exp, r=['pcol'], w=['dcol'], scale=-1.0)
        act(dcol[:, 36 + j:37 + j], dcol[:, 32 + j:33 + j], AF.Ln, r=['dcol'], w=['dcol'], bias=1.0)
        ts('pool', dcol[:, 24 + j:25 + j], dcol[:, 36 + j:37 + j], -8.0, None, ALU.mult, None, r=['dcol'], w=['dcol'])
        ts('pool', dcol[:, 28 + j:29 + j], dcol[:, 36 + j:37 + j], -16.0, None, ALU.mult, None, r=['dcol'], w=['dcol'])

    cact = ar.f32(16)
    act(cact, cb, AF.Silu, r=['cb'], w=['cact'])
    cbc = ar.f32(16 * 128)
    cp('dve', cbc.rearrange("p (k m) -> p k m", k=16), cact.unsqueeze(2).to_broadcast([128, 16, 128]),
       r=['cact'], w=['cbc'])
    m_phase = ar.mark()
    MB = {}

    def alloc_mb():
        MB['bad'] = ar.f32(2048)
        MB['gv'] = ar.f32(2048)
        MB['wst'] = [ar.f32(2 * 2048), ar.f32(2 * 2048)]
    wada_v = wada_d.rearrange("(p k) n -> p k n", k=16)

    def mod_chunk(jm, evac):
        nst = 0
        for kq in range(8):
            b = MB['wst'][nst % 2]
            nst += 1
            dma(b.rearrange("p (k n) -> p k n", k=2), wada_v[:, kq * 2:(kq + 1) * 2, jm * 2048:(jm + 1) * 2048],
                w=[('wst', id(b))])
            for kk in range(2):
                k = kq * 2 + kk
                for q in range(4):
                    mm(ps[q][:, :], cbc[:, k * 128:(k + 1) * 128], b[:, kk * 2048 + q * 512: kk * 2048 + (q + 1) * 512],
                       k == 0, k == 15, r=['cbc', ('wst', id(b))], w=[('ps', q)])
        for q in range(4):
            evac(q, ps[q][:, :])

    def mod_vec(jm, dst, gidx=None):
        bad = MB['bad']
        gv = MB['gv']
        dma(bad, bada_d[jm * 2048:(jm + 1) * 2048].partition_broadcast(128), w=['bad'])
        if gidx is not None:
            dma(gv, gvec_d[gidx, :].partition_broadcast(128), w=['gv'])

        def ev(q, p):
            sl = slice(q * 512, (q + 1) * 512)
            tt('dve', dst[:, sl], p, bad[:, sl], ALU.add, r=[('ps', q), 'bad'], w=[('vec', id(dst))])
            if gidx is not None:
                stt(dst[:, sl], dst[:, sl], 1.0, gv[:, sl], ALU.add, ALU.mult, r=[('vec', id(dst)), 'gv'],
                    w=[('vec', id(dst))])
        mod_chunk(jm, ev)

    hT = ar.f32(16 * 2048 // 2)
    hTb = hT.bitcast(BF16).rearrange("p (k t) -> p k t", k=16)
    p1 = ar.mark()
    alloc_mb()
    gm1 = ar.f32(2048)
    shm = ar.f32(2048)
    mod_vec(1, gm1, gidx=0)
    mod_vec(0, shm)
    xs = [ar.f32(2048), ar.f32(2048)]
    hf = ar.f32(2048)
    hb = ar.bf16(2048)
    junk = ar.bf16(2048)
    ssq = ar.f32(16)
    rst = ar.f32(16)
    cp('dve', identb, ident, r=['ident'], w=['identb', 'identb2'])
    for tt_i in range(16):
        xb = xs[tt_i % 2]
        xt = ('xs', tt_i % 2)
        dma(xb, x_d[tt_i * 128:(tt_i + 1) * 128, :], w=[xt], q='sp')
        act(junk, xb, AF.Square, r=[xt], w=['junk', ('ssq', tt_i)], accum_out=ssq[:, tt_i:tt_i + 1])
        act(rst[:, tt_i:tt_i + 1], ssq[:, tt_i:tt_i + 1], AF.Sqrt, r=[('ssq', tt_i), 'epsc'], w=[('rst', tt_i)],
            bias=epsc[:, 0:1], scale=1.0 / D)
        tr.op('dve', lambda e, a=rst[:, tt_i:tt_i + 1]: e.reciprocal(a, a), [('rst', tt_i)], [('rst', tt_i)])
        stt(hf, xb, rst[:, tt_i:tt_i + 1], gm1, ALU.mult, ALU.mult, r=[xt, ('rst', tt_i), ('vec', id(gm1))], w=['hf'])
        tt('pool', hb, hf, shm, ALU.add, r=['hf', ('vec', id(shm))], w=['hb'])
        for half in range(2):
            pb = ps[4 + (tt_i * 2 + half) % 4]
            pt = ('ps', 4 + (tt_i * 2 + half) % 4)
            pbv = pb[:, :].bitcast(BF16)
            for q in range(8):
                dc = half * 8 + q
                tp(pbv[:, q * 128:(q + 1) * 128], hb[:, dc * 128:(dc + 1) * 128], identb, r=['hb', 'identb'], w=[pt])
            cp('act' if half == 0 else 'dve', hTb[:, half * 8:(half + 1) * 8, tt_i * 128:(tt_i + 1) * 128],
               pbv.rearrange("p (q t) -> p q t", q=8), r=[pt], w=[('hT', tt_i)])
    hT_all = [('hT', i) for i in range(16)]
    if 'hT' in dbg_d:
        tmpd = ar.f32(2048)
        cp('dve', tmpd, hTb[:, 3, :], r=hT_all, w=['tmpd'])
        dump('hT', tmpd, ['tmpd'])
    dump('gm1', gm1, [('vec', id(gm1))])
    stage_end('p1')
    tr.fence()
    ar.release(p1)

    wbufs = [ar.bf16(16 * 512), ar.bf16(16 * 512)]
    wctr = [0]

    def load_flat(dst, src, off, n, tok, piece=2048):
        for a in range(0, n, piece):
            b_ = min(n, a + piece)
            dma(dst[:, a:b_], src[:, off + a: off + b_], w=[tok], q='pool', max_dma_last_dim=8192)

    def load_w(c0, ncols):
        b = wbufs[wctr[0] % 2]
        wctr[0] += 1
        bv = b[:, 0:16 * ncols].rearrange("p (k n) -> p k n", k=16)
        tok = ('wbuf', id(b))
        load_flat(b, win_d, 16 * c0, 16 * ncols, tok)
        return bv, tok

    psrr = [0]

    def proj(bv, tok, cofs, ncol, tg, tlen=512):
        bi = psrr[0] % 4
        psrr[0] += 1
        p = ps[bi]
        for k in range(16):
            mm(p[0:ncol, 0:tlen], bv[:, k, cofs:cofs + ncol], hTb[:, k, tg * 512: tg * 512 + tlen], k == 0, k == 15,
               r=[tok] + hT_all, w=[('ps', bi)])
        return p[0:ncol, 0:tlen], ('ps', bi)

    pmu = ar.f32(516)
    memset('pool', pmu[:, 0:1], 0.0, w=['pmu'])

    def lerp_evac(p, ptok, nrow, mui, dst, dtok, tg):
        if tg > 0:
            cp('pool', pmu[0:nrow, 0:1], pmu[0:nrow, 512:513], r=['pmu'], w=['pmu'])
        else:
            memset('pool', pmu[0:nrow, 0:1], 0.0, w=['pmu'])
        act(pmu[0:nrow, 1:513], p, AF.Copy, r=[ptok, 'pcol'], w=['pmu'], scale=pc(mui)[0:nrow, :])
        stt(dst[0:nrow, tg * 512:(tg + 1) * 512], p, omm(mui)[0:nrow, :], pmu[0:nrow, 0:512], ALU.mult, ALU.add,
            r=[ptok, 'pmu', 'dcol'], w=[dtok])

    lwin = ar.f32(2048)
    glB = ar.f32(2048)
    glC = ar.f32(2048, 32)
    bv, tok = load_w(0, 288)
    for tg in range(4):
        p, ptk = proj(bv, tok, 0, 128, tg)
        lerp_evac(p, ptk, 128, 32, lwin, ('lwin', tg), tg)
        act(lwin[0:64, tg * 512:(tg + 1) * 512], lwin[0:64, tg * 512:(tg + 1) * 512], AF.Tanh, r=[('lwin', tg)],
            w=[('lwin', tg)])
    for tg in range(4):
        p, ptk = proj(bv, tok, 128, 128, tg)
        lerp_evac(p, ptk, 128, 33, glB, ('glB', tg), tg)
        act(glB[:, tg * 512:(tg + 1) * 512], glB[:, tg * 512:(tg + 1) * 512], AF.Sigmoid, r=[('glB', tg)],
            w=[('glB', tg)])
    for tg in range(4):
        p, ptk = proj(bv, tok, 256, 32, tg)
        lerp_evac(p, ptk, 32, 34, glC, ('glC', tg), tg)
        act(glC[:, tg * 512:(tg + 1) * 512], glC[:, tg * 512:(tg + 1) * 512], AF.Sigmoid, r=[('glC', tg)],
            w=[('glC', tg)])
    dump('lwin', lwin, [('lwin', g) for g in range(4)])
    dump('glB', glB, [('glB', g) for g in range(4)])
    stage_end('lora')

    def pq(bank, q0, nq=1, rows=128):
        return ps[bank][0:rows, q0 * 128:(q0 + nq) * 128], [('ps', bank)]

    def proj_prev(bv, tok, cofs, ncol, tcol, width, bank, col):
        for k in range(16):
            mm(ps[bank][0:ncol, col:col + width], bv[:, k, cofs:cofs + ncol], hTb[:, k, tcol:tcol + width], k == 0,
               k == 15, r=[tok] + hT_all, w=[('ps', bank)])
        return ps[bank][0:ncol, col:col + width], ('ps', bank)

    mask4 = ar.f32(512)
    cp('pool', mask4[:, 0:128], MUS, r=['masks'], w=['mask4'])
    cp('pool', mask4[:, 128:256], MUI, r=['masks'], w=['mask4'])
    cp('pool', mask4[:, 256:384], MUS, r=['masks'], w=['mask4'])
    cp('pool', mask4[:, 384:512], MUI, r=['masks'], w=['mask4'])

    ydma = []
    mrw = ar.mark()
    Hst = [[ar.f32(64), ar.f32(64)], [ar.f32(64), ar.f32(64)]]
    btp = [ar.f32(512), ar.f32(512)]
    ktp = [ar.f32(512), ar.f32(512)]
    for z_ in btp + ktp:
        memset('pool', z_, 0.0, w=[('pad', id(z_))])
    hcnt = [0]
    pqs = [[ar.f32(384), ar.f32(384)] for _ in range(2)]
    amat = [ar.f32(512) for _ in range(2)]
    trn = ar.f32(384)
    Xs = ar.f32(128)
    Us = ar.f32(128)
    gtok = ar.f32(512)
    s_sb = ar.f32(8)
    ybT = ar.bf16(512)
    yb_ = ar.f32(128)
    yc_ = ar.f32(128)
    sq_ = ar.f32(128)
    st_ = ar.f32(8)
    yfb = ar.bf16(128)
    dec = ar.f32(4)
    A = {}
    for nm in ['r', 'k', 'v', 'sg', 'L', 'a', 'kk', 'EL', 'tmp']:
        A[nm] = ar.f32(512)
    tk = lambda nm: ('rw', nm)
    for j in range(4):
        bv, tok = load_w(288 + 384 * j, 384)
        for par_ in range(2):
            for hd_ in range(2):
                memset('pool', Hst[par_][hd_], 0.0, w=[('H', par_)])
        for tb in range(4):
            tsl = slice(tb * 512, (tb + 1) * 512)
            for q, nm in enumerate(['r', 'k', 'v']):
                p, ptk = proj(bv, tok, 128 * q, 128, tb)
                if tb > 0:
                    pp, pptk = proj_prev(bv, tok, 128 * q, 128, tb * 512 - 1, 1, 7, 0)
                    act(pmu[:, 0:1], pp, AF.Copy, r=[pptk, 'pcol'], w=['pmu'], scale=pc(8 * j + q))
                else:
                    memset('pool', pmu[:, 0:1], 0.0, w=['pmu'])
                act(pmu[:, 1:513], p, AF.Copy, r=[ptk, 'pcol'], w=['pmu'], scale=pc(8 * j + q))
                stt(A[nm], p, omm(8 * j + q), pmu[:, 0:512], ALU.mult, ALU.add, r=[ptk, 'pmu', 'dcol'], w=[tk(nm)])
            pw, pwt = pq(4, 0, 4)
            mm(pw, wa2[0:64, j * 128:(j + 1) * 128], lwin[0:64, tsl], True, True, r=['wa2', ('lwin', tb)], w=pwt)
            act(A['sg'], pw, AF.Sigmoid, r=pwt + ['pcol'], w=[tk('sg')], bias=pc(8 * j + 5))
            pa, pat = pq(5, 0, 4)
            mm(pa, wa2[64:128, j * 128:(j + 1) * 128], lwin[64:128, tsl], True, True, r=['wa2', ('lwin', tb)], w=pat)
            act(A['a'], pa, AF.Sigmoid, r=pat + ['pcol'], w=[tk('a')], bias=pc(8 * j + 6))
            for c in range(4):
                cs = slice(c * 128, (c + 1) * 128)
                tr.op('dve', lambda e, o=A['L'][:, cs], d1=A['sg'][:, cs]: e.tensor_tensor_scan(
                    o, ones, d1, 0.0, ALU.mult, ALU.add), [tk('sg'), 'ones'], [tk('L')])
            tt('pool', A['tmp'], A['L'], A['sg'], ALU.subtract, r=[tk('L'), tk('sg')], w=[tk('tmp')])
            act(A['tmp'], A['tmp'], AF.Exp, r=[tk('tmp')], w=[tk('tmp')], scale=-C0)
            act(A['EL'], A['L'], AF.Exp, r=[tk('L')], w=[tk('EL')], scale=-C0)
            act(A['L'], A['L'], AF.Exp, r=[tk('L')], w=[tk('L')], scale=C0)
            ts('pool', A['kk'], A['k'], pc(8 * j + 3), None, ALU.mult, None, r=[tk('k'), 'pcol'], w=[tk('kk')])
            act(A['sg'], A['kk'], AF.Square, r=[tk('kk')], w=[tk('sg')])
            pn, pnt = pq(6, 0, 4)
            mm(pn, BLK, A['sg'], True, True, r=['masks', tk('sg')], w=pnt)
            act(A['sg'], pn, AF.Sqrt, r=pnt, w=[tk('sg')])
            ts('dve', A['sg'], A['sg'], L2_EPS, None, ALU.max, None, r=[tk('sg')], w=[tk('sg')])
            tr.op('dve', lambda e, a_=A['sg']: e.reciprocal(a_, a_), [tk('sg')], [tk('sg')])
            tt('dve', A['kk'], A['kk'], A['sg'], ALU.mult, r=[tk('kk'), tk('sg')], w=[tk('kk')])
            ts('dve', A['sg'], A['a'], pc(8 * j + 4), dcol[:, 16 + j:17 + j], ALU.mult, ALU.add,
               r=[tk('a'), 'pcol', 'dcol'], w=[tk('sg')])
            tt('dve', A['k'], A['k'], A['sg'], ALU.mult, r=[tk('k'), tk('sg')], w=[tk('k')])
            tt('pool', A['sg'], A['r'], A['k'], ALU.mult, r=[tk('r'), tk('k')], w=[tk('sg')])
            psS, psSt = pq(7, 1, 1)
            for c in range(4):
                mm(psS[:, 2 * c:2 * c + 2], A['sg'][:, c * 128:(c + 1) * 128], rkm[:, 2 * j:2 * j + 2], True, True,
                   r=[tk('sg'), 'rkm'], w=psSt)
            cp('act', s_sb, psS[:, 0:8], r=psSt, w=['s_sb'])
            tt('pool', A['a'], A['kk'], A['a'], ALU.mult, r=[tk('kk'), tk('a')], w=[tk('a')])
            tt('dve', A['r'], A['r'], A['EL'], ALU.mult, r=[tk('r'), tk('EL')], w=[tk('r')])
            tt('pool', A['k'], A['k'], A['L'], ALU.mult, r=[tk('k'), tk('L')], w=[tk('k')])
            tt('dve', A['a'], A['a'], A['L'], ALU.mult, r=[tk('a'), tk('L')], w=[tk('a')])
            stt(A['kk'], A['kk'], -1.0, A['tmp'], ALU.mult, ALU.mult, r=[tk('kk'), tk('tmp')], w=[tk('kk')])
            cp('pool', dec, A['EL'].rearrange("p (c t) -> p c t", c=4)[:, :, 127], r=[tk('EL')], w=['dec'])
            decb = dec.unsqueeze(2).to_broadcast([128, 4, 128])
            tt('dve', A['L'].rearrange("p (c t) -> p c t", c=4), A['a'].rearrange("p (c t) -> p c t", c=4), decb,
               ALU.mult, r=[tk('a'), 'dec', tk('L')], w=[tk('L')])
            tt('pool', A['tmp'].rearrange("p (c t) -> p c t", c=4), A['k'].rearrange("p (c t) -> p c t", c=4), decb,
               ALU.mult, r=[tk('k'), 'dec', tk('tmp')], w=[tk('tmp')])
            rt_, kt_, bt_, at_, Bd_, Kd_ = A['r'], A['k'], A['a'], A['kk'], A['L'], A['tmp']
            for hd_ in range(2):
                hs_ = slice(64 * hd_, 64 * hd_ + 64)
                cp('pool', btp[hd_][hs_, :], bt_[hs_, :], r=[tk('a')], w=[('pad', id(btp[hd_]))])
                cp('act', ktp[hd_][hs_, :], kt_[hs_, :], r=[tk('k')], w=[('pad', id(ktp[hd_]))])
            if j == 0 and tb < 2:
                for nm_, ap_, t_ in [('rt', rt_, 'r'), ('kt', kt_, 'k'), ('bt', bt_, 'a'), ('at', at_, 'kk'),
                                     ('Bd', Bd_, 'L'), ('Kd', Kd_, 'tmp'), ('vv', A['v'], 'v')]:
                    dump('%s%d' % (nm_, tb), ap_, [tk(t_)])
            if j == 0 and tb == 0:
                stage_end('rwA')
            for c in range(4):
                cs = slice(tb * 512 + c * 128, tb * 512 + (c + 1) * 128)
                pg, pgt = pq(4, c, 1)
                mm(pg, glB[:, cs], g2a[:, j * 128:(j + 1) * 128], True, False, r=[('glB', tb), 'g2'], w=pgt)
                mm(pg, glC[0:32, cs], g2b[0:32, j * 128:(j + 1) * 128], False, True, r=[('glC', tb), 'g2'], w=pgt)
            pg4, pg4t = pq(4, 0, 4)
            cp('act', gtok, pg4, r=pg4t, w=['gtok'])
            if j == 0 and tb == 0:
                stage_end('rwA1')
            for c in range(4):
                cs = slice(c * 128, (c + 1) * 128)
                ptr_, ptrt = pq(4, 0, 3)
                tp(ptr_[:, 0:128], A['v'][:, cs], ident, r=[tk('v'), 'ident'], w=ptrt)
                tp(ptr_[:, 128:256], Bd_[:, cs], ident, r=[tk('L'), 'ident'], w=ptrt)
                tp(ptr_[:, 256:384], Kd_[:, cs], ident, r=[tk('tmp'), 'ident'], w=ptrt)
                cp('act', trn, ptr_, r=ptrt, w=['trn'])
                vtok, BdT, KdT = trn[:, 0:128], trn[:, 128:256], trn[:, 256:384]
                if j == 0 and tb == 0 and c == 0:
                    dump('trn', trn, ['trn'])
                    stage_end('rwA2')
                import os
                for hd in range(int(os.environ.get('KDBG_NHD', '2'))):
                    hs = slice(64 * hd, 64 * hd + 64)
                    pA, pAt = pq(5 + hd, 0, 4)
                    bpt, kpt = ('pad', id(btp[hd])), ('pad', id(ktp[hd]))
                    mm(pA[:, 0:128], btp[hd][:, cs], at_[:, cs], True, True, r=[bpt, tk('kk')], w=pAt)
                    mm(pA[:, 128:256], btp[hd][:, cs], rt_[:, cs], True, True, r=[bpt, tk('r')], w=pAt)
                    mm(pA[:, 256:384], ktp[hd][:, cs], at_[:, cs], True, True, r=[kpt, tk('kk')], w=pAt)
                    mm(pA[:, 384:512], ktp[hd][:, cs], rt_[:, cs], True, True, r=[kpt, tk('r')], w=pAt)
                    pQ, pQt = pq(7, 2 + hd, 1)
                    mm(pQ, at_[:, cs], btp[hd][:, cs], True, True, r=[bpt, tk('kk')], w=pQt)
                    tt('dve', amat[hd], pA, mask4, ALU.mult, r=pAt + ['mask4'], w=[('amat', hd)])
                    b0 = pqs[hd][0]
                    cp('act', b0[:, 0:128], amat[hd][:, 0:128], r=[('amat', hd)], w=[('pqs', hd, 0)])
                    tt('dve', b0[:, 128:256], pQ, MLS, ALU.mult, r=pQt + ['masks'], w=[('pqs', hd, 0)])
                    tt('pool', b0[:, 256:384], amat[hd][:, 0:128], ident, ALU.add, r=[('amat', hd), 'ident'],
                       w=[('pqs', hd, 0)])
                if j == 0 and tb == 0 and c == 0:
                    dump('amat0', amat[0], [('amat', 0)])
                    stage_end('rwA3')
                for it in range(int(os.environ.get('KDBG_NIT', '6'))):
                    for hd in range(2):
                        cur, nxt = pqs[hd][it % 2], pqs[hd][(it + 1) % 2]
                        ct, nt = ('pqs', hd, it % 2), ('pqs', hd, (it + 1) % 2)
                        P_, Q_, S_ = cur[:, 0:128], cur[:, 128:256], cur[:, 256:384]
                        pI, pIt = pq(5 + hd, 0, 3)
                        mm(pI[:, 0:128], P_, Q_, True, True, r=[ct], w=pIt)
                        cp('act', nxt[:, 128:256], pI[:, 0:128], r=pIt, w=[nt])
                        if it < 5:
                            mm(pI[:, 128:256], Q_, P_, True, True, r=[ct], w=pIt)
                            cp('act', nxt[:, 0:128], pI[:, 128:256], r=pIt, w=[nt])
                        mm(pI[:, 256:384], nxt[:, 128:256], S_, True, True, r=[ct, nt], w=pIt)
                        tt('dve', nxt[:, 256:384], pI[:, 256:384], S_, ALU.add, r=pIt + [ct], w=[nt])
                if j == 0 and tb == 0 and c < 2:
                    dump('amat%d' % c, amat[0], [('amat', 0)])
                    dump('Tt%d' % c, pqs[0][0][:, 256:384], [('pqs', 0, 0)])
                if j == 0 and tb == 0 and c == 0:
                    stage_end('rwB')
                Tt = [pqs[hd][0][:, 256:384] for hd in range(2)]
                Ttt = [('pqs', hd, 0) for hd in range(2)]
                Hc = Hst[hcnt[0] % 2]
                Hn = Hst[(hcnt[0] + 1) % 2]
                Hcp = Hc
                Hct, Hnt = ('H', hcnt[0] % 2), ('H', (hcnt[0] + 1) % 2)
                hcnt[0] += 1
                pX, pXt = pq(4, 3, 1)
                for hd in range(2):
                    hs = slice(64 * hd, 64 * hd + 64)
                    vs = slice(64 * hd, 64 * hd + 64)
                    mm(pX[:, vs], at_[:, cs], Hc[hd], True, False, r=[tk('kk'), Hct], w=pXt)
                    mm(pX[:, vs], amat[hd][:, 256:384], vtok[:, vs], False, True, r=[('amat', hd), 'trn'], w=pXt)
                cp('act', Xs, pX, r=pXt, w=['Xs'])
                pU, pUt = pq(7, 0, 1)
                for hd in range(2):
                    vs = slice(64 * hd, 64 * hd + 64)
                    mm(pU[:, vs], Tt[hd], Xs[:, vs], True, True, r=[Ttt[hd], 'Xs'], w=pUt)
                cp('act', Us, pU, r=pUt, w=['Us'])
                if j == 0 and tb == 0 and c == 0:
                    dump('Xs0', Xs, ['Xs'])
                    dump('Us0', Us, ['Us'])
                    stage_end('rwB1')
                pY, pYt = pq(4, 0, 1)
                for hd in range(2):
                    hs = slice(64 * hd, 64 * hd + 64)
                    vs = slice(64 * hd, 64 * hd + 64)
                    mm(pY[:, vs], rt_[:, cs], Hc[hd], True, False, r=[tk('r'), Hct], w=pYt)
                    mm(pY[:, vs], amat[hd][:, 128:256], Us[:, vs], False, False, r=[('amat', hd), 'Us'], w=pYt)
                    mm(pY[:, vs], amat[hd][:, 384:512], vtok[:, vs], False, True, r=[('amat', hd), 'trn'], w=pYt)
                pH, pHt = pq(4, 1, 1)
                mm(pH, BdT, Us, True, False, r=['trn', 'Us'], w=pHt)
                mm(pH, KdT, vtok, False, True, r=['trn'], w=pHt)
                for hd in range(2):
                    hs = slice(64 * hd, 64 * hd + 64)
                    stt(Hn[hd][hs, :], Hc[hd][hs, :], dec[hs, c:c + 1], pH[hs, 64 * hd:64 * hd + 64], ALU.mult, ALU.add,
                        r=[Hct, 'dec'] + pHt, w=[Hnt])
                if j == 0 and tb == 0 and c == 0:
                    dump('Hn0', Hn[0], [Hnt])
                    dump('Hm0', Hn[1], [Hnt])
                    stage_end('rwB2')
                cp('act', yb_, pY, r=pYt, w=['yb'])
                if j == 0 and tb == 0 and c < 2:
                    dump('Xs%d' % c, Xs, ['Xs'])
                    dump('Us%d' % c, Us, ['Us'])
                    dump('yb%d' % c, yb_, ['yb'])
                    dump('Hn%d' % c, Hn[0], [Hnt])
                    dump('Hm%d' % c, Hn[1], [Hnt])
                y3 = yb_.rearrange("p (h v) -> p h v", h=2)
                if os.environ.get('KDBG_RED'):
                    tr.op('dve', lambda e, o=st_[:, 0:2], i_=y3: e.tensor_reduce(o, i_, AX.X, ALU.add), ['yb'], ['st'])
                else:
                    for hd_ in range(2):
                        act(sq_[:, 64 * hd_:64 * hd_ + 64], yb_[:, 64 * hd_:64 * hd_ + 64], AF.Copy, r=['yb'],
                            w=['sq', 'st'], accum_out=st_[:, hd_:hd_ + 1])
                ts('dve', st_[:, 0:2], st_[:, 0:2], -1.0 / 64, None, ALU.mult, None, r=['st'], w=['st'])
                tt('dve', yc_.rearrange("p (h v) -> p h v", h=2), y3, st_[:, 0:2].unsqueeze(2).to_broadcast([128, 2, 64]),
                   ALU.add, r=['yb', 'st'], w=['yc'])
                if os.environ.get('KDBG_RED'):
                    act(sq_, yc_, AF.Square, r=['yc'], w=['sq'])
                else:
                    for hd_ in range(2):
                        act(sq_[:, 64 * hd_:64 * hd_ + 64], yc_[:, 64 * hd_:64 * hd_ + 64], AF.Square, r=['yc'],
                            w=['sq', 'st'], accum_out=st_[:, 2 + hd_:3 + hd_])
                if os.environ.get('KDBG_RED'):
                    tr.op('dve', lambda e, o=st_[:, 2:4], i_=sq_.rearrange("p (h v) -> p h v", h=2): e.tensor_reduce(
                        o, i_, AX.X, ALU.add), ['sq'], ['st'])
                act(st_[:, 2:4], st_[:, 2:4], AF.Sqrt, r=['st', 'epsc'], w=['st'], bias=epsc[:, 1:2], scale=1.0 / 64)
                tr.op('dve', lambda e, a_=st_[:, 2:4]: e.reciprocal(a_, a_), ['st'], ['st'])
                tt('dve', yc_.rearrange("p (h v) -> p h v", h=2), yc_.rearrange("p (h v) -> p h v", h=2),
                   st_[:, 2:4].unsqueeze(2).to_broadcast([128, 2, 64]), ALU.mult, r=['yc', 'st'], w=['yc'])
                if j == 0 and tb == 0 and c == 0:
                    stage_end('rwB3')
                SK = os.environ.get('KDBG_SKIP', '')
                if '1' not in SK:
                    tt('pool', yc_, yc_, lng[:, j * 128:(j + 1) * 128], ALU.mult, r=['yc', 'rowv'], w=['yc'])
                if '2' not in SK:
                    tt('pool', yc_, yc_, lnb[:, j * 128:(j + 1) * 128], ALU.add, r=['yc', 'rowv'], w=['yc'])
                if '3' not in SK:
                    tt('dve', sq_.rearrange("p (h v) -> p h v", h=2), vtok.rearrange("p (h v) -> p h v", h=2),
                       s_sb[:, 2 * c:2 * c + 2].unsqueeze(2).to_broadcast([128, 2, 64]), ALU.mult, r=['trn', 's_sb'],
                       w=['sq'])
                if '4' not in SK:
                    tt('dve', yc_, yc_, sq_, ALU.add, r=['yc', 'sq'], w=['yc'])
                if j == 0 and tb == 0 and c < 2:
                    dump('ypre%d' % c, yc_, ['yc'])
                if j == 0 and tb == 0 and c == 0:
                    stage_end('rwC')
                tt('dve', yfb, yc_, gtok[:, cs], ALU.mult, r=['yc', 'gtok'], w=['yfb'])
                pT, pTt = pq(7, 1, 1)
                pTb = pT.bitcast(BF16)[:, 0:128]
                tp(pTb, yfb, identb2, r=['yfb', 'identb2'], w=pTt)
                cp('act', ybT[:, cs], pTb, r=pTt, w=['ybT'])
            ydma.append(dma(ysrc_l[tb][512 + j * 128: 512 + (j + 1) * 128, :], ybT, r=['ybT'], w=[('ysrc', 4 + j, tb)]))
            if 'ybT%d' % tb in dbg_d and j == 0:
                tmpd2 = A['sg']
                cp('dve', tmpd2, ybT, r=['ybT'], w=[tk('sg')])
                dump('ybT%d' % tb, tmpd2, [tk('sg')])
            if j == 0 and tb == 1:
                stage_end('rwkv1')
    tr.fence()
    ar.release(mrw)

    mlr = ar.mark()
    lw = ar.f32(1024)
    lwv = lw.rearrange("p (g q n) -> p g q n", g=2, q=2)
    xp = [ar.f32(516), ar.f32(516)]
    uu = [ar.f32(512), ar.f32(512)]
    hh_ = [ar.f32(512), ar.f32(512)]
    hprev = ar.f32(2)
    R_ = ar.f32(512)
    I_ = ar.f32(512)
    A_ = ar.f32(512)
    M_ = ar.f32(512)
    G1 = ar.f32(512)
    G2 = ar.f32(512)
    yaT = ar.bf16(512)
    for h2 in range(2):
        bv, tok = load_w(288 + 1536 + 512 * h2, 512)
        for g in range(2):
            dma(lwv[:, g, :, :], lruw_d[g, h2, :, :].rearrange("(q p) n -> p q n", p=128), w=['lw'])
        for tb in range(4):
            tsl = slice(tb * 512, (tb + 1) * 512)
            for q in range(2):
                jt = 2 * h2 + q
                pb_ = 40 + 8 * jt
                p, ptk = proj(bv, tok, 128 * q, 128, tb)
                cp('act', xp[q][:, 3:515], p, r=[ptk], w=[('xp', q)])
                if tb > 0:
                    pp, pptk = proj_prev(bv, tok, 128 * q, 128, tb * 512 - 3, 3, 7, 0)
                    cp('act', xp[q][:, 0:3], pp, r=[pptk], w=[('xp', q)])
                else:
                    memset('pool', xp[q][:, 0:3], 0.0, w=[('xp', q)])
                ts('dve', uu[q], xp[q][:, 3:515], pc(pb_ + 3), pc(pb_ + 4), ALU.mult, ALU.add, r=[('xp', q), 'pcol'],
                   w=[('uu', q)])
                for kx in range(3):
                    stt(uu[q], xp[q][:, kx:kx + 512], pc(pb_ + kx), uu[q], ALU.mult, ALU.add,
                        r=[('xp', q), 'pcol', ('uu', q)], w=[('uu', q)])
            for qo in range(2):
                jt = 2 * h2 + qo
                pb_ = 40 + 8 * jt
                pr, prt = pq(4, 0, 4)
                pi, pit = pq(5, 0, 4)
                for q in range(2):
                    mm(pr, lwv[:, 0, q, qo * 128:(qo + 1) * 128], uu[q], q == 0, q == 1, r=['lw', ('uu', q)], w=prt)
                for q in range(2):
                    mm(pi, lwv[:, 1, q, qo * 128:(qo + 1) * 128], uu[q], q == 0, q == 1, r=['lw', ('uu', q)], w=pit)
                act(R_, pr, AF.Sigmoid, r=prt + ['pcol'], w=['R_'], bias=pc(pb_ + 5))
                act(I_, pi, AF.Sigmoid, r=pit + ['pcol'], w=['I_'], bias=pc(pb_ + 6))
                act(A_, R_, AF.Exp, r=['R_', 'dcol'], w=['A_'], scale=dcol[:, 24 + jt:25 + jt])
                act(M_, R_, AF.Exp, r=['R_', 'dcol'], w=['M_'], scale=dcol[:, 28 + jt:29 + jt])
                ts('dve', M_, M_, -1.0, 1.0, ALU.mult, ALU.add, r=['M_'], w=['M_'])
                ts('dve', M_, M_, 0.0, None, ALU.max, None, r=['M_'], w=['M_'])
                act(M_, M_, AF.Sqrt, r=['M_'], w=['M_'])
                if tb == 0:
                    memset('pool', M_[:, 0:1], 1.0, w=['M_'])
                tt('dve', I_, I_, M_, ALU.mult, r=['I_', 'M_'], w=['I_'])
                tt('pool', I_, I_, uu[qo], ALU.mult, r=['I_', ('uu', qo)], w=['I_'])
                if tb == 0:
                    init = 0.0
                    rr = ['A_', 'I_']
                else:
                    cp('pool', hprev[:, qo:qo + 1], hh_[qo][:, 511:512], r=[('hh', qo)], w=[('hprev', qo)])
                    init = hprev[:, qo:qo + 1]
                    rr = ['A_', 'I_', ('hprev', qo)]
                tr.op('dve', lambda e, o=hh_[qo], i0=init: e.tensor_tensor_scan(o, A_, I_, i0, ALU.mult, ALU.add),
                      rr, [('hh', qo)])
                p, ptk = proj(bv, tok, 256 + 128 * qo, 128, tb)
                cp('act', G1, p, r=[ptk], w=['G1'])
                act(G2, p, AF.Square, r=[ptk], w=['G2'])
                ts('dve', G2, G2, 0.044715, 1.0, ALU.mult, ALU.add, r=['G2'], w=['G2'])
                tt('pool', G2, G2, G1, ALU.mult, r=['G2', 'G1'], w=['G2'])
                act(G2, G2, AF.Sigmoid, r=['G2'], w=['G2'], scale=1.5957691216057308)
                tt('dve', G1, G1, G2, ALU.mult, r=['G1', 'G2'], w=['G1'])
                tt('dve', yaT, hh_[qo], G1, ALU.mult, r=[('hh', qo), 'G1'], w=['yaT'])
                ydma.append(dma(ysrc_l[tb][jt * 128:(jt + 1) * 128, :], yaT, r=['yaT'], w=[('ysrc', jt, tb)]))
                if ('yaT%d' % tb) in dbg_d and h2 == 0 and qo == 1:
                    cp('dve', G2, yaT, r=['yaT'], w=['G2'])
                    dump('yaT%d' % tb, G2, ['G2'])
    stage_end('mix')
    tr.fence()
    ar.release(mlr)
    ar.release(m_phase)

    import os
    for tb in range(4):
        toks_ = [('ysrc', a, tb) for a in range(8)]
        if os.environ.get('KDBG_NOCC'):
            dma(ydst_l[tb][0:1024, :], ysrc_l[tb], r=toks_, w=[('ydst', tb)], q='sp')
            dma(ydst_l[tb][1024:2048, :], ysrc_l[tb], r=toks_, w=[('ydst', tb)], q='sp')
        else:
            tr.op('pool', lambda e, a_=ysrc_l[tb], b_=ydst_l[tb]: e.collective_compute(
                "AllGather", ALU.bypass, replica_groups=[[0, 1], [2, 3], [4, 5], [6, 7]], ins=[a_], outs=[b_]),
                toks_, [('ydst', tb)], cc=True)
    x1 = ar.f32(8 * 2048)
    x1v = x1.rearrange("p (a n) -> p a n", a=8)
    vecA = ar.f32(2048)
    vecB = ar.f32(2048)
    vecC = ar.f32(2048)
    vecD = vecA
    tmpf = ar.f32(512)
    mtr = ar.mark()
    alloc_mb()
    mod_vec(2, vecA)
    mod_vec(4, vecB, gidx=1)
    mod_vec(3, vecC)
    tr.fence()
    ar.release(mtr)
    for a in range(8):
        dma(x1v[:, a, :], xown_d[a * 128:(a + 1) * 128, :], w=[('x1', a)])
    mys = ar.mark()
    ysel = ar.bf16(16 * 1024)
    yselv = ysel.rearrange("p (k t) -> p k t", k=16)
    tA = [ar.bf16(1024), ar.bf16(1024)]
    tB = [ar.bf16(1024), ar.bf16(1024)]
    for k in range(16):
        a_, b_ = tA[k % 2], tB[k % 2]
        for h_ in range(2):
            dma(a_[:, h_ * 512:(h_ + 1) * 512], ydst_l[h_][k * 128:(k + 1) * 128, :], r=[('ydst', h_)], w=[('tA', k % 2)])
            dma(b_[:, h_ * 512:(h_ + 1) * 512], ydst_l[2 + h_][k * 128:(k + 1) * 128, :], r=[('ydst', 2 + h_)],
                w=[('tB', k % 2)])
        ts('pool', a_, a_, hsel[:, 0:1], None, ALU.mult, None, r=[('tA', k % 2), 'hsel'], w=[('tA', k % 2)])
        stt(yselv[:, k, :], b_, hsel[:, 1:2], a_, ALU.mult, ALU.add, r=[('tA', k % 2), ('tB', k % 2), 'hsel'],
            w=[('ysel', k)])
    ysel_all = [('ysel', k) for k in range(16)]
    if 'ysel' in dbg_d:
        tmpq = ar.f32(1024)
        cp('dve', tmpq, yselv[:, 5, :], r=ysel_all, w=['tmpq'])
        dump('ysel', tmpq, ['tmpq'])
        cp('dve', tmpq, yselv[:, 13, :], r=ysel_all + [('dbgout', 'ysel')], w=['tmpq'])
        dump('ysel2', tmpq, ['tmpq'])
    stage_end('xchg')
    wb2 = [ar.bf16(16 * 512), ar.bf16(16 * 512)]
    w2c = [0]

    def load_blk(src, off, nk, ncols, buf=None):
        if buf is None:
            buf = wb2[w2c[0] % 2]
            w2c[0] += 1
        tok_ = ('wb2', id(buf))
        load_flat(buf, src, off, nk * ncols, tok_)
        return buf[:, 0:nk * ncols].rearrange("p (k n) -> p k n", k=nk), tok_

    for cg in range(4):
        wv, wtok = load_blk(wout_d, cg * 8192, 16, 512)
        for a in range(8):
            bi = (cg * 8 + a) % 4
            for k in range(16):
                mm(ps[bi][:, :], yselv[:, k, a * 128:(a + 1) * 128], wv[:, k, :], k == 0, k == 15,
                   r=ysel_all + [wtok], w=[('ps', bi)])
            csl = slice(cg * 512, (cg + 1) * 512)
            tt('dve', tmpf, ps[bi][:, :], vecA[:, csl], ALU.mult, r=[('ps', bi), ('vec', id(vecA))], w=['tmpf'])
            tt('pool', x1v[:, a, csl], x1v[:, a, csl], tmpf, ALU.add, r=['tmpf', ('x1', a)], w=[('x1', a)])
    dump('x1a', x1v[:, 0, :], [('x1', 0)])
    dump('x1b', x1v[:, 7, :], [('x1', 7)])
    stage_end('wout')
    tr.fence()
    ar.release(mys)
    alloc_mb()
    mod_vec(5, vecD)
    tr.fence()
    ar.release(mys)

    h2T = ar.bf16(16 * 1024)
    h2Tv = h2T.rearrange("p (k t) -> p k t", k=16)
    actT = ar.bf16(4 * 1024)
    actv = actT.rearrange("p (f t) -> p f t", f=4)
    wg = [ar.bf16(16 * 256), ar.bf16(16 * 256)]
    wd = [ar.bf16(4 * 2048)] * 2
    hf2 = ar.f32(2048)
    hb2 = ar.bf16(2048)
    junk2 = hb2
    st2 = ar.f32(32)

    def rms_rstd(a, col):
        act(junk2, x1v[:, a, :], AF.Square, r=[('x1', a)], w=['hb2', ('st2', col)], accum_out=st2[:, col:col + 1])
        act(st2[:, col:col + 1], st2[:, col:col + 1], AF.Sqrt, r=[('st2', col), 'epsc'], w=[('st2', col)],
            bias=epsc[:, 0:1], scale=1.0 / D)
        tr.op('dve', lambda e, a_=st2[:, col:col + 1]: e.reciprocal(a_, a_), [('st2', col)], [('st2', col)])

    for a in range(8):
        rms_rstd(a, a)
        stt(hf2, x1v[:, a, :], st2[:, a:a + 1], vecB, ALU.mult, ALU.mult, r=[('x1', a), ('st2', a), ('vec', id(vecB))],
            w=['hf2'])
        tt('pool', hb2, hf2, vecC, ALU.add, r=['hf2', ('vec', id(vecC))], w=['hb2'])
        for half in range(2):
            bi = 4 + (a * 2 + half) % 4
            pbv = ps[bi][:, :].bitcast(BF16)
            for q in range(8):
                dc = half * 8 + q
                tp(pbv[:, q * 128:(q + 1) * 128], hb2[:, dc * 128:(dc + 1) * 128], identb, r=['hb2', 'identb'],
                   w=[('ps', bi)])
            cp('act' if half == 0 else 'dve', h2Tv[:, half * 8:(half + 1) * 8, a * 128:(a + 1) * 128],
               pbv.rearrange("p (q t) -> p q t", q=8), r=[('ps', bi)], w=[('h2T', a)])
    h2T_all = [('h2T', a) for a in range(8)]

    tmpg = ar.f32(512)
    for fg in range(11):
        dv, dtok = load_blk(wdown_d, fg * 8192, 4, 2048, buf=wd[fg % 2])
        for ft in range(4):
            f = fg * 4 + ft
            gb = wg[f % 2]
            gv_, gtok_ = load_blk(wgu_d, f * 4096, 16, 256, buf=gb)
            for tg in range(2):
                bg, bu = (tg * 2) % 4, (tg * 2 + 1) % 4
                for k in range(16):
                    mm(ps[bg][:, :], gv_[:, k, 0:128], h2Tv[:, k, tg * 512:(tg + 1) * 512], k == 0, k == 15,
                       r=[gtok_] + h2T_all, w=[('ps', bg)])
                for k in range(16):
                    mm(ps[bu][:, :], gv_[:, k, 128:256], h2Tv[:, k, tg * 512:(tg + 1) * 512], k == 0, k == 15,
                       r=[gtok_] + h2T_all, w=[('ps', bu)])
                act(tmpg, ps[bg][:, :], AF.Silu, r=[('ps', bg)], w=['tmpg'])
                tt('dve', actv[:, ft, tg * 512:(tg + 1) * 512], tmpg, ps[bu][:, :], ALU.mult, r=['tmpg', ('ps', bu)],
                   w=[('actT', ft)])
        for a in range(8):
            for cg in range(4):
                bi = 4 + (a * 4 + cg) % 4
                for ft in range(4):
                    mm(ps[bi][:, :], actv[:, ft, a * 128:(a + 1) * 128], dv[:, ft, cg * 512:(cg + 1) * 512], ft == 0,
                       ft == 3, r=[('actT', ft), dtok], w=[('ps', bi)])
                csl = slice(cg * 512, (cg + 1) * 512)
                tt('dve', tmpf, ps[bi][:, :], vecD[:, csl], ALU.mult, r=[('ps', bi), ('vec', id(vecD))], w=['tmpf'])
                tt('pool', x1v[:, a, csl], x1v[:, a, csl], tmpf, ALU.add, r=['tmpf', ('x1', a)], w=[('x1', a)])
    tr.fence()
    dma(vecB, gvec_d[2, :].partition_broadcast(128), w=[('vec', id(vecB))])
    outs = []
    for a in range(8):
        rms_rstd(a, 8 + a)
        stt(hf2, x1v[:, a, :], st2[:, 8 + a:9 + a], vecB, ALU.mult, ALU.mult,
            r=[('x1', a), ('st2', 8 + a), ('vec', id(vecB))], w=['hf2'])
        outs.append(dma(out_d[a * 128:(a + 1) * 128, :], hf2, r=['hf2'], w=[('out', a)], q='sp'))
    fin_deps = list(outs)
    for nm in dbg_d:
        pass
    blk = es.enter_context(nc.Block())
    tr.emit(nc, es, blk, fin_deps + tr.dbg_ops)
    es.close()
    return nc


def _perm_cols(hh):
    cols = []
    rw = 2048
    wl0, al0, gl0 = rw + 3072, rw + 3072 + 64, rw + 3072 + 128
    cols += list(range(wl0, wl0 + 64)) + list(range(al0, al0 + 64)) + list(range(gl0, gl0 + 160))
    for j in range(4):
        c = hh * 512 + j * 128
        for q in range(3):
            cols += list(range(rw + q * 1024 + c, rw + q * 1024 + c + 128))
    for h2 in range(2):
        c = hh * 512 + h2 * 256
        cols += list(range(c, c + 256)) + list(range(1024 + c, 1024 + c + 256))
    return np.array(cols)


_CACHE = {}


def kernel(x, c, w_ada, b_ada, norm_mix_g, w_in, conv_w, conv_b, lru_wa, lru_ba, lru_wx, lru_bx, lru_lambda,
           rwkv_mu, rwkv_w0, rwkv_w2, rwkv_a0, rwkv_a2, rwkv_g2, rwkv_k_k, rwkv_k_a, rwkv_r_k, rwkv_ln_g,
           rwkv_ln_b, w_out, norm_ffn_g, w_gu, w_down, final_norm_g, _dbg=None, _stage=None, _ncores=8):
    f = lambda a: np.ascontiguousarray(np.asarray(a, dtype=np.float32))
    x, c = f(x), f(c)
    if 'nc' not in _CACHE or _dbg is not None:
        _CACHE['nc'] = build(_dbg, _stage)
    nc = _CACHE['nc']
    ident = np.eye(128, dtype=np.float32)
    masks = np.zeros((128, 4, 128), np.float32)
    masks[:, 0] = np.triu(np.ones((128, 128)), 1)
    masks[:, 1] = np.triu(np.ones((128, 128)), 0)
    masks[:, 2] = np.tril(np.ones((128, 128)), -1)
    masks[0:64, 3, 0:64] = 1
    masks[64:128, 3, 64:128] = 1
    mu = f(rwkv_mu)[0]
    wout_perm = np.concatenate([np.arange(0, 512), np.arange(1024, 1536), np.arange(512, 1024), np.arange(1536, 2048)])
    wout_p = f(w_out)[0][wout_perm]
    wgu_ = f(w_gu)[0]
    g_ = wgu_[:, 0:DFF].reshape(16, 128, 44, 128)
    u_ = wgu_[:, DFF:2 * DFF].reshape(16, 128, 44, 128)
    wgu = np.ascontiguousarray(np.concatenate([g_, u_], axis=3).transpose(1, 2, 0, 3).reshape(128, 44 * 4096))
    wdown = np.ascontiguousarray(f(w_down)[0].reshape(11, 4, 128, 2048).transpose(2, 0, 1, 3).reshape(128, 11 * 8192))
    wout_p = np.ascontiguousarray(wout_p.reshape(16, 128, 4, 512).transpose(1, 2, 0, 3).reshape(128, 4 * 8192))
    wada = f(w_ada)[0]
    bada = f(b_ada)[0]
    gvec = np.stack([f(norm_mix_g)[0], f(norm_ffn_g)[0], f(final_norm_g)])
    in_maps = []
    for r in range(8):
        b, hh = r // 2, r % 2
        cols = _perm_cols(hh)
        winc = f(w_in)[0][:, cols]
        blocks = []
        for (c0_, n_) in [(0, 288)] + [(288 + 384 * j_, 384) for j_ in range(4)] + [(1824 + 512 * h_, 512) for h_ in range(2)]:
            blocks.append(winc[:, c0_:c0_ + n_].reshape(16, 128, n_).transpose(1, 0, 2).reshape(128, 16 * n_))
        win = np.ascontiguousarray(np.concatenate(blocks, axis=1))
        pcol = np.zeros((128, 80), np.float32)
        rkm = np.zeros((128, 8), np.float32)
        for j in range(4):
            ch = hh * 512 + j * 128 + np.arange(128)
            pcol[:, 8 * j + 0] = mu[ch]
            pcol[:, 8 * j + 1] = mu[1024 + ch]
            pcol[:, 8 * j + 2] = mu[2048 + ch]
            pcol[:, 8 * j + 3] = f(rwkv_k_k)[0][ch]
            pcol[:, 8 * j + 4] = f(rwkv_k_a)[0][ch]
            pcol[:, 8 * j + 5] = f(rwkv_w0)[0][ch]
            pcol[:, 8 * j + 6] = f(rwkv_a0)[0][ch]
            rkf = f(rwkv_r_k)[0].reshape(-1)[ch]
            rkm[0:64, 2 * j] = rkf[0:64]
            rkm[64:128, 2 * j + 1] = rkf[64:128]
            pcol[:, 40 + 8 * j + 0:40 + 8 * j + 4] = f(conv_w)[0][:, ch].T
            pcol[:, 40 + 8 * j + 4] = f(conv_b)[0][ch]
            pcol[:, 40 + 8 * j + 5] = f(lru_ba)[0][ch]
            pcol[:, 40 + 8 * j + 6] = f(lru_bx)[0][ch]
            pcol[:, 40 + 8 * j + 7] = f(lru_lambda)[0][ch]
        pcol[:, 32] = mu[3072:3200]
        pcol[:, 33] = mu[3200:3328]
        pcol[0:32, 34] = mu[3328:3360]
        chs = hh * 512 + np.arange(512)
        rowv = np.stack([f(rwkv_ln_g)[0][chs], f(rwkv_ln_b)[0][chs]])
        wa2 = np.concatenate([f(rwkv_w2)[0][:, chs], f(rwkv_a2)[0][:, chs]], axis=0)
        g2 = np.ascontiguousarray(f(rwkv_g2)[0][:, chs])
        lruw = np.stack([f(lru_wa)[0][2 * hh:2 * hh + 2], f(lru_wx)[0][2 * hh:2 * hh + 2]])
        hsel = np.zeros((128, 2), np.float32)
        hsel[:, hh] = 1.0
        m = dict(x=x[b], xown=np.ascontiguousarray(x[b, hh * 1024:(hh + 1) * 1024]),
                 cb=np.ascontiguousarray(c[b].reshape(128, 16)), wada=wada, bada=bada, gvec=gvec, win=win, pcol=pcol,
                 rowv=np.ascontiguousarray(rowv), rkm=rkm, wa2=np.ascontiguousarray(wa2), g2=g2,
                 lruw=np.ascontiguousarray(lruw), wout=wout_p, wgu=wgu, wdown=wdown, ident=ident, masks=masks, hsel=hsel)
        in_maps.append(m)
    res = run_bass_kernel_spmd(nc, in_maps[:_ncores], core_ids=list(range(_ncores)))
    if _stage is not None:
        return None, res
    out = np.zeros((4, 2048, 2048), np.float32)
    for r in range(8):
        b, hh = r // 2, r % 2
        out[b, hh * 1024:(hh + 1) * 1024] = res.results[r]["out"]
    if _dbg is not None:
        return out, res
    return out
```

```python
import os
import numpy as np
from contextlib import ExitStack
import concourse.bass as bass
import concourse.mybir as mybir
from concourse.bass_utils import run_bass_kernel_spmd

F32 = mybir.dt.float32
BF16 = mybir.dt.bfloat16
AF = mybir.ActivationFunctionType
ALU = mybir.AluOpType
AX = mybir.AxisListType

D = 2048
T = 2048
TO = 1024
DFF = 5632
NWIN = 2848
C0 = float(np.exp(-0.5))
RMS_EPS = 1e-6
GN_EPS = 64e-5
L2_EPS = 1e-12
NDMASEM = 6


class Tr:
    def __init__(self):
        self.ops = []
        self.lw = {}
        self.rd = {}
        self.fenced = 0
        self.dbg_ops = []

    def op(self, eng, fn, r=(), w=(), dma=False, cc=False):
        i = len(self.ops)
        deps = set()
        psr = [t for t in r if isinstance(t, tuple) and t and t[0] == 'ps']
        if psr:
            r = [t for t in r if t not in psr]
            w = list(w) + psr
        for t in r:
            x = self.lw.get(t)
            if x is not None:
                deps.add(x)
        for t in w:
            x = self.lw.get(t)
            if x is not None:
                deps.add(x)
            for y in self.rd.get(t, ()):
                deps.add(y)
        deps.discard(i)
        for t in r:
            self.rd.setdefault(t, []).append(i)
        for t in w:
            self.lw[t] = i
            self.rd[t] = []
        self.ops.append(dict(eng=eng, fn=fn, deps=deps, dma=dma, cc=cc))
        return i

    def fence(self):
        last = {}
        dmas = []
        for i, o in enumerate(self.ops):
            if o['dma'] or o['cc']:
                dmas.append(i)
            elif o['fn'] is not None:
                last[o['eng']] = i
        deps = set(last.values()) | set(dmas[self.fenced:])
        self.fenced = len(dmas)
        for e in ['pe', 'act', 'dve', 'pool', 'sp']:
            i = self.op(e, None, (), ())
            self.ops[i]['deps'] = set(deps)

    def emit(self, nc, es, block, final_deps):
        ops = self.ops
        engs = ['pe', 'act', 'dve', 'pool', 'sp']
        csem = {e: es.enter_context(nc.semaphore("c_" + e)) for e in engs}
        dsem = {e: [es.enter_context(nc.semaphore("d_%s%d" % (e, k))) for k in range(NDMASEM)]
                for e in ['act', 'pool', 'sp']}
        ccsem = es.enter_context(nc.semaphore("ccs"))
        fin = self.op('sp', None, (), ())
        ops[fin]['deps'] = set(final_deps)
        used = set()
        for o in ops:
            for d in o['deps']:
                used.add(d)
        ccnt = {e: 0 for e in engs}
        dcnt = {e: 0 for e in dsem}
        dval = {}
        cccount = 0
        for i, o in enumerate(ops):
            e = o['eng']
            if o['cc']:
                cccount += 1
                o['sem'] = ccsem
                o['val'] = cccount
            elif o['dma']:
                k = dcnt[e] % NDMASEM
                dcnt[e] += 1
                s = dsem[e][k]
                prev = dval.get((e, k), 0)
                o['sem'] = s
                o['prev'] = prev
                o['val'] = prev + 16
                dval[(e, k)] = prev + 16
            elif o['fn'] is not None and i in used:
                ccnt[e] += 1
                o['sem'] = csem[e]
                o['val'] = ccnt[e]
            else:
                o['sem'] = None
        per = {e: [] for e in engs}
        for i, o in enumerate(ops):
            per[o['eng']].append(i)

        def run(e, eng):
            waited = {}

            def wait(sem, val):
                key = sem.num if hasattr(sem, 'num') else id(sem)
                if waited.get(key, 0) >= val:
                    return
                eng.wait_ge(sem, val)
                waited[key] = val

            for i in per[e]:
                o = ops[i]
                for d in sorted(o['deps']):
                    p = ops[d]
                    if p['sem'] is None:
                        continue
                    if p['eng'] == e and e == 'pe' and not p['dma'] and not p['cc']:
                        continue
                    wait(p['sem'], p['val'])
                if o['dma'] and not o['cc'] and o['prev'] > 0:
                    wait(o['sem'], o['prev'])
                if o['fn'] is None:
                    continue
                ins = o['fn'](eng)
                if o['sem'] is not None:
                    if o['cc']:
                        ins.then_inc(o['sem'], 1)
                    elif o['dma']:
                        ins.then_inc(o['sem'], 16)
                    else:
                        ins.then_inc(o['sem'], 1)

        @block.tensor
        def _(eng):
            run('pe', eng)

        @block.scalar
        def _(eng):
            run('act', eng)

        @block.vector
        def _(eng):
            run('dve', eng)

        @block.gpsimd
        def _(eng):
            run('pool', eng)

        @block.sync
        def _(eng):
            run('sp', eng)


class Arena:
    def __init__(self, ap_f32, nwords):
        self.ap = ap_f32
        self.n = nwords
        self.top = 0
        self.hi = 0

    def mark(self):
        return self.top

    def release(self, m):
        self.top = m

    def f32(self, cols, parts=128):
        a = self.top
        self.top += cols
        self.hi = max(self.hi, self.top)
        assert self.top <= self.n, ("SBUF arena overflow", self.top, self.n)
        return self.ap[0:parts, a:a + cols]

    def bf16(self, cols, parts=128):
        w = (cols + 1) // 2
        a = self.top
        self.top += w
        self.hi = max(self.hi, self.top)
        assert self.top <= self.n, ("SBUF arena overflow", self.top, self.n)
        return self.ap[0:parts, a:a + w].bitcast(BF16)[:, 0:cols]


class _Done(Exception):
    def __init__(self, nc):
        self.nc = nc


def build(dbg=None, stage=None):
    try:
        return build_nc(dbg, stage)
    except _Done as d:
        return d.nc


def build_nc(dbg=None, stage=None):
    nc = bass.Bass("TRN2", target_bir_lowering=False)
    es = ExitStack()
    tr = Tr()

    def stage_end(name):
        if stage == name:
            blk_ = es.enter_context(nc.Block())
            tr.emit(nc, es, blk_, list(tr.dbg_ops))
            es.close()
            raise _Done(nc)

    def din(name, shape, dt=F32):
        return nc.dram_tensor(name, list(shape), dt, kind="ExternalInput").ap()

    x_d = din("x", [T, D])
    xown_d = din("xown", [TO, D])
    cb_d = din("cb", [128, 16])
    wada_d = din("wada", [D, 6 * D])
    bada_d = din("bada", [6 * D])
    gvec_d = din("gvec", [3, D])
    win_d = din("win", [128, 16 * NWIN])
    pcol_d = din("pcol", [128, 80])
    rowv_d = din("rowv", [2, 512])
    rk_d = din("rkm", [128, 8])
    wa2_d = din("wa2", [128, 512])
    g2_d = din("g2", [160, 512])
    lruw_d = din("lruw", [2, 2, 256, 256])
    wout_d = din("wout", [128, 4 * 8192])
    wgu_d = din("wgu", [128, 44 * 4096])
    wdown_d = din("wdown", [128, 11 * 8192])
    ident_d = din("ident", [128, 128])
    masks_d = din("masks", [128, 4, 128])
    hsel_d = din("hsel", [128, 2])
    out_d = nc.dram_tensor("out", [TO, D], F32, kind="ExternalOutput").ap()
    ysrc_l = [nc.dram_tensor("ysrc%d" % i, [1024, 512], BF16, kind="Internal").ap() for i in range(4)]
    ydst_l = [nc.dram_tensor("ydst%d" % i, [2048, 512], BF16, kind="Internal").ap() for i in range(4)]
    dbg_d = {}
    if dbg:
        for nm, shp in dbg.items():
            dbg_d[nm] = nc.dram_tensor("dbg_" + nm, list(shp), F32, kind="ExternalOutput").ap()

    NW = 53000
    arena_t = es.enter_context(nc.sbuf_tensor("arena", [128, NW], F32))
    ar = Arena(arena_t, NW)
    ps = [es.enter_context(nc.psum_tensor("ps%d" % i, [128, 512], F32)) for i in range(8)]

    dmaq = ['sp', 'act']
    dctr = [0]

    def dma(out, in_, r=(), w=(), q=None, **kw):
        if q is None:
            q = dmaq[dctr[0] % len(dmaq)]
            dctr[0] += 1
        return tr.op(q, lambda e: e.dma_start(out=out, in_=in_, **kw), r, w, dma=True)

    def mm(out, lhsT, rhs, start, stop, r=(), w=()):
        return tr.op('pe', lambda e: e.matmul(out, lhsT, rhs, start=start, stop=stop), r, w)

    def tp(out, in_, idn, r=(), w=()):
        return tr.op('pe', lambda e: e.transpose(out, in_, idn), r, w)

    def act(out, in_, func, r=(), w=(), bias=None, scale=None, accum_out=None):
        kw = {}
        if bias is not None:
            kw['bias'] = bias
        if scale is not None:
            kw['scale'] = scale
        if accum_out is not None:
            kw['accum_out'] = accum_out
        return tr.op('act', lambda e: e.activation(out, in_, func, **kw), r, w)

    def ts(eng, out, in0, s1, s2, op0, op1, r=(), w=()):
        if op1 is None:
            return tr.op(eng, lambda e: e.tensor_scalar(out, in0, s1, None, op0), r, w)
        return tr.op(eng, lambda e: e.tensor_scalar(out, in0, s1, s2, op0, op1), r, w)

    def tt(eng, out, in0, in1, op, r=(), w=()):
        return tr.op(eng, lambda e: e.tensor_tensor(out, in0, in1, op), r, w)

    def stt(out, in0, sc, in1, op0, op1, r=(), w=()):
        return tr.op('dve', lambda e: e.scalar_tensor_tensor(out, in0, sc, in1, op0, op1), r, w)

    def cp(eng, out, in_, r=(), w=()):
        if eng == 'act':
            return tr.op('act', lambda e: e.activation(out, in_, AF.Copy), r, w)
        return tr.op(eng, lambda e: e.tensor_copy(out, in_), r, w)

    def memset(eng, ap, val, w=()):
        return tr.op(eng, lambda e: e.memset(ap, val), (), w)

    def dump(name, ap, r):
        if name in dbg_d:
            i_ = dma(dbg_d[name], ap, r=r, w=[('dbgout', name)], q='sp')
            tr.dbg_ops.append(i_)
            return i_
        return None

    ident = ar.f32(128)
    masks = ar.f32(512)
    pcol = ar.f32(80)
    rkm = ar.f32(8)
    hsel = ar.f32(2)
    ones = ar.f32(128)
    epsc = ar.f32(4)
    wa2 = ar.f32(512)
    g2a = ar.f32(512)
    g2b = ar.f32(512, 32)
    rowv = ar.f32(1024)
    cb = ar.f32(16)
    identb = ar.bf16(128)
    identb2 = identb
    dma(ident, ident_d, w=['ident'])
    dma(masks, masks_d.rearrange("p a b -> p (a b)"), w=['masks'])
    dma(pcol, pcol_d, w=['pcol'])
    dma(rkm, rk_d, w=['rkm'])
    dma(hsel, hsel_d, w=['hsel'])
    dma(wa2, wa2_d, w=['wa2'])
    dma(g2a, g2_d[0:128, :], w=['g2'])
    dma(g2b, g2_d[128:160, :], w=['g2'])
    dma(rowv, rowv_d.rearrange("a n -> (a n)").partition_broadcast(128), w=['rowv'])
    dma(cb, cb_d, w=['cb'])
    memset('pool', ones, 1.0, w=['ones'])
    memset('pool', epsc[:, 0:1], RMS_EPS, w=['epsc'])
    memset('pool', epsc[:, 1:2], GN_EPS, w=['epsc'])
    memset('pool', epsc[:, 2:3], 0.0, w=['epsc'])
    MUS = masks[:, 0:128]
    MUI = masks[:, 128:256]
    MLS = masks[:, 256:384]
    BLK = masks[:, 384:512]
    lng = rowv[:, 0:512]
    lnb = rowv[:, 512:1024]

    def pc(i):
        return pcol[:, i:i + 1]

    dcol = ar.f32(64)
    mu_idx = [8 * j + q for j in range(4) for q in range(3)] + [32, 33, 34]
    for n, mi in enumerate(mu_idx):
        ts('pool', dcol[:, n:n + 1], pc(mi), -1.0, 1.0, ALU.mult, ALU.add, r=['pcol'], w=['dcol'])

    def omm(mi):
        return dcol[:, mu_idx.index(mi):mu_idx.index(mi) + 1]

    for j in range(4):
        ts('pool', dcol[:, 16 + j:17 + j], pc(8 * j + 4), -1.0, 1.0, ALU.mult, ALU.add, r=['pcol'], w=['dcol'])
        act(dcol[:, 32 + j:33 + j], pc(40 + 8 * j + 7), AF.Exp, r=['pcol'], w=['dcol'], scale=-1.0)
        act(dcol[:, 36 + j:37 + j], dcol[:, 32 + j:33 + j], AF.Ln, r=['dcol'], w=['dcol'], bias=1.0)
        ts('pool', dcol[:, 24 + j:25 + j], dcol[:, 36 + j:37 + j], -8.0, None, ALU.mult, None, r=['dcol'], w=['dcol'])
        ts('pool', dcol[:, 28 + j:29 + j], dcol[:, 36 + j:37 + j], -16.0, None, ALU.mult, None, r=['dcol'], w=['dcol'])

    cact = ar.f32(16)
    act(cact, cb, AF.Silu, r=['cb'], w=['cact'])
    cbc = ar.f32(16 * 128)
    cp('dve', cbc.rearrange("p (k m) -> p k m", k=16), cact.unsqueeze(2).to_broadcast([128, 16, 128]),
       r=['cact'], w=['cbc'])
    m_phase = ar.mark()
    MB = {}

    def alloc_mb():
        MB['bad'] = ar.f32(2048)
        MB['gv'] = ar.f32(2048)
        MB['wst'] = [ar.f32(2 * 2048), ar.f32(2 * 2048)]
    wada_v = wada_d.rearrange("(p k) n -> p k n", k=16)

    def mod_chunk(jm, evac):
        nst = 0
        for kq in range(8):
            b = MB['wst'][nst % 2]
            nst += 1
            dma(b.rearrange("p (k n) -> p k n", k=2), wada_v[:, kq * 2:(kq + 1) * 2, jm * 2048:(jm + 1) * 2048],
                w=[('wst', id(b))])
            for kk in range(2):
                k = kq * 2 + kk
                for q in range(4):
                    mm(ps[q][:, :], cbc[:, k * 128:(k + 1) * 128], b[:, kk * 2048 + q * 512: kk * 2048 + (q + 1) * 512],
                       k == 0, k == 15, r=['cbc', ('wst', id(b))], w=[('ps', q)])
        for q in range(4):
            evac(q, ps[q][:, :])

    def mod_vec(jm, dst, gidx=None):
        bad = MB['bad']
        gv = MB['gv']
        dma(bad, bada_d[jm * 2048:(jm + 1) * 2048].partition_broadcast(128), w=['bad'])
        if gidx is not None:
            dma(gv, gvec_d[gidx, :].partition_broadcast(128), w=['gv'])

        def ev(q, p):
            sl = slice(q * 512, (q + 1) * 512)
            tt('dve', dst[:, sl], p, bad[:, sl], ALU.add, r=[('ps', q), 'bad'], w=[('vec', id(dst))])
            if gidx is not None:
                stt(dst[:, sl], dst[:, sl], 1.0, gv[:, sl], ALU.add, ALU.mult, r=[('vec', id(dst)), 'gv'],
                    w=[('vec', id(dst))])
        mod_chunk(jm, ev)

    hT = ar.f32(16 * 2048 // 2)
    hTb = hT.bitcast(BF16).rearrange("p (k t) -> p k t", k=16)
    p1 = ar.mark()
    alloc_mb()
    gm1 = ar.f32(2048)
    shm = ar.f32(2048)
    mod_vec(1, gm1, gidx=0)
    mod_vec(0, shm)
    xs = [ar.f32(2048), ar.f32(2048)]
    hf = ar.f32(2048)
    hb = ar.bf16(2048)
    junk = ar.bf16(2048)
    ssq = ar.f32(16)
    rst = ar.f32(16)
    cp('dve', identb, ident, r=['ident'], w=['identb', 'identb2'])
    for tt_i in range(16):
        xb = xs[tt_i % 2]
        xt = ('xs', tt_i % 2)
        dma(xb, x_d[tt_i * 128:(tt_i + 1) * 128, :], w=[xt], q='sp')
        act(junk, xb, AF.Square, r=[xt], w=['junk', ('ssq', tt_i)], accum_out=ssq[:, tt_i:tt_i + 1])
        act(rst[:, tt_i:tt_i + 1], ssq[:, tt_i:tt_i + 1], AF.Sqrt, r=[('ssq', tt_i), 'epsc'], w=[('rst', tt_i)],
            bias=epsc[:, 0:1], scale=1.0 / D)
        tr.op('dve', lambda e, a=rst[:, tt_i:tt_i + 1]: e.reciprocal(a, a), [('rst', tt_i)], [('rst', tt_i)])
        stt(hf, xb, rst[:, tt_i:tt_i + 1], gm1, ALU.mult, ALU.mult, r=[xt, ('rst', tt_i), ('vec', id(gm1))], w=['hf'])
        tt('pool', hb, hf, shm, ALU.add, r=['hf', ('vec', id(shm))], w=['hb'])
        for half in range(2):
            pb = ps[4 + (tt_i * 2 + half) % 4]
            pt = ('ps', 4 + (tt_i * 2 + half) % 4)
            pbv = pb[:, :].bitcast(BF16)
            for q in range(8):
                dc = half * 8 + q
                tp(pbv[:, q * 128:(q + 1) * 128], hb[:, dc * 128:(dc + 1) * 128], identb, r=['hb', 'identb'], w=[pt])
            cp('act' if half == 0 else 'dve', hTb[:, half * 8:(half + 1) * 8, tt_i * 128:(tt_i + 1) * 128],
               pbv.rearrange("p (q t) -> p q t", q=8), r=[pt], w=[('hT', tt_i)])
    hT_all = [('hT', i) for i in range(16)]
    if 'hT' in dbg_d:
        tmpd = ar.f32(2048)
        cp('dve', tmpd, hTb[:, 3, :], r=hT_all, w=['tmpd'])
        dump('hT', tmpd, ['tmpd'])
    dump('gm1', gm1, [('vec', id(gm1))])
    stage_end('p1')
    tr.fence()
    ar.release(p1)

    wbufs = [ar.bf16(16 * 512), ar.bf16(16 * 512)]
    wctr = [0]

    def load_flat(dst, src, off, n, tok, piece=2048):
        for a in range(0, n, piece):
            b_ = min(n, a + piece)
            dma(dst[:, a:b_], src[:, off + a: off + b_], w=[tok], q='pool', max_dma_last_dim=8192)

    def load_w(c0, ncols):
        b = wbufs[wctr[0] % 2]
        wctr[0] += 1
        bv = b[:, 0:16 * ncols].rearrange("p (k n) -> p k n", k=16)
        tok = ('wbuf', id(b))
        load_flat(b, win_d, 16 * c0, 16 * ncols, tok)
        return bv, tok

    psrr = [0]

    def proj(bv, tok, cofs, ncol, tg, tlen=512):
        bi = psrr[0] % 4
        psrr[0] += 1
        p = ps[bi]
        for k in range(16):
            mm(p[0:ncol, 0:tlen], bv[:, k, cofs:cofs + ncol], hTb[:, k, tg * 512: tg * 512 + tlen], k == 0, k == 15,
               r=[tok] + hT_all, w=[('ps', bi)])
        return p[0:ncol, 0:tlen], ('ps', bi)

    pmu = ar.f32(516)
    memset('pool', pmu[:, 0:1], 0.0, w=['pmu'])

    def lerp_evac(p, ptok, nrow, mui, dst, dtok, tg):
        if tg > 0:
            cp('pool', pmu[0:nrow, 0:1], pmu[0:nrow, 512:513], r=['pmu'], w=['pmu'])
        else:
            memset('pool', pmu[0:nrow, 0:1], 0.0, w=['pmu'])
        act(pmu[0:nrow, 1:513], p, AF.Copy, r=[ptok, 'pcol'], w=['pmu'], scale=pc(mui)[0:nrow, :])
        stt(dst[0:nrow, tg * 512:(tg + 1) * 512], p, omm(mui)[0:nrow, :], pmu[0:nrow, 0:512], ALU.mult, ALU.add,
            r=[ptok, 'pmu', 'dcol'], w=[dtok])

    lwin = ar.f32(2048)
    glB = ar.f32(2048)
    glC = ar.f32(2048, 32)
    bv, tok = load_w(0, 288)
    for tg in range(4):
        p, ptk = proj(bv, tok, 0, 128, tg)
        lerp_evac(p, ptk, 128, 32, lwin, ('lwin', tg), tg)
        act(lwin[0:64, tg * 512:(tg + 1) * 512], lwin[0:64, tg * 512:(tg + 1) * 512], AF.Tanh, r=[('lwin', tg)],
            w=[('lwin', tg)])
    for tg in range(4):
        p, ptk = proj(bv, tok, 128, 128, tg)
        lerp_evac(p, ptk, 128, 33, glB, ('glB', tg), tg)
        act(glB[:, tg * 512:(tg + 1) * 512], glB[:, tg * 512:(tg + 1) * 512], AF.Sigmoid, r=[('glB', tg)],
            w=[('glB', tg)])
    for tg in range(4):
        p, ptk = proj(bv, tok, 256, 32, tg)
        lerp_evac(p, ptk, 32, 34, glC, ('glC', tg), tg)
        act(glC[:, tg * 512:(tg + 1) * 512], glC[:, tg * 512:(tg + 1) * 512], AF.Sigmoid, r=[('glC', tg)],
            w=[('glC', tg)])
    dump('lwin', lwin, [('lwin', g) for g in range(4)])
    dump('glB', glB, [('glB', g) for g in range(4)])
    stage_end('lora')

    def pq(bank, q0, nq=1, rows=128):
        return ps[bank][0:rows, q0 * 128:(q0 + nq) * 128], [('ps', bank)]

    def proj_prev(bv, tok, cofs, ncol, tcol, width, bank, col):
        for k in range(16):
            mm(ps[bank][0:ncol, col:col + width], bv[:, k, cofs:cofs + ncol], hTb[:, k, tcol:tcol + width], k == 0,
               k == 15, r=[tok] + hT_all, w=[('ps', bank)])
        return ps[bank][0:ncol, col:col + width], ('ps', bank)

    mask4 = ar.f32(512)
    cp('pool', mask4[:, 0:128], MUS, r=['masks'], w=['mask4'])
    cp('pool', mask4[:, 128:256], MUI, r=['masks'], w=['mask4'])
    cp('pool', mask4[:, 256:384], MUS, r=['masks'], w=['mask4'])
    cp('pool', mask4[:, 384:512], MUI, r=['masks'], w=['mask4'])

    ydma = []
    mrw = ar.mark()
    Hst = [[ar.f32(64), ar.f32(64)], [ar.f32(64), ar.f32(64)]]
    btp = [ar.f32(512), ar.f32(512)]
    ktp = [ar.f32(512), ar.f32(512)]
    for z_ in btp + ktp:
        memset('pool', z_, 0.0, w=[('pad', id(z_))])
    hcnt = [0]
    pqs = [[ar.f32(384), ar.f32(384)] for _ in range(4)]
    amat = [ar.f32(512) for _ in range(4)]
    trn = ar.f32(384)
    Xs = ar.f32(128)
    Us = ar.f32(128)
    gtok = ar.f32(512)
    s_sb = ar.f32(8)
    ybT = ar.bf16(512)
    yb_ = ar.f32(128)
    yc_ = ar.f32(128)
    sq_ = ar.f32(128)
    st_ = ar.f32(8)
    yfb = ar.bf16(128)
    dec = ar.f32(4)
    A = {}
    for nm in ['r', 'k', 'v', 'sg', 'L', 'a', 'kk', 'EL', 'tmp']:
        A[nm] = ar.f32(512)
    tk = lambda nm: ('rw', nm)
    for j in range(4):
        bv, tok = load_w(288 + 384 * j, 384)
        for par_ in range(2):
            for hd_ in range(2):
                memset('pool', Hst[par_][hd_], 0.0, w=[('H', par_)])
        for tb in range(4):
            tsl = slice(tb * 512, (tb + 1) * 512)
            for q, nm in enumerate(['r', 'k', 'v']):
                p, ptk = proj(bv, tok, 128 * q, 128, tb)
                if tb > 0:
                    pp, pptk = proj_prev(bv, tok, 128 * q, 128, tb * 512 - 1, 1, 7, 0)
                    act(pmu[:, 0:1], pp, AF.Copy, r=[pptk, 'pcol'], w=['pmu'], scale=pc(8 * j + q))
                else:
                    memset('pool', pmu[:, 0:1], 0.0, w=['pmu'])
                act(pmu[:, 1:513], p, AF.Copy, r=[ptk, 'pcol'], w=['pmu'], scale=pc(8 * j + q))
                stt(A[nm], p, omm(8 * j + q), pmu[:, 0:512], ALU.mult, ALU.add, r=[ptk, 'pmu', 'dcol'], w=[tk(nm)])
            pw, pwt = pq(4, 0, 4)
            mm(pw, wa2[0:64, j * 128:(j + 1) * 128], lwin[0:64, tsl], True, True, r=['wa2', ('lwin', tb)], w=pwt)
            act(A['sg'], pw, AF.Sigmoid, r=pwt + ['pcol'], w=[tk('sg')], bias=pc(8 * j + 5))
            pa, pat = pq(5, 0, 4)
            mm(pa, wa2[64:128, j * 128:(j + 1) * 128], lwin[64:128, tsl], True, True, r=['wa2', ('lwin', tb)], w=pat)
            act(A['a'], pa, AF.Sigmoid, r=pat + ['pcol'], w=[tk('a')], bias=pc(8 * j + 6))
            for c in range(4):
                cs = slice(c * 128, (c + 1) * 128)
                tr.op('dve', lambda e, o=A['L'][:, cs], d1=A['sg'][:, cs]: e.tensor_tensor_scan(
                    o, ones, d1, 0.0, ALU.mult, ALU.add), [tk('sg'), 'ones'], [tk('L')])
            tt('pool', A['tmp'], A['L'], A['sg'], ALU.subtract, r=[tk('L'), tk('sg')], w=[tk('tmp')])
            act(A['tmp'], A['tmp'], AF.Exp, r=[tk('tmp')], w=[tk('tmp')], scale=-C0)
            act(A['EL'], A['L'], AF.Exp, r=[tk('L')], w=[tk('EL')], scale=-C0)
            act(A['L'], A['L'], AF.Exp, r=[tk('L')], w=[tk('L')], scale=C0)
            ts('pool', A['kk'], A['k'], pc(8 * j + 3), None, ALU.mult, None, r=[tk('k'), 'pcol'], w=[tk('kk')])
            act(A['sg'], A['kk'], AF.Square, r=[tk('kk')], w=[tk('sg')])
            pn, pnt = pq(6, 0, 4)
            mm(pn, BLK, A['sg'], True, True, r=['masks', tk('sg')], w=pnt)
            act(A['sg'], pn, AF.Sqrt, r=pnt, w=[tk('sg')])
            ts('dve', A['sg'], A['sg'], L2_EPS, None, ALU.max, None, r=[tk('sg')], w=[tk('sg')])
            tr.op('dve', lambda e, a_=A['sg']: e.reciprocal(a_, a_), [tk('sg')], [tk('sg')])
            tt('dve', A['kk'], A['kk'], A['sg'], ALU.mult, r=[tk('kk'), tk('sg')], w=[tk('kk')])
            ts('dve', A['sg'], A['a'], pc(8 * j + 4), dcol[:, 16 + j:17 + j], ALU.mult, ALU.add,
               r=[tk('a'), 'pcol', 'dcol'], w=[tk('sg')])
            tt('dve', A['k'], A['k'], A['sg'], ALU.mult, r=[tk('k'), tk('sg')], w=[tk('k')])
            tt('pool', A['sg'], A['r'], A['k'], ALU.mult, r=[tk('r'), tk('k')], w=[tk('sg')])
            psS, psSt = pq(7, 1, 1)
            for c in range(4):
                mm(psS[:, 2 * c:2 * c + 2], A['sg'][:, c * 128:(c + 1) * 128], rkm[:, 2 * j:2 * j + 2], True, True,
                   r=[tk('sg'), 'rkm'], w=psSt)
            cp('act', s_sb, psS[:, 0:8], r=psSt, w=['s_sb'])
            tt('pool', A['a'], A['kk'], A['a'], ALU.mult, r=[tk('kk'), tk('a')], w=[tk('a')])
            tt('dve', A['r'], A['r'], A['EL'], ALU.mult, r=[tk('r'), tk('EL')], w=[tk('r')])
            tt('pool', A['k'], A['k'], A['L'], ALU.mult, r=[tk('k'), tk('L')], w=[tk('k')])
            tt('dve', A['a'], A['a'], A['L'], ALU.mult, r=[tk('a'), tk('L')], w=[tk('a')])
            stt(A['kk'], A['kk'], -1.0, A['tmp'], ALU.mult, ALU.mult, r=[tk('kk'), tk('tmp')], w=[tk('kk')])
            cp('pool', dec, A['EL'].rearrange("p (c t) -> p c t", c=4)[:, :, 127], r=[tk('EL')], w=['dec'])
            decb = dec.unsqueeze(2).to_broadcast([128, 4, 128])
            tt('dve', A['L'].rearrange("p (c t) -> p c t", c=4), A['a'].rearrange("p (c t) -> p c t", c=4), decb,
               ALU.mult, r=[tk('a'), 'dec', tk('L')], w=[tk('L')])
            tt('pool', A['tmp'].rearrange("p (c t) -> p c t", c=4), A['k'].rearrange("p (c t) -> p c t", c=4), decb,
               ALU.mult, r=[tk('k'), 'dec', tk('tmp')], w=[tk('tmp')])
            rt_, kt_, bt_, at_, Bd_, Kd_ = A['r'], A['k'], A['a'], A['kk'], A['L'], A['tmp']
            for hd_ in range(2):
                hs_ = slice(64 * hd_, 64 * hd_ + 64)
                cp('pool', btp[hd_][hs_, :], bt_[hs_, :], r=[tk('a')], w=[('pad', id(btp[hd_]))])
                cp('act', ktp[hd_][hs_, :], kt_[hs_, :], r=[tk('k')], w=[('pad', id(ktp[hd_]))])
            if j == 0 and tb < 2:
                for nm_, ap_, t_ in [('rt', rt_, 'r'), ('kt', kt_, 'k'), ('bt', bt_, 'a'), ('at', at_, 'kk'),
                                     ('Bd', Bd_, 'L'), ('Kd', Kd_, 'tmp'), ('vv', A['v'], 'v')]:
                    dump('%s%d' % (nm_, tb), ap_, [tk(t_)])
            if j == 0 and tb == 0:
                stage_end('rwA')
            for c in range(4):
                cs = slice(tb * 512 + c * 128, tb * 512 + (c + 1) * 128)
                pg, pgt = pq(4, c, 1)
                mm(pg, glB[:, cs], g2a[:, j * 128:(j + 1) * 128], True, False, r=[('glB', tb), 'g2'], w=pgt)
                mm(pg, glC[0:32, cs], g2b[0:32, j * 128:(j + 1) * 128], False, True, r=[('glC', tb), 'g2'], w=pgt)
            pg4, pg4t = pq(4, 0, 4)
            cp('act', gtok, pg4, r=pg4t, w=['gtok'])
            if j == 0 and tb == 0:
                stage_end('rwA1')
            for c in range(4):
                cs = slice(c * 128, (c + 1) * 128)
                ptr_, ptrt = pq(4, 0, 3)
                tp(ptr_[:, 0:128], A['v'][:, cs], ident, r=[tk('v'), 'ident'], w=ptrt)
                tp(ptr_[:, 128:256], Bd_[:, cs], ident, r=[tk('L'), 'ident'], w=ptrt)
                tp(ptr_[:, 256:384], Kd_[:, cs], ident, r=[tk('tmp'), 'ident'], w=ptrt)
                cp('act', trn, ptr_, r=ptrt, w=['trn'])
                vtok, BdT, KdT = trn[:, 0:128], trn[:, 128:256], trn[:, 256:384]
                if j == 0 and tb == 0 and c == 0:
                    dump('trn', trn, ['trn'])
                    stage_end('rwA2')
                if c % 2 == 0:
                    probs = [(cc_, hd_) for cc_ in (c, c + 1) for hd_ in range(2)]
                    PB = [5, 6, 0, 1]
                    for pi, (cc_, hd) in enumerate(probs):
                        ccs = slice(cc_ * 128, (cc_ + 1) * 128)
                        pA, pAt = pq(PB[pi], 0, 4)
                        bpt, kpt = ('pad', id(btp[hd])), ('pad', id(ktp[hd]))
                        mm(pA[:, 0:128], btp[hd][:, ccs], at_[:, ccs], True, True, r=[bpt, tk('kk')], w=pAt)
                        mm(pA[:, 128:256], btp[hd][:, ccs], rt_[:, ccs], True, True, r=[bpt, tk('r')], w=pAt)
                        mm(pA[:, 256:384], ktp[hd][:, ccs], at_[:, ccs], True, True, r=[kpt, tk('kk')], w=pAt)
                        mm(pA[:, 384:512], ktp[hd][:, ccs], rt_[:, ccs], True, True, r=[kpt, tk('r')], w=pAt)
                    pQa, pQt = pq(2, 0, 4)
                    for pi, (cc_, hd) in enumerate(probs):
                        ccs = slice(cc_ * 128, (cc_ + 1) * 128)
                        mm(pQa[:, pi * 128:(pi + 1) * 128], at_[:, ccs], btp[hd][:, ccs], True, True,
                           r=[('pad', id(btp[hd])), tk('kk')], w=pQt)
                    for pi in range(4):
                        pA, pAt = pq(PB[pi], 0, 4)
                        tt('dve', amat[pi], pA, mask4, ALU.mult, r=pAt + ['mask4'], w=[('amat', pi)])
                    for pi in range(4):
                        b0 = pqs[pi][0]
                        cp('act', b0[:, 0:128], amat[pi][:, 0:128], r=[('amat', pi)], w=[('pqs', pi, 0)])
                        tt('dve', b0[:, 128:256], pQa[:, pi * 128:(pi + 1) * 128], MLS, ALU.mult, r=pQt + ['masks'],
                           w=[('pqs', pi, 0)])
                        tt('pool', b0[:, 256:384], amat[pi][:, 0:128], ident, ALU.add, r=[('amat', pi), 'ident'],
                           w=[('pqs', pi, 0)])
                    for it in range(6):
                        for pi in range(4):
                            cur = pqs[pi][it % 2]
                            ct = ('pqs', pi, it % 2)
                            pI, pIt = pq(PB[pi], 0, 3)
                            mm(pI[:, 0:128], cur[:, 0:128], cur[:, 128:256], True, True, r=[ct], w=pIt)
                            if it < 5:
                                mm(pI[:, 128:256], cur[:, 128:256], cur[:, 0:128], True, True, r=[ct], w=pIt)
                        for pi in range(4):
                            nxt = pqs[pi][(it + 1) % 2]
                            nt = ('pqs', pi, (it + 1) % 2)
                            pI, pIt = pq(PB[pi], 0, 3)
                            n2 = 256 if it < 5 else 128
                            if it < 5:
                                cp('act', nxt[:, 128:256], pI[:, 0:128], r=pIt, w=[nt])
                                cp('act', nxt[:, 0:128], pI[:, 128:256], r=pIt, w=[nt])
                            else:
                                cp('act', nxt[:, 128:256], pI[:, 0:128], r=pIt, w=[nt])
                        for pi in range(4):
                            cur, nxt = pqs[pi][it % 2], pqs[pi][(it + 1) % 2]
                            ct, nt = ('pqs', pi, it % 2), ('pqs', pi, (it + 1) % 2)
                            pI, pIt = pq(PB[pi], 0, 3)
                            mm(pI[:, 256:384], nxt[:, 128:256], cur[:, 256:384], True, True, r=[ct, nt], w=pIt)
                        for pi in range(4):
                            cur, nxt = pqs[pi][it % 2], pqs[pi][(it + 1) % 2]
                            ct, nt = ('pqs', pi, it % 2), ('pqs', pi, (it + 1) % 2)
                            pI, pIt = pq(PB[pi], 0, 3)
                            tt('dve', nxt[:, 256:384], pI[:, 256:384], cur[:, 256:384], ALU.add, r=pIt + [ct], w=[nt])
                pi0 = 2 * (c % 2)
                if j == 0 and tb == 0 and c < 2:
                    dump('amat%d' % c, amat[pi0], [('amat', pi0)])
                    dump('Tt%d' % c, pqs[pi0][0][:, 256:384], [('pqs', pi0, 0)])
                if j == 0 and tb == 0 and c == 0:
                    stage_end('rwB')
                Tt = [pqs[pi0 + hd][0][:, 256:384] for hd in range(2)]
                Ttt = [('pqs', pi0 + hd, 0) for hd in range(2)]
                Hc = Hst[hcnt[0] % 2]
                Hn = Hst[(hcnt[0] + 1) % 2]
                Hcp = Hc
                Hct, Hnt = ('H', hcnt[0] % 2), ('H', (hcnt[0] + 1) % 2)
                hcnt[0] += 1
                pX, pXt = pq(4, 3, 1)
                for hd in range(2):
                    hs = slice(64 * hd, 64 * hd + 64)
                    vs = slice(64 * hd, 64 * hd + 64)
                    mm(pX[:, vs], at_[:, cs], Hc[hd], True, False, r=[tk('kk'), Hct], w=pXt)
                    mm(pX[:, vs], amat[pi0 + hd][:, 256:384], vtok[:, vs], False, True, r=[('amat', pi0 + hd), 'trn'], w=pXt)
                cp('act', Xs, pX, r=pXt, w=['Xs'])
                pU, pUt = pq(7, 0, 1)
                for hd in range(2):
                    vs = slice(64 * hd, 64 * hd + 64)
                    mm(pU[:, vs], Tt[hd], Xs[:, vs], True, True, r=[Ttt[hd], 'Xs'], w=pUt)
                cp('act', Us, pU, r=pUt, w=['Us'])
                if j == 0 and tb == 0 and c == 0:
                    dump('Xs0', Xs, ['Xs'])
                    dump('Us0', Us, ['Us'])
                    stage_end('rwB1')
                pY, pYt = pq(4, 0, 1)
                for hd in range(2):
                    hs = slice(64 * hd, 64 * hd + 64)
                    vs = slice(64 * hd, 64 * hd + 64)
                    mm(pY[:, vs], rt_[:, cs], Hc[hd], True, False, r=[tk('r'), Hct], w=pYt)
                    mm(pY[:, vs], amat[pi0 + hd][:, 128:256], Us[:, vs], False, False, r=[('amat', pi0 + hd), 'Us'], w=pYt)
                    mm(pY[:, vs], amat[pi0 + hd][:, 384:512], vtok[:, vs], False, True, r=[('amat', pi0 + hd), 'trn'], w=pYt)
                pH, pHt = pq(4, 1, 1)
                mm(pH, BdT, Us, True, False, r=['trn', 'Us'], w=pHt)
                mm(pH, KdT, vtok, False, True, r=['trn'], w=pHt)
                for hd in range(2):
                    hs = slice(64 * hd, 64 * hd + 64)
                    stt(Hn[hd][hs, :], Hc[hd][hs, :], dec[hs, c:c + 1], pH[hs, 64 * hd:64 * hd + 64], ALU.mult, ALU.add,
                        r=[Hct, 'dec'] + pHt, w=[Hnt])
                if j == 0 and tb == 0 and c == 0:
                    dump('Hn0', Hn[0], [Hnt])
                    dump('Hm0', Hn[1], [Hnt])
                    stage_end('rwB2')
                cp('act', yb_, pY, r=pYt, w=['yb'])
                if j == 0 and tb == 0 and c < 2:
                    dump('Xs%d' % c, Xs, ['Xs'])
                    dump('Us%d' % c, Us, ['Us'])
                    dump('yb%d' % c, yb_, ['yb'])
                    dump('Hn%d' % c, Hn[0], [Hnt])
                    dump('Hm%d' % c, Hn[1], [Hnt])
                y3 = yb_.rearrange("p (h v) -> p h v", h=2)
                if os.environ.get('KDBG_RED'):
                    tr.op('dve', lambda e, o=st_[:, 0:2], i_=y3: e.tensor_reduce(o, i_, AX.X, ALU.add), ['yb'], ['st'])
                else:
                    for hd_ in range(2):
                        act(sq_[:, 64 * hd_:64 * hd_ + 64], yb_[:, 64 * hd_:64 * hd_ + 64], AF.Copy, r=['yb'],
                            w=['sq', 'st'], accum_out=st_[:, hd_:hd_ + 1])
                ts('dve', st_[:, 0:2], st_[:, 0:2], -1.0 / 64, None, ALU.mult, None, r=['st'], w=['st'])
                tt('dve', yc_.rearrange("p (h v) -> p h v", h=2), y3, st_[:, 0:2].unsqueeze(2).to_broadcast([128, 2, 64]),
                   ALU.add, r=['yb', 'st'], w=['yc'])
                if os.environ.get('KDBG_RED'):
                    act(sq_, yc_, AF.Square, r=['yc'], w=['sq'])
                else:
                    for hd_ in range(2):
                        act(sq_[:, 64 * hd_:64 * hd_ + 64], yc_[:, 64 * hd_:64 * hd_ + 64], AF.Square, r=['yc'],
                            w=['sq', 'st'], accum_out=st_[:, 2 + hd_:3 + hd_])
                if os.environ.get('KDBG_RED'):
                    tr.op('dve', lambda e, o=st_[:, 2:4], i_=sq_.rearrange("p (h v) -> p h v", h=2): e.tensor_reduce(
                        o, i_, AX.X, ALU.add), ['sq'], ['st'])
                act(st_[:, 2:4], st_[:, 2:4], AF.Sqrt, r=['st', 'epsc'], w=['st'], bias=epsc[:, 1:2], scale=1.0 / 64)
                tr.op('dve', lambda e, a_=st_[:, 2:4]: e.reciprocal(a_, a_), ['st'], ['st'])
                tt('dve', yc_.rearrange("p (h v) -> p h v", h=2), yc_.rearrange("p (h v) -> p h v", h=2),
                   st_[:, 2:4].unsqueeze(2).to_broadcast([128, 2, 64]), ALU.mult, r=['yc', 'st'], w=['yc'])
                if j == 0 and tb == 0 and c == 0:
                    stage_end('rwB3')
                SK = os.environ.get('KDBG_SKIP', '')
                if '1' not in SK:
                    tt('pool', yc_, yc_, lng[:, j * 128:(j + 1) * 128], ALU.mult, r=['yc', 'rowv'], w=['yc'])
                if '2' not in SK:
                    tt('pool', yc_, yc_, lnb[:, j * 128:(j + 1) * 128], ALU.add, r=['yc', 'rowv'], w=['yc'])
                if '3' not in SK:
                    tt('dve', sq_.rearrange("p (h v) -> p h v", h=2), vtok.rearrange("p (h v) -> p h v", h=2),
                       s_sb[:, 2 * c:2 * c + 2].unsqueeze(2).to_broadcast([128, 2, 64]), ALU.mult, r=['trn', 's_sb'],
                       w=['sq'])
                if '4' not in SK:
                    tt('dve', yc_, yc_, sq_, ALU.add, r=['yc', 'sq'], w=['yc'])
                if j == 0 and tb == 0 and c < 2:
                    dump('ypre%d' % c, yc_, ['yc'])
                if j == 0 and tb == 0 and c == 0:
                    stage_end('rwC')
                tt('dve', yfb, yc_, gtok[:, cs], ALU.mult, r=['yc', 'gtok'], w=['yfb'])
                pT, pTt = pq(7, 1, 1)
                pTb = pT.bitcast(BF16)[:, 0:128]
                tp(pTb, yfb, identb2, r=['yfb', 'identb2'], w=pTt)
                cp('act', ybT[:, cs], pTb, r=pTt, w=['ybT'])
            ydma.append(dma(ysrc_l[tb][512 + j * 128: 512 + (j + 1) * 128, :], ybT, r=['ybT'], w=[('ysrc', 4 + j, tb)]))
            if 'ybT%d' % tb in dbg_d and j == 0:
                tmpd2 = A['sg']
                cp('dve', tmpd2, ybT, r=['ybT'], w=[tk('sg')])
                dump('ybT%d' % tb, tmpd2, [tk('sg')])
            if j == 0 and tb == 1:
                stage_end('rwkv1')
    tr.fence()
    ar.release(mrw)

    mlr = ar.mark()
    lw = ar.f32(1024)
    lwv = lw.rearrange("p (g q n) -> p g q n", g=2, q=2)
    xp = [ar.f32(516), ar.f32(516)]
    uu = [ar.f32(512), ar.f32(512)]
    hh_ = [ar.f32(512), ar.f32(512)]
    hprev = ar.f32(2)
    R_ = ar.f32(512)
    I_ = ar.f32(512)
    A_ = ar.f32(512)
    M_ = ar.f32(512)
    G1 = ar.f32(512)
    G2 = ar.f32(512)
    yaT = ar.bf16(512)
    for h2 in range(2):
        bv, tok = load_w(288 + 1536 + 512 * h2, 512)
        for g in range(2):
            dma(lwv[:, g, :, :], lruw_d[g, h2, :, :].rearrange("(q p) n -> p q n", p=128), w=['lw'])
        for tb in range(4):
            tsl = slice(tb * 512, (tb + 1) * 512)
            for q in range(2):
                jt = 2 * h2 + q
                pb_ = 40 + 8 * jt
                p, ptk = proj(bv, tok, 128 * q, 128, tb)
                cp('act', xp[q][:, 3:515], p, r=[ptk], w=[('xp', q)])
                if tb > 0:
                    pp, pptk = proj_prev(bv, tok, 128 * q, 128, tb * 512 - 3, 3, 7, 0)
                    cp('act', xp[q][:, 0:3], pp, r=[pptk], w=[('xp', q)])
                else:
                    memset('pool', xp[q][:, 0:3], 0.0, w=[('xp', q)])
                ts('dve', uu[q], xp[q][:, 3:515], pc(pb_ + 3), pc(pb_ + 4), ALU.mult, ALU.add, r=[('xp', q), 'pcol'],
                   w=[('uu', q)])
                for kx in range(3):
                    stt(uu[q], xp[q][:, kx:kx + 512], pc(pb_ + kx), uu[q], ALU.mult, ALU.add,
                        r=[('xp', q), 'pcol', ('uu', q)], w=[('uu', q)])
            for qo in range(2):
                jt = 2 * h2 + qo
                pb_ = 40 + 8 * jt
                pr, prt = pq(4, 0, 4)
                pi, pit = pq(5, 0, 4)
                for q in range(2):
                    mm(pr, lwv[:, 0, q, qo * 128:(qo + 1) * 128], uu[q], q == 0, q == 1, r=['lw', ('uu', q)], w=prt)
                for q in range(2):
                    mm(pi, lwv[:, 1, q, qo * 128:(qo + 1) * 128], uu[q], q == 0, q == 1, r=['lw', ('uu', q)], w=pit)
                act(R_, pr, AF.Sigmoid, r=prt + ['pcol'], w=['R_'], bias=pc(pb_ + 5))
                act(I_, pi, AF.Sigmoid, r=pit + ['pcol'], w=['I_'], bias=pc(pb_ + 6))
                act(A_, R_, AF.Exp, r=['R_', 'dcol'], w=['A_'], scale=dcol[:, 24 + jt:25 + jt])
                act(M_, R_, AF.Exp, r=['R_', 'dcol'], w=['M_'], scale=dcol[:, 28 + jt:29 + jt])
                ts('dve', M_, M_, -1.0, 1.0, ALU.mult, ALU.add, r=['M_'], w=['M_'])
                ts('dve', M_, M_, 0.0, None, ALU.max, None, r=['M_'], w=['M_'])
                act(M_, M_, AF.Sqrt, r=['M_'], w=['M_'])
                if tb == 0:
                    memset('pool', M_[:, 0:1], 1.0, w=['M_'])
                tt('dve', I_, I_, M_, ALU.mult, r=['I_', 'M_'], w=['I_'])
                tt('pool', I_, I_, uu[qo], ALU.mult, r=['I_', ('uu', qo)], w=['I_'])
                if tb == 0:
                    init = 0.0
                    rr = ['A_', 'I_']
                else:
                    cp('pool', hprev[:, qo:qo + 1], hh_[qo][:, 511:512], r=[('hh', qo)], w=[('hprev', qo)])
                    init = hprev[:, qo:qo + 1]
                    rr = ['A_', 'I_', ('hprev', qo)]
                tr.op('dve', lambda e, o=hh_[qo], i0=init: e.tensor_tensor_scan(o, A_, I_, i0, ALU.mult, ALU.add),
                      rr, [('hh', qo)])
                p, ptk = proj(bv, tok, 256 + 128 * qo, 128, tb)
                cp('act', G1, p, r=[ptk], w=['G1'])
                act(G2, p, AF.Square, r=[ptk], w=['G2'])
                ts('dve', G2, G2, 0.044715, 1.0, ALU.mult, ALU.add, r=['G2'], w=['G2'])
                tt('pool', G2, G2, G1, ALU.mult, r=['G2', 'G1'], w=['G2'])
                act(G2, G2, AF.Sigmoid, r=['G2'], w=['G2'], scale=1.5957691216057308)
                tt('dve', G1, G1, G2, ALU.mult, r=['G1', 'G2'], w=['G1'])
                tt('dve', yaT, hh_[qo], G1, ALU.mult, r=[('hh', qo), 'G1'], w=['yaT'])
                ydma.append(dma(ysrc_l[tb][jt * 128:(jt + 1) * 128, :], yaT, r=['yaT'], w=[('ysrc', jt, tb)]))
                if ('yaT%d' % tb) in dbg_d and h2 == 0 and qo == 1:
                    cp('dve', G2, yaT, r=['yaT'], w=['G2'])
                    dump('yaT%d' % tb, G2, ['G2'])
    stage_end('mix')
    tr.fence()
    ar.release(mlr)
    ar.release(m_phase)

    for tb in range(4):
        toks_ = [('ysrc', a, tb) for a in range(8)]
        if os.environ.get('KDBG_NOCC'):
            dma(ydst_l[tb][0:1024, :], ysrc_l[tb], r=toks_, w=[('ydst', tb)], q='sp')
            dma(ydst_l[tb][1024:2048, :], ysrc_l[tb], r=toks_, w=[('ydst', tb)], q='sp')
        else:
            tr.op('pool', lambda e, a_=ysrc_l[tb], b_=ydst_l[tb]: e.collective_compute(
                "AllGather", ALU.bypass, replica_groups=[[0, 1], [2, 3], [4, 5], [6, 7]], ins=[a_], outs=[b_]),
                toks_, [('ydst', tb)], cc=True)
    x1 = ar.f32(8 * 2048)
    x1v = x1.rearrange("p (a n) -> p a n", a=8)
    vecA = ar.f32(2048)
    vecB = ar.f32(2048)
    vecC = ar.f32(2048)
    vecD = vecA
    tmpf = ar.f32(512)
    mtr = ar.mark()
    alloc_mb()
    mod_vec(2, vecA)
    mod_vec(4, vecB, gidx=1)
    mod_vec(3, vecC)
    tr.fence()
    ar.release(mtr)
    for a in range(8):
        dma(x1v[:, a, :], xown_d[a * 128:(a + 1) * 128, :], w=[('x1', a)])
    mys = ar.mark()
    ysel = ar.bf16(16 * 1024)
    yselv = ysel.rearrange("p (k t) -> p k t", k=16)
    tA = [ar.bf16(1024), ar.bf16(1024)]
    tB = [ar.bf16(1024), ar.bf16(1024)]
    for k in range(16):
        a_, b_ = tA[k % 2], tB[k % 2]
        for h_ in range(2):
            dma(a_[:, h_ * 512:(h_ + 1) * 512], ydst_l[h_][k * 128:(k + 1) * 128, :], r=[('ydst', h_)], w=[('tA', k % 2)])
            dma(b_[:, h_ * 512:(h_ + 1) * 512], ydst_l[2 + h_][k * 128:(k + 1) * 128, :], r=[('ydst', 2 + h_)],
                w=[('tB', k % 2)])
        ts('pool', a_, a_, hsel[:, 0:1], None, ALU.mult, None, r=[('tA', k % 2), 'hsel'], w=[('tA', k % 2)])
        stt(yselv[:, k, :], b_, hsel[:, 1:2], a_, ALU.mult, ALU.add, r=[('tA', k % 2), ('tB', k % 2), 'hsel'],
            w=[('ysel', k)])
    ysel_all = [('ysel', k) for k in range(16)]
    if 'ysel' in dbg_d:
        tmpq = ar.f32(1024)
        cp('dve', tmpq, yselv[:, 5, :], r=ysel_all, w=['tmpq'])
        dump('ysel', tmpq, ['tmpq'])
        cp('dve', tmpq, yselv[:, 13, :], r=ysel_all + [('dbgout', 'ysel')], w=['tmpq'])
        dump('ysel2', tmpq, ['tmpq'])
    stage_end('xchg')
    wb2 = [ar.bf16(16 * 512), ar.bf16(16 * 512)]
    w2c = [0]

    def load_blk(src, off, nk, ncols, buf=None):
        if buf is None:
            buf = wb2[w2c[0] % 2]
            w2c[0] += 1
        tok_ = ('wb2', id(buf))
        load_flat(buf, src, off, nk * ncols, tok_)
        return buf[:, 0:nk * ncols].rearrange("p (k n) -> p k n", k=nk), tok_

    for cg in range(4):
        wv, wtok = load_blk(wout_d, cg * 8192, 16, 512)
        for a in range(8):
            bi = (cg * 8 + a) % 4
            for k in range(16):
                mm(ps[bi][:, :], yselv[:, k, a * 128:(a + 1) * 128], wv[:, k, :], k == 0, k == 15,
                   r=ysel_all + [wtok], w=[('ps', bi)])
            csl = slice(cg * 512, (cg + 1) * 512)
            tt('dve', tmpf, ps[bi][:, :], vecA[:, csl], ALU.mult, r=[('ps', bi), ('vec', id(vecA))], w=['tmpf'])
            tt('pool', x1v[:, a, csl], x1v[:, a, csl], tmpf, ALU.add, r=['tmpf', ('x1', a)], w=[('x1', a)])
    dump('x1a', x1v[:, 0, :], [('x1', 0)])
    dump('x1b', x1v[:, 7, :], [('x1', 7)])
    stage_end('wout')
    tr.fence()
    ar.release(mys)
    alloc_mb()
    mod_vec(5, vecD)
    tr.fence()
    ar.release(mys)

    h2T = ar.bf16(16 * 1024)
    h2Tv = h2T.rearrange("p (k t) -> p k t", k=16)
    actT = ar.bf16(4 * 1024)
    actv = actT.rearrange("p (f t) -> p f t", f=4)
    wg = [ar.bf16(16 * 256), ar.bf16(16 * 256)]
    wd = [ar.bf16(4 * 2048)] * 2
    hf2 = ar.f32(2048)
    hb2 = ar.bf16(2048)
    junk2 = hb2
    st2 = ar.f32(32)

    def rms_rstd(a, col):
        act(junk2, x1v[:, a, :], AF.Square, r=[('x1', a)], w=['hb2', ('st2', col)], accum_out=st2[:, col:col + 1])
        act(st2[:, col:col + 1], st2[:, col:col + 1], AF.Sqrt, r=[('st2', col), 'epsc'], w=[('st2', col)],
            bias=epsc[:, 0:1], scale=1.0 / D)
        tr.op('dve', lambda e, a_=st2[:, col:col + 1]: e.reciprocal(a_, a_), [('st2', col)], [('st2', col)])

    for a in range(8):
        rms_rstd(a, a)
        stt(hf2, x1v[:, a, :], st2[:, a:a + 1], vecB, ALU.mult, ALU.mult, r=[('x1', a), ('st2', a), ('vec', id(vecB))],
            w=['hf2'])
        tt('pool', hb2, hf2, vecC, ALU.add, r=['hf2', ('vec', id(vecC))], w=['hb2'])
        for half in range(2):
            bi = 4 + (a * 2 + half) % 4
            pbv = ps[bi][:, :].bitcast(BF16)
            for q in range(8):
                dc = half * 8 + q
                tp(pbv[:, q * 128:(q + 1) * 128], hb2[:, dc * 128:(dc + 1) * 128], identb, r=['hb2', 'identb'],
                   w=[('ps', bi)])
            cp('act' if half == 0 else 'dve', h2Tv[:, half * 8:(half + 1) * 8, a * 128:(a + 1) * 128],
               pbv.rearrange("p (q t) -> p q t", q=8), r=[('ps', bi)], w=[('h2T', a)])
    h2T_all = [('h2T', a) for a in range(8)]

    tmpg = ar.f32(512)
    for fg in range(11):
        dv, dtok = load_blk(wdown_d, fg * 8192, 4, 2048, buf=wd[fg % 2])
        for ft in range(4):
            f = fg * 4 + ft
            gb = wg[f % 2]
            gv_, gtok_ = load_blk(wgu_d, f * 4096, 16, 256, buf=gb)
            for tg in range(2):
                bg, bu = (tg * 2) % 4, (tg * 2 + 1) % 4
                for k in range(16):
                    mm(ps[bg][:, :], gv_[:, k, 0:128], h2Tv[:, k, tg * 512:(tg + 1) * 512], k == 0, k == 15,
                       r=[gtok_] + h2T_all, w=[('ps', bg)])
                for k in range(16):
                    mm(ps[bu][:, :], gv_[:, k, 128:256], h2Tv[:, k, tg * 512:(tg + 1) * 512], k == 0, k == 15,
                       r=[gtok_] + h2T_all, w=[('ps', bu)])
                act(tmpg, ps[bg][:, :], AF.Silu, r=[('ps', bg)], w=['tmpg'])
                tt('dve', actv[:, ft, tg * 512:(tg + 1) * 512], tmpg, ps[bu][:, :], ALU.mult, r=['tmpg', ('ps', bu)],
                   w=[('actT', ft)])
        for a in range(8):
            for cg in range(4):
                bi = 4 + (a * 4 + cg) % 4
                for ft in range(4):
                    mm(ps[bi][:, :], actv[:, ft, a * 128:(a + 1) * 128], dv[:, ft, cg * 512:(cg + 1) * 512], ft == 0,
                       ft == 3, r=[('actT', ft), dtok], w=[('ps', bi)])
                csl = slice(cg * 512, (cg + 1) * 512)
                tt('dve', tmpf, ps[bi][:, :], vecD[:, csl], ALU.mult, r=[('ps', bi), ('vec', id(vecD))], w=['tmpf'])
                tt('pool', x1v[:, a, csl], x1v[:, a, csl], tmpf, ALU.add, r=['tmpf', ('x1', a)], w=[('x1', a)])
    tr.fence()
    dma(vecB, gvec_d[2, :].partition_broadcast(128), w=[('vec', id(vecB))])
    outs = []
    for a in range(8):
        rms_rstd(a, 8 + a)
        stt(hf2, x1v[:, a, :], st2[:, 8 + a:9 + a], vecB, ALU.mult, ALU.mult,
            r=[('x1', a), ('st2', 8 + a), ('vec', id(vecB))], w=['hf2'])
        outs.append(dma(out_d[a * 128:(a + 1) * 128, :], hf2, r=['hf2'], w=[('out', a)], q='sp'))
    fin_deps = list(outs)
    for nm in dbg_d:
        pass
    blk = es.enter_context(nc.Block())
    tr.emit(nc, es, blk, fin_deps + tr.dbg_ops)
    es.close()
    return nc


def _perm_cols(hh):
    cols = []
    rw = 2048
    wl0, al0, gl0 = rw + 3072, rw + 3072 + 64, rw + 3072 + 128
    cols += list(range(wl0, wl0 + 64)) + list(range(al0, al0 + 64)) + list(range(gl0, gl0 + 160))
    for j in range(4):
        c = hh * 512 + j * 128
        for q in range(3):
            cols += list(range(rw + q * 1024 + c, rw + q * 1024 + c + 128))
    for h2 in range(2):
        c = hh * 512 + h2 * 256
        cols += list(range(c, c + 256)) + list(range(1024 + c, 1024 + c + 256))
    return np.array(cols)


_CACHE = {}


def kernel(x, c, w_ada, b_ada, norm_mix_g, w_in, conv_w, conv_b, lru_wa, lru_ba, lru_wx, lru_bx, lru_lambda,
           rwkv_mu, rwkv_w0, rwkv_w2, rwkv_a0, rwkv_a2, rwkv_g2, rwkv_k_k, rwkv_k_a, rwkv_r_k, rwkv_ln_g,
           rwkv_ln_b, w_out, norm_ffn_g, w_gu, w_down, final_norm_g, _dbg=None, _stage=None, _ncores=8):
    f = lambda a: np.ascontiguousarray(np.asarray(a, dtype=np.float32))
    x, c = f(x), f(c)
    if 'nc' not in _CACHE or _dbg is not None:
        _CACHE['nc'] = build(_dbg, _stage)
    nc = _CACHE['nc']
    ident = np.eye(128, dtype=np.float32)
    masks = np.zeros((128, 4, 128), np.float32)
    masks[:, 0] = np.triu(np.ones((128, 128)), 1)
    masks[:, 1] = np.triu(np.ones((128, 128)), 0)
    masks[:, 2] = np.tril(np.ones((128, 128)), -1)
    masks[0:64, 3, 0:64] = 1
    masks[64:128, 3, 64:128] = 1
    mu = f(rwkv_mu)[0]
    wout_perm = np.concatenate([np.arange(0, 512), np.arange(1024, 1536), np.arange(512, 1024), np.arange(1536, 2048)])
    wout_p = f(w_out)[0][wout_perm]
    wgu_ = f(w_gu)[0]
    g_ = wgu_[:, 0:DFF].reshape(16, 128, 44, 128)
    u_ = wgu_[:, DFF:2 * DFF].reshape(16, 128, 44, 128)
    wgu = np.ascontiguousarray(np.concatenate([g_, u_], axis=3).transpose(1, 2, 0, 3).reshape(128, 44 * 4096))
    wdown = np.ascontiguousarray(f(w_down)[0].reshape(11, 4, 128, 2048).transpose(2, 0, 1, 3).reshape(128, 11 * 8192))
    wout_p = np.ascontiguousarray(wout_p.reshape(16, 128, 4, 512).transpose(1, 2, 0, 3).reshape(128, 4 * 8192))
    wada = f(w_ada)[0]
    bada = f(b_ada)[0]
    gvec = np.stack([f(norm_mix_g)[0], f(norm_ffn_g)[0], f(final_norm_g)])
    in_maps = []
    for r in range(8):
        b, hh = r // 2, r % 2
        cols = _perm_cols(hh)
        winc = f(w_in)[0][:, cols]
        blocks = []
        for (c0_, n_) in [(0, 288)] + [(288 + 384 * j_, 384) for j_ in range(4)] + [(1824 + 512 * h_, 512) for h_ in range(2)]:
            blocks.append(winc[:, c0_:c0_ + n_].reshape(16, 128, n_).transpose(1, 0, 2).reshape(128, 16 * n_))
        win = np.ascontiguousarray(np.concatenate(blocks, axis=1))
        pcol = np.zeros((128, 80), np.float32)
        rkm = np.zeros((128, 8), np.float32)
        for j in range(4):
            ch = hh * 512 + j * 128 + np.arange(128)
            pcol[:, 8 * j + 0] = mu[ch]
            pcol[:, 8 * j + 1] = mu[1024 + ch]
            pcol[:, 8 * j + 2] = mu[2048 + ch]
            pcol[:, 8 * j + 3] = f(rwkv_k_k)[0][ch]
            pcol[:, 8 * j + 4] = f(rwkv_k_a)[0][ch]
            pcol[:, 8 * j + 5] = f(rwkv_w0)[0][ch]
            pcol[:, 8 * j + 6] = f(rwkv_a0)[0][ch]
            rkf = f(rwkv_r_k)[0].reshape(-1)[ch]
            rkm[0:64, 2 * j] = rkf[0:64]
            rkm[64:128, 2 * j + 1] = rkf[64:128]
            pcol[:, 40 + 8 * j + 0:40 + 8 * j + 4] = f(conv_w)[0][:, ch].T
            pcol[:, 40 + 8 * j + 4] = f(conv_b)[0][ch]
            pcol[:, 40 + 8 * j + 5] = f(lru_ba)[0][ch]
            pcol[:, 40 + 8 * j + 6] = f(lru_bx)[0][ch]
            pcol[:, 40 + 8 * j + 7] = f(lru_lambda)[0][ch]
        pcol[:, 32] = mu[3072:3200]
        pcol[:, 33] = mu[3200:3328]
        pcol[0:32, 34] = mu[3328:3360]
        chs = hh * 512 + np.arange(512)
        rowv = np.stack([f(rwkv_ln_g)[0][chs], f(rwkv_ln_b)[0][chs]])
        wa2 = np.concatenate([f(rwkv_w2)[0][:, chs], f(rwkv_a2)[0][:, chs]], axis=0)
        g2 = np.ascontiguousarray(f(rwkv_g2)[0][:, chs])
        lruw = np.stack([f(lru_wa)[0][2 * hh:2 * hh + 2], f(lru_wx)[0][2 * hh:2 * hh + 2]])
        hsel = np.zeros((128, 2), np.float32)
        hsel[:, hh] = 1.0
        m = dict(x=x[b], xown=np.ascontiguousarray(x[b, hh * 1024:(hh + 1) * 1024]),
                 cb=np.ascontiguousarray(c[b].reshape(128, 16)), wada=wada, bada=bada, gvec=gvec, win=win, pcol=pcol,
                 rowv=np.ascontiguousarray(rowv), rkm=rkm, wa2=np.ascontiguousarray(wa2), g2=g2,
                 lruw=np.ascontiguousarray(lruw), wout=wout_p, wgu=wgu, wdown=wdown, ident=ident, masks=masks, hsel=hsel)
        in_maps.append(m)
    res = run_bass_kernel_spmd(nc, in_maps[:_ncores], core_ids=list(range(_ncores)))
    if _stage is not None:
        return None, res
    out = np.zeros((4, 2048, 2048), np.float32)
    for r in range(8):
        b, hh = r // 2, r % 2
        out[b, hh * 1024:(hh + 1) * 1024] = res.results[r]["out"]
    if _dbg is not None:
        return out, res
    return out
```

```python
import os
import numpy as np
from contextlib import ExitStack
import concourse.bass as bass
import concourse.mybir as mybir
from concourse.bass_utils import run_bass_kernel_spmd

F32 = mybir.dt.float32
BF16 = mybir.dt.bfloat16
AF = mybir.ActivationFunctionType
ALU = mybir.AluOpType
AX = mybir.AxisListType

D = 2048
T = 2048
TO = 1024
DFF = 5632
NWIN = 2848
C0 = float(np.exp(-0.5))
RMS_EPS = 1e-6
GN_EPS = 64e-5
L2_EPS = 1e-12
NDMASEM = 6


class Tr:
    def __init__(self):
        self.ops = []
        self.lw = {}
        self.rd = {}
        self.fenced = 0
        self.dbg_ops = []

    def op(self, eng, fn, r=(), w=(), dma=False, cc=False):
        i = len(self.ops)
        deps = set()
        psr = [t for t in r if isinstance(t, tuple) and t and t[0] == 'ps']
        if psr:
            r = [t for t in r if t not in psr]
            w = list(w) + psr
        for t in r:
            x = self.lw.get(t)
            if x is not None:
                deps.add(x)
        for t in w:
            x = self.lw.get(t)
            if x is not None:
                deps.add(x)
            for y in self.rd.get(t, ()):
                deps.add(y)
        deps.discard(i)
        for t in r:
            self.rd.setdefault(t, []).append(i)
        for t in w:
            self.lw[t] = i
            self.rd[t] = []
        self.ops.append(dict(eng=eng, fn=fn, deps=deps, dma=dma, cc=cc))
        return i

    def fence(self):
        last = {}
        dmas = []
        for i, o in enumerate(self.ops):
            if o['dma'] or o['cc']:
                dmas.append(i)
            elif o['fn'] is not None:
                last[o['eng']] = i
        deps = set(last.values()) | set(dmas[self.fenced:])
        self.fenced = len(dmas)
        for e in ['pe', 'act', 'dve', 'pool', 'sp']:
            i = self.op(e, None, (), ())
            self.ops[i]['deps'] = set(deps)

    def emit(self, nc, es, block, final_deps):
        ops = self.ops
        engs = ['pe', 'act', 'dve', 'pool', 'sp']
        csem = {e: es.enter_context(nc.semaphore("c_" + e)) for e in engs}
        dsem = {e: [es.enter_context(nc.semaphore("d_%s%d" % (e, k))) for k in range(NDMASEM)]
                for e in ['act', 'pool', 'sp']}
        ccsem = es.enter_context(nc.semaphore("ccs"))
        fin = self.op('sp', None, (), ())
        ops[fin]['deps'] = set(final_deps)
        used = set()
        for o in ops:
            for d in o['deps']:
                used.add(d)
        ccnt = {e: 0 for e in engs}
        dcnt = {e: 0 for e in dsem}
        dval = {}
        cccount = 0
        for i, o in enumerate(ops):
            e = o['eng']
            if o['cc']:
                cccount += 1
                o['sem'] = ccsem
                o['val'] = cccount
            elif o['dma']:
                k = dcnt[e] % NDMASEM
                dcnt[e] += 1
                s = dsem[e][k]
                prev = dval.get((e, k), 0)
                o['sem'] = s
                o['prev'] = prev
                o['val'] = prev + 16
                dval[(e, k)] = prev + 16
            elif o['fn'] is not None and i in used:
                ccnt[e] += 1
                o['sem'] = csem[e]
                o['val'] = ccnt[e]
            else:
                o['sem'] = None
        per = {e: [] for e in engs}
        for i, o in enumerate(ops):
            per[o['eng']].append(i)

        def run(e, eng):
            waited = {}

            def wait(sem, val):
                key = sem.num if hasattr(sem, 'num') else id(sem)
                if waited.get(key, 0) >= val:
                    return
                eng.wait_ge(sem, val)
                waited[key] = val

            for i in per[e]:
                o = ops[i]
                for d in sorted(o['deps']):
                    p = ops[d]
                    if p['sem'] is None:
                        continue
                    if p['eng'] == e and e == 'pe' and not p['dma'] and not p['cc']:
                        continue
                    wait(p['sem'], p['val'])
                if o['dma'] and not o['cc'] and o['prev'] > 0:
                    wait(o['sem'], o['prev'])
                if o['fn'] is None:
                    continue
                ins = o['fn'](eng)
                if o['sem'] is not None:
                    if o['cc']:
                        ins.then_inc(o['sem'], 1)
                    elif o['dma']:
                        ins.then_inc(o['sem'], 16)
                    else:
                        ins.then_inc(o['sem'], 1)

        @block.tensor
        def _(eng):
            run('pe', eng)

        @block.scalar
        def _(eng):
            run('act', eng)

        @block.vector
        def _(eng):
            run('dve', eng)

        @block.gpsimd
        def _(eng):
            run('pool', eng)

        @block.sync
        def _(eng):
            run('sp', eng)


class Arena:
    def __init__(self, ap_f32, nwords):
        self.ap = ap_f32
        self.n = nwords
        self.top = 0
        self.hi = 0

    def mark(self):
        return self.top

    def release(self, m):
        self.top = m

    def f32(self, cols, parts=128):
        a = self.top
        self.top += cols
        self.hi = max(self.hi, self.top)
        assert self.top <= self.n, ("SBUF arena overflow", self.top, self.n)
        return self.ap[0:parts, a:a + cols]

    def bf16(self, cols, parts=128):
        w = (cols + 1) // 2
        a = self.top
        self.top += w
        self.hi = max(self.hi, self.top)
        assert self.top <= self.n, ("SBUF arena overflow", self.top, self.n)
        return self.ap[0:parts, a:a + w].bitcast(BF16)[:, 0:cols]


class _Done(Exception):
    def __init__(self, nc):
        self.nc = nc


def build(dbg=None, stage=None):
    try:
        return build_nc(dbg, stage)
    except _Done as d:
        return d.nc


def build_nc(dbg=None, stage=None):
    nc = bass.Bass("TRN2", target_bir_lowering=False)
    es = ExitStack()
    tr = Tr()

    def stage_end(name):
        if stage == name:
            blk_ = es.enter_context(nc.Block())
            tr.emit(nc, es, blk_, list(tr.dbg_ops))
            es.close()
            raise _Done(nc)

    def din(name, shape, dt=F32):
        return nc.dram_tensor(name, list(shape), dt, kind="ExternalInput").ap()

    x_d = din("x", [T, D])
    xown_d = din("xown", [TO, D])
    cb_d = din("cb", [128, 16])
    wada_d = din("wada", [D, 6 * D])
    bada_d = din("bada", [6 * D])
    gvec_d = din("gvec", [3, D])
    win_d = din("win", [128, 16 * NWIN])
    pcol_d = din("pcol", [128, 80])
    rowv_d = din("rowv", [2, 512])
    rk_d = din("rkm", [128, 8])
    wa2_d = din("wa2", [128, 512])
    g2_d = din("g2", [160, 512])
    lruw_d = din("lruw", [2, 2, 256, 256])
    wout_d = din("wout", [128, 4 * 8192])
    wgu_d = din("wgu", [128, 44 * 4096])
    wdown_d = din("wdown", [128, 11 * 8192])
    ident_d = din("ident", [128, 128])
    masks_d = din("masks", [128, 4, 128])
    hsel_d = din("hsel", [128, 2])
    out_d = nc.dram_tensor("out", [TO, D], F32, kind="ExternalOutput").ap()
    ysrc_l = [nc.dram_tensor("ysrc%d" % i, [1024, 512], BF16, kind="Internal").ap() for i in range(4)]
    ydst_l = [nc.dram_tensor("ydst%d" % i, [2048, 512], BF16, kind="Internal").ap() for i in range(4)]
    dbg_d = {}
    if dbg:
        for nm, shp in dbg.items():
            dbg_d[nm] = nc.dram_tensor("dbg_" + nm, list(shp), F32, kind="ExternalOutput").ap()

    NW = 53000
    arena_t = es.enter_context(nc.sbuf_tensor("arena", [128, NW], F32))
    ar = Arena(arena_t, NW)
    ps = [es.enter_context(nc.psum_tensor("ps%d" % i, [128, 512], F32)) for i in range(8)]

    dmaq = ['sp', 'act']
    dctr = [0]

    def dma(out, in_, r=(), w=(), q=None, **kw):
        if q is None:
            q = dmaq[dctr[0] % len(dmaq)]
            dctr[0] += 1
        return tr.op(q, lambda e: e.dma_start(out=out, in_=in_, **kw), r, w, dma=True)

    def mm(out, lhsT, rhs, start, stop, r=(), w=()):
        return tr.op('pe', lambda e: e.matmul(out, lhsT, rhs, start=start, stop=stop), r, w)

    def tp(out, in_, idn, r=(), w=()):
        return tr.op('pe', lambda e: e.transpose(out, in_, idn), r, w)

    def act(out, in_, func, r=(), w=(), bias=None, scale=None, accum_out=None):
        kw = {}
        if bias is not None:
            kw['bias'] = bias
        if scale is not None:
            kw['scale'] = scale
        if accum_out is not None:
            kw['accum_out'] = accum_out
        return tr.op('act', lambda e: e.activation(out, in_, func, **kw), r, w)

    def ts(eng, out, in0, s1, s2, op0, op1, r=(), w=()):
        if op1 is None:
            return tr.op(eng, lambda e: e.tensor_scalar(out, in0, s1, None, op0), r, w)
        return tr.op(eng, lambda e: e.tensor_scalar(out, in0, s1, s2, op0, op1), r, w)

    def tt(eng, out, in0, in1, op, r=(), w=()):
        return tr.op(eng, lambda e: e.tensor_tensor(out, in0, in1, op), r, w)

    def stt(out, in0, sc, in1, op0, op1, r=(), w=()):
        return tr.op('dve', lambda e: e.scalar_tensor_tensor(out, in0, sc, in1, op0, op1), r, w)

    def cp(eng, out, in_, r=(), w=()):
        if eng == 'act':
            return tr.op('act', lambda e: e.activation(out, in_, AF.Copy), r, w)
        return tr.op(eng, lambda e: e.tensor_copy(out, in_), r, w)

    def memset(eng, ap, val, w=()):
        return tr.op(eng, lambda e: e.memset(ap, val), (), w)

    def dump(name, ap, r):
        if name in dbg_d:
            i_ = dma(dbg_d[name], ap, r=r, w=[('dbgout', name)], q='sp')
            tr.dbg_ops.append(i_)
            return i_
        return None

    ident = ar.f32(128)
    masks = ar.f32(512)
    pcol = ar.f32(80)
    rkm = ar.f32(8)
    hsel = ar.f32(2)
    ones = ar.f32(128)
    epsc = ar.f32(4)
    wa2 = ar.f32(512)
    g2a = ar.f32(512)
    g2b = ar.f32(512, 32)
    rowv = ar.f32(1024)
    cb = ar.f32(16)
    identb = ar.bf16(128)
    identb2 = identb
    dma(ident, ident_d, w=['ident'])
    dma(masks, masks_d.rearrange("p a b -> p (a b)"), w=['masks'])
    dma(pcol, pcol_d, w=['pcol'])
    dma(rkm, rk_d, w=['rkm'])
    dma(hsel, hsel_d, w=['hsel'])
    dma(wa2, wa2_d, w=['wa2'])
    dma(g2a, g2_d[0:128, :], w=['g2'])
    dma(g2b, g2_d[128:160, :], w=['g2'])
    dma(rowv, rowv_d.rearrange("a n -> (a n)").partition_broadcast(128), w=['rowv'])
    dma(cb, cb_d, w=['cb'])
    memset('pool', ones, 1.0, w=['ones'])
    memset('pool', epsc[:, 0:1], RMS_EPS, w=['epsc'])
    memset('pool', epsc[:, 1:2], GN_EPS, w=['epsc'])
    memset('pool', epsc[:, 2:3], 0.0, w=['epsc'])
    MUS = masks[:, 0:128]
    MUI = masks[:, 128:256]
    MLS = masks[:, 256:384]
    BLK = masks[:, 384:512]
    lng = rowv[:, 0:512]
    lnb = rowv[:, 512:1024]

    def pc(i):
        return pcol[:, i:i + 1]

    dcol = ar.f32(64)
    mu_idx = [8 * j + q for j in range(4) for q in range(3)] + [32, 33, 34]
    for n, mi in enumerate(mu_idx):
        ts('pool', dcol[:, n:n + 1], pc(mi), -1.0, 1.0, ALU.mult, ALU.add, r=['pcol'], w=['dcol'])

    def omm(mi):
        return dcol[:, mu_idx.index(mi):mu_idx.index(mi) + 1]

    for j in range(4):
        ts('pool', dcol[:, 16 + j:17 + j], pc(8 * j + 4), -1.0, 1.0, ALU.mult, ALU.add, r=['pcol'], w=['dcol'])
        act(dcol[:, 32 + j:33 + j], pc(40 + 8 * j + 7), AF.Exp, r=['pcol'], w=['dcol'], scale=-1.0)
        act(dcol[:, 36 + j:37 + j], dcol[:, 32 + j:33 + j], AF.Ln, r=['dcol'], w=['dcol'], bias=1.0)
        ts('pool', dcol[:, 24 + j:25 + j], dcol[:, 36 + j:37 + j], -8.0, None, ALU.mult, None, r=['dcol'], w=['dcol'])
        ts('pool', dcol[:, 28 + j:29 + j], dcol[:, 36 + j:37 + j], -16.0, None, ALU.mult, None, r=['dcol'], w=['dcol'])

    cact = ar.f32(16)
    act(cact, cb, AF.Silu, r=['cb'], w=['cact'])
    cbc = ar.f32(16 * 128)
    cp('dve', cbc.rearrange("p (k m) -> p k m", k=16), cact.unsqueeze(2).to_broadcast([128, 16, 128]),
       r=['cact'], w=['cbc'])
    m_phase = ar.mark()
    MB = {}

    def alloc_mb():
        MB['bad'] = ar.f32(2048)
        MB['gv'] = ar.f32(2048)
        MB['wst'] = [ar.f32(2 * 2048), ar.f32(2 * 2048)]
    wada_v = wada_d.rearrange("(p k) n -> p k n", k=16)

    def mod_chunk(jm, evac):
        nst = 0
        for kq in range(8):
            b = MB['wst'][nst % 2]
            nst += 1
            dma(b.rearrange("p (k n) -> p k n", k=2), wada_v[:, kq * 2:(kq + 1) * 2, jm * 2048:(jm + 1) * 2048],
                w=[('wst', id(b))])
            for kk in range(2):
                k = kq * 2 + kk
                for q in range(4):
                    mm(ps[q][:, :], cbc[:, k * 128:(k + 1) * 128], b[:, kk * 2048 + q * 512: kk * 2048 + (q + 1) * 512],
                       k == 0, k == 15, r=['cbc', ('wst', id(b))], w=[('ps', q)])
        for q in range(4):
            evac(q, ps[q][:, :])

    def mod_vec(jm, dst, gidx=None):
        bad = MB['bad']
        gv = MB['gv']
        dma(bad, bada_d[jm * 2048:(jm + 1) * 2048].partition_broadcast(128), w=['bad'])
        if gidx is not None:
            dma(gv, gvec_d[gidx, :].partition_broadcast(128), w=['gv'])

        def ev(q, p):
            sl = slice(q * 512, (q + 1) * 512)
            tt('dve', dst[:, sl], p, bad[:, sl], ALU.add, r=[('ps', q), 'bad'], w=[('vec', id(dst))])
            if gidx is not None:
                stt(dst[:, sl], dst[:, sl], 1.0, gv[:, sl], ALU.add, ALU.mult, r=[('vec', id(dst)), 'gv'],
                    w=[('vec', id(dst))])
        mod_chunk(jm, ev)

    hT = ar.f32(16 * 2048 // 2)
    hTb = hT.bitcast(BF16).rearrange("p (k t) -> p k t", k=16)
    p1 = ar.mark()
    alloc_mb()
    gm1 = ar.f32(2048)
    shm = ar.f32(2048)
    mod_vec(1, gm1, gidx=0)
    mod_vec(0, shm)
    xs = [ar.f32(2048), ar.f32(2048)]
    hf = ar.f32(2048)
    hb = ar.bf16(2048)
    junk = ar.bf16(2048)
    ssq = ar.f32(16)
    rst = ar.f32(16)
    cp('dve', identb, ident, r=['ident'], w=['identb', 'identb2'])
    for tt_i in range(16):
        xb = xs[tt_i % 2]
        xt = ('xs', tt_i % 2)
        dma(xb, x_d[tt_i * 128:(tt_i + 1) * 128, :], w=[xt], q='sp')
        act(junk, xb, AF.Square, r=[xt], w=['junk', ('ssq', tt_i)], accum_out=ssq[:, tt_i:tt_i + 1])
        act(rst[:, tt_i:tt_i + 1], ssq[:, tt_i:tt_i + 1], AF.Sqrt, r=[('ssq', tt_i), 'epsc'], w=[('rst', tt_i)],
            bias=epsc[:, 0:1], scale=1.0 / D)
        tr.op('dve', lambda e, a=rst[:, tt_i:tt_i + 1]: e.reciprocal(a, a), [('rst', tt_i)], [('rst', tt_i)])
        stt(hf, xb, rst[:, tt_i:tt_i + 1], gm1, ALU.mult, ALU.mult, r=[xt, ('rst', tt_i), ('vec', id(gm1))], w=['hf'])
        tt('pool', hb, hf, shm, ALU.add, r=['hf', ('vec', id(shm))], w=['hb'])
        for half in range(2):
            pb = ps[4 + (tt_i * 2 + half) % 4]
            pt = ('ps', 4 + (tt_i * 2 + half) % 4)
            pbv = pb[:, :].bitcast(BF16)
            for q in range(8):
                dc = half * 8 + q
                tp(pbv[:, q * 128:(q + 1) * 128], hb[:, dc * 128:(dc + 1) * 128], identb, r=['hb', 'identb'], w=[pt])
            cp('act' if half == 0 else 'dve', hTb[:, half * 8:(half + 1) * 8, tt_i * 128:(tt_i + 1) * 128],
               pbv.rearrange("p (q t) -> p q t", q=8), r=[pt], w=[('hT', tt_i)])
    hT_all = [('hT', i) for i in range(16)]
    if 'hT' in dbg_d:
        tmpd = ar.f32(2048)
        cp('dve', tmpd, hTb[:, 3, :], r=hT_all, w=['tmpd'])
        dump('hT', tmpd, ['tmpd'])
    dump('gm1', gm1, [('vec', id(gm1))])
    stage_end('p1')
    tr.fence()
    ar.release(p1)

    wbufs = [ar.bf16(16 * 512), ar.bf16(16 * 512)]
    wctr = [0]

    def load_flat(dst, src, off, n, tok, piece=2048):
        for a in range(0, n, piece):
            b_ = min(n, a + piece)
            dma(dst[:, a:b_], src[:, off + a: off + b_], w=[tok], q='pool', max_dma_last_dim=8192)

    def load_w(c0, ncols):
        b = wbufs[wctr[0] % 2]
        wctr[0] += 1
        bv = b[:, 0:16 * ncols].rearrange("p (k n) -> p k n", k=16)
        tok = ('wbuf', id(b))
        load_flat(b, win_d, 16 * c0, 16 * ncols, tok)
        return bv, tok

    psrr = [0]

    def proj(bv, tok, cofs, ncol, tg, tlen=512):
        bi = psrr[0] % 4
        psrr[0] += 1
        p = ps[bi]
        for k in range(16):
            mm(p[0:ncol, 0:tlen], bv[:, k, cofs:cofs + ncol], hTb[:, k, tg * 512: tg * 512 + tlen], k == 0, k == 15,
               r=[tok] + hT_all, w=[('ps', bi)])
        return p[0:ncol, 0:tlen], ('ps', bi)

    pmu = ar.f32(516)
    memset('pool', pmu[:, 0:1], 0.0, w=['pmu'])

    def lerp_evac(p, ptok, nrow, mui, dst, dtok, tg):
        if tg > 0:
            cp('pool', pmu[0:nrow, 0:1], pmu[0:nrow, 512:513], r=['pmu'], w=['pmu'])
        else:
            memset('pool', pmu[0:nrow, 0:1], 0.0, w=['pmu'])
        act(pmu[0:nrow, 1:513], p, AF.Copy, r=[ptok, 'pcol'], w=['pmu'], scale=pc(mui)[0:nrow, :])
        stt(dst[0:nrow, tg * 512:(tg + 1) * 512], p, omm(mui)[0:nrow, :], pmu[0:nrow, 0:512], ALU.mult, ALU.add,
            r=[ptok, 'pmu', 'dcol'], w=[dtok])

    lwin = ar.f32(2048)
    glB = ar.f32(2048)
    glC = ar.f32(2048, 32)
    bv, tok = load_w(0, 288)
    for tg in range(4):
        p, ptk = proj(bv, tok, 0, 128, tg)
        lerp_evac(p, ptk, 128, 32, lwin, ('lwin', tg), tg)
        act(lwin[0:64, tg * 512:(tg + 1) * 512], lwin[0:64, tg * 512:(tg + 1) * 512], AF.Tanh, r=[('lwin', tg)],
            w=[('lwin', tg)])
    for tg in range(4):
        p, ptk = proj(bv, tok, 128, 128, tg)
        lerp_evac(p, ptk, 128, 33, glB, ('glB', tg), tg)
        act(glB[:, tg * 512:(tg + 1) * 512], glB[:, tg * 512:(tg + 1) * 512], AF.Sigmoid, r=[('glB', tg)],
            w=[('glB', tg)])
    for tg in range(4):
        p, ptk = proj(bv, tok, 256, 32, tg)
        lerp_evac(p, ptk, 32, 34, glC, ('glC', tg), tg)
        act(glC[:, tg * 512:(tg + 1) * 512], glC[:, tg * 512:(tg + 1) * 512], AF.Sigmoid, r=[('glC', tg)],
            w=[('glC', tg)])
    dump('lwin', lwin, [('lwin', g) for g in range(4)])
    dump('glB', glB, [('glB', g) for g in range(4)])
    stage_end('lora')

    def pq(bank, q0, nq=1, rows=128):
        return ps[bank][0:rows, q0 * 128:(q0 + nq) * 128], [('ps', bank)]

    def proj_prev(bv, tok, cofs, ncol, tcol, width, bank, col):
        for k in range(16):
            mm(ps[bank][0:ncol, col:col + width], bv[:, k, cofs:cofs + ncol], hTb[:, k, tcol:tcol + width], k == 0,
               k == 15, r=[tok] + hT_all, w=[('ps', bank)])
        return ps[bank][0:ncol, col:col + width], ('ps', bank)

    mask4 = ar.f32(512)
    cp('pool', mask4[:, 0:128], MUS, r=['masks'], w=['mask4'])
    cp('pool', mask4[:, 128:256], MUI, r=['masks'], w=['mask4'])
    cp('pool', mask4[:, 256:384], MUS, r=['masks'], w=['mask4'])
    cp('pool', mask4[:, 384:512], MUI, r=['masks'], w=['mask4'])

    ydma = []
    mrw = ar.mark()
    Hst = [[ar.f32(64), ar.f32(64)], [ar.f32(64), ar.f32(64)]]
    btp = [ar.f32(512), ar.f32(512)]
    ktp = [ar.f32(512), ar.f32(512)]
    for z_ in btp + ktp:
        memset('pool', z_, 0.0, w=[('pad', id(z_))])
    hcnt = [0]
    pqs = [[ar.f32(384), ar.f32(384)] for _ in range(4)]
    amat = [ar.f32(512) for _ in range(4)]
    trn = ar.f32(384)
    Xs = ar.f32(128)
    Us = ar.f32(128)
    gtok = ar.f32(512)
    s_sb = ar.f32(8)
    ybT = ar.bf16(512)
    yb_ = ar.f32(128)
    yc_ = ar.f32(128)
    sq_ = ar.f32(128)
    st_ = ar.f32(8)
    yfb = ar.bf16(128)
    dec = ar.f32(4)
    A = {}
    for nm in ['r', 'k', 'v', 'sg', 'L', 'a', 'kk', 'EL', 'tmp']:
        A[nm] = ar.f32(512)
    tk = lambda nm: ('rw', nm)
    for j in range(4):
        bv, tok = load_w(288 + 384 * j, 384)
        for par_ in range(2):
            for hd_ in range(2):
                memset('pool', Hst[par_][hd_], 0.0, w=[('H', par_)])
        for tb in range(4):
            tsl = slice(tb * 512, (tb + 1) * 512)
            for q, nm in enumerate(['r', 'k', 'v']):
                p, ptk = proj(bv, tok, 128 * q, 128, tb)
                if tb > 0:
                    pp, pptk = proj_prev(bv, tok, 128 * q, 128, tb * 512 - 1, 1, 7, 0)
                    act(pmu[:, 0:1], pp, AF.Copy, r=[pptk, 'pcol'], w=['pmu'], scale=pc(8 * j + q))
                else:
                    memset('pool', pmu[:, 0:1], 0.0, w=['pmu'])
                act(pmu[:, 1:513], p, AF.Copy, r=[ptk, 'pcol'], w=['pmu'], scale=pc(8 * j + q))
                stt(A[nm], p, omm(8 * j + q), pmu[:, 0:512], ALU.mult, ALU.add, r=[ptk, 'pmu', 'dcol'], w=[tk(nm)])
            pw, pwt = pq(4, 0, 4)
            mm(pw, wa2[0:64, j * 128:(j + 1) * 128], lwin[0:64, tsl], True, True, r=['wa2', ('lwin', tb)], w=pwt)
            act(A['sg'], pw, AF.Sigmoid, r=pwt + ['pcol'], w=[tk('sg')], bias=pc(8 * j + 5))
            pa, pat = pq(5, 0, 4)
            mm(pa, wa2[64:128, j * 128:(j + 1) * 128], lwin[64:128, tsl], True, True, r=['wa2', ('lwin', tb)], w=pat)
            act(A['a'], pa, AF.Sigmoid, r=pat + ['pcol'], w=[tk('a')], bias=pc(8 * j + 6))
            for c in range(4):
                cs = slice(c * 128, (c + 1) * 128)
                tr.op('dve', lambda e, o=A['L'][:, cs], d1=A['sg'][:, cs]: e.tensor_tensor_scan(
                    o, ones, d1, 0.0, ALU.mult, ALU.add), [tk('sg'), 'ones'], [tk('L')])
            tt('pool', A['tmp'], A['L'], A['sg'], ALU.subtract, r=[tk('L'), tk('sg')], w=[tk('tmp')])
            act(A['tmp'], A['tmp'], AF.Exp, r=[tk('tmp')], w=[tk('tmp')], scale=-C0)
            act(A['EL'], A['L'], AF.Exp, r=[tk('L')], w=[tk('EL')], scale=-C0)
            act(A['L'], A['L'], AF.Exp, r=[tk('L')], w=[tk('L')], scale=C0)
            ts('pool', A['kk'], A['k'], pc(8 * j + 3), None, ALU.mult, None, r=[tk('k'), 'pcol'], w=[tk('kk')])
            act(A['sg'], A['kk'], AF.Square, r=[tk('kk')], w=[tk('sg')])
            pn, pnt = pq(6, 0, 4)
            mm(pn, BLK, A['sg'], True, True, r=['masks', tk('sg')], w=pnt)
            act(A['sg'], pn, AF.Sqrt, r=pnt, w=[tk('sg')])
            ts('dve', A['sg'], A['sg'], L2_EPS, None, ALU.max, None, r=[tk('sg')], w=[tk('sg')])
            tr.op('dve', lambda e, a_=A['sg']: e.reciprocal(a_, a_), [tk('sg')], [tk('sg')])
            tt('dve', A['kk'], A['kk'], A['sg'], ALU.mult, r=[tk('kk'), tk('sg')], w=[tk('kk')])
            ts('dve', A['sg'], A['a'], pc(8 * j + 4), dcol[:, 16 + j:17 + j], ALU.mult, ALU.add,
               r=[tk('a'), 'pcol', 'dcol'], w=[tk('sg')])
            tt('dve', A['k'], A['k'], A['sg'], ALU.mult, r=[tk('k'), tk('sg')], w=[tk('k')])
            tt('pool', A['sg'], A['r'], A['k'], ALU.mult, r=[tk('r'), tk('k')], w=[tk('sg')])
            psS, psSt = pq(7, 1, 1)
            for c in range(4):
                mm(psS[:, 2 * c:2 * c + 2], A['sg'][:, c * 128:(c + 1) * 128], rkm[:, 2 * j:2 * j + 2], True, True,
                   r=[tk('sg'), 'rkm'], w=psSt)
            cp('act', s_sb, psS[:, 0:8], r=psSt, w=['s_sb'])
            tt('pool', A['a'], A['kk'], A['a'], ALU.mult, r=[tk('kk'), tk('a')], w=[tk('a')])
            tt('dve', A['r'], A['r'], A['EL'], ALU.mult, r=[tk('r'), tk('EL')], w=[tk('r')])
            tt('pool', A['k'], A['k'], A['L'], ALU.mult, r=[tk('k'), tk('L')], w=[tk('k')])
            tt('dve', A['a'], A['a'], A['L'], ALU.mult, r=[tk('a'), tk('L')], w=[tk('a')])
            stt(A['kk'], A['kk'], -1.0, A['tmp'], ALU.mult, ALU.mult, r=[tk('kk'), tk('tmp')], w=[tk('kk')])
            cp('pool', dec, A['EL'].rearrange("p (c t) -> p c t", c=4)[:, :, 127], r=[tk('EL')], w=['dec'])
            decb = dec.unsqueeze(2).to_broadcast([128, 4, 128])
            tt('dve', A['L'].rearrange("p (c t) -> p c t", c=4), A['a'].rearrange("p (c t) -> p c t", c=4), decb,
               ALU.mult, r=[tk('a'), 'dec', tk('L')], w=[tk('L')])
            tt('pool', A['tmp'].rearrange("p (c t) -> p c t", c=4), A['k'].rearrange("p (c t) -> p c t", c=4), decb,
               ALU.mult, r=[tk('k'), 'dec', tk('tmp')], w=[tk('tmp')])
            rt_, kt_, bt_, at_, Bd_, Kd_ = A['r'], A['k'], A['a'], A['kk'], A['L'], A['tmp']
            for hd_ in range(2):
                hs_ = slice(64 * hd_, 64 * hd_ + 64)
                cp('pool', btp[hd_][hs_, :], bt_[hs_, :], r=[tk('a')], w=[('pad', id(btp[hd_]))])
                cp('act', ktp[hd_][hs_, :], kt_[hs_, :], r=[tk('k')], w=[('pad', id(ktp[hd_]))])
            if j == 0 and tb < 2:
                for nm_, ap_, t_ in [('rt', rt_, 'r'), ('kt', kt_, 'k'), ('bt', bt_, 'a'), ('at', at_, 'kk'),
                                     ('Bd', Bd_, 'L'), ('Kd', Kd_, 'tmp'), ('vv', A['v'], 'v')]:
                    dump('%s%d' % (nm_, tb), ap_, [tk(t_)])
            if j == 0 and tb == 0:
                stage_end('rwA')
            for c in range(4):
                cs = slice(tb * 512 + c * 128, tb * 512 + (c + 1) * 128)
                pg, pgt = pq(4, c, 1)
                mm(pg, glB[:, cs], g2a[:, j * 128:(j + 1) * 128], True, False, r=[('glB', tb), 'g2'], w=pgt)
                mm(pg, glC[0:32, cs], g2b[0:32, j * 128:(j + 1) * 128], False, True, r=[('glC', tb), 'g2'], w=pgt)
            pg4, pg4t = pq(4, 0, 4)
            cp('act', gtok, pg4, r=pg4t, w=['gtok'])
            if j == 0 and tb == 0:
                stage_end('rwA1')
            for c in range(4):
                cs = slice(c * 128, (c + 1) * 128)
                ptr_, ptrt = pq(4, 0, 3)
                tp(ptr_[:, 0:128], A['v'][:, cs], ident, r=[tk('v'), 'ident'], w=ptrt)
                tp(ptr_[:, 128:256], Bd_[:, cs], ident, r=[tk('L'), 'ident'], w=ptrt)
                tp(ptr_[:, 256:384], Kd_[:, cs], ident, r=[tk('tmp'), 'ident'], w=ptrt)
                cp('act', trn, ptr_, r=ptrt, w=['trn'])
                vtok, BdT, KdT = trn[:, 0:128], trn[:, 128:256], trn[:, 256:384]
                if j == 0 and tb == 0 and c == 0:
                    dump('trn', trn, ['trn'])
                    stage_end('rwA2')
                if c % 2 == 0:
                    probs = [(cc_, hd_) for cc_ in (c, c + 1) for hd_ in range(2)]
                    PB = [5, 6, 0, 1]
                    for pi, (cc_, hd) in enumerate(probs):
                        ccs = slice(cc_ * 128, (cc_ + 1) * 128)
                        pA, pAt = pq(PB[pi], 0, 4)
                        bpt, kpt = ('pad', id(btp[hd])), ('pad', id(ktp[hd]))
                        mm(pA[:, 0:128], btp[hd][:, ccs], at_[:, ccs], True, True, r=[bpt, tk('kk')], w=pAt)
                        mm(pA[:, 128:256], btp[hd][:, ccs], rt_[:, ccs], True, True, r=[bpt, tk('r')], w=pAt)
                        mm(pA[:, 256:384], ktp[hd][:, ccs], at_[:, ccs], True, True, r=[kpt, tk('kk')], w=pAt)
                        mm(pA[:, 384:512], ktp[hd][:, ccs], rt_[:, ccs], True, True, r=[kpt, tk('r')], w=pAt)
                    pQa, pQt = pq(2, 0, 4)
                    for pi, (cc_, hd) in enumerate(probs):
                        ccs = slice(cc_ * 128, (cc_ + 1) * 128)
                        mm(pQa[:, pi * 128:(pi + 1) * 128], at_[:, ccs], btp[hd][:, ccs], True, True,
                           r=[('pad', id(btp[hd])), tk('kk')], w=pQt)
                    for pi in range(4):
                        pA, pAt = pq(PB[pi], 0, 4)
                        tt('dve', amat[pi], pA, mask4, ALU.mult, r=pAt + ['mask4'], w=[('amat', pi)])
                    for pi in range(4):
                        b0 = pqs[pi][0]
                        cp('act', b0[:, 0:128], amat[pi][:, 0:128], r=[('amat', pi)], w=[('pqs', pi, 0)])
                        tt('dve', b0[:, 128:256], pQa[:, pi * 128:(pi + 1) * 128], MLS, ALU.mult, r=pQt + ['masks'],
                           w=[('pqs', pi, 0)])
                        tt('pool', b0[:, 256:384], amat[pi][:, 0:128], ident, ALU.add, r=[('amat', pi), 'ident'],
                           w=[('pqs', pi, 0)])
                    for it in range(6):
                        for pi in range(4):
                            cur = pqs[pi][it % 2]
                            ct = ('pqs', pi, it % 2)
                            pI, pIt = pq(PB[pi], 0, 3)
                            mm(pI[:, 0:128], cur[:, 0:128], cur[:, 128:256], True, True, r=[ct], w=pIt)
                            if it < 5:
                                mm(pI[:, 128:256], cur[:, 128:256], cur[:, 0:128], True, True, r=[ct], w=pIt)
                        for pi in range(4):
                            nxt = pqs[pi][(it + 1) % 2]
                            nt = ('pqs', pi, (it + 1) % 2)
                            pI, pIt = pq(PB[pi], 0, 3)
                            n2 = 256 if it < 5 else 128
                            if it < 5:
                                cp('act', nxt[:, 128:256], pI[:, 0:128], r=pIt, w=[nt])
                                cp('act', nxt[:, 0:128], pI[:, 128:256], r=pIt, w=[nt])
                            else:
                                cp('act', nxt[:, 128:256], pI[:, 0:128], r=pIt, w=[nt])
                        for pi in range(4):
                            cur, nxt = pqs[pi][it % 2], pqs[pi][(it + 1) % 2]
                            ct, nt = ('pqs', pi, it % 2), ('pqs', pi, (it + 1) % 2)
                            pI, pIt = pq(PB[pi], 0, 3)
                            mm(pI[:, 256:384], nxt[:, 128:256], cur[:, 256:384], True, True, r=[ct, nt], w=pIt)
                        for pi in range(4):
                            cur, nxt = pqs[pi][it % 2], pqs[pi][(it + 1) % 2]
                            ct, nt = ('pqs', pi, it % 2), ('pqs', pi, (it + 1) % 2)
                            pI, pIt = pq(PB[pi], 0, 3)
                            tt('dve', nxt[:, 256:384], pI[:, 256:384], cur[:, 256:384], ALU.add, r=pIt + [ct], w=[nt])
                pi0 = 2 * (c % 2)
                if j == 0 and tb == 0 and c < 2:
                    dump('amat%d' % c, amat[pi0], [('amat', pi0)])
                    dump('Tt%d' % c, pqs[pi0][0][:, 256:384], [('pqs', pi0, 0)])
                if j == 0 and tb == 0 and c == 0:
                    stage_end('rwB')
                Tt = [pqs[pi0 + hd][0][:, 256:384] for hd in range(2)]
                Ttt = [('pqs', pi0 + hd, 0) for hd in range(2)]
                Hc = Hst[hcnt[0] % 2]
                Hn = Hst[(hcnt[0] + 1) % 2]
                Hcp = Hc
                Hct, Hnt = ('H', hcnt[0] % 2), ('H', (hcnt[0] + 1) % 2)
                hcnt[0] += 1
                pX, pXt = pq(4, 3, 1)
                for hd in range(2):
                    hs = slice(64 * hd, 64 * hd + 64)
                    vs = slice(64 * hd, 64 * hd + 64)
                    mm(pX[:, vs], at_[:, cs], Hc[hd], True, False, r=[tk('kk'), Hct], w=pXt)
                    mm(pX[:, vs], amat[pi0 + hd][:, 256:384], vtok[:, vs], False, True, r=[('amat', pi0 + hd), 'trn'], w=pXt)
                cp('act', Xs, pX, r=pXt, w=['Xs'])
                pU, pUt = pq(7, 0, 1)
                for hd in range(2):
                    vs = slice(64 * hd, 64 * hd + 64)
                    mm(pU[:, vs], Tt[hd], Xs[:, vs], True, True, r=[Ttt[hd], 'Xs'], w=pUt)
                cp('act', Us, pU, r=pUt, w=['Us'])
                if j == 0 and tb == 0 and c == 0:
                    dump('Xs0', Xs, ['Xs'])
                    dump('Us0', Us, ['Us'])
                    stage_end('rwB1')
                pY, pYt = pq(4, 0, 1)
                for hd in range(2):
                    hs = slice(64 * hd, 64 * hd + 64)
                    vs = slice(64 * hd, 64 * hd + 64)
                    mm(pY[:, vs], rt_[:, cs], Hc[hd], True, False, r=[tk('r'), Hct], w=pYt)
                    mm(pY[:, vs], amat[pi0 + hd][:, 128:256], Us[:, vs], False, False, r=[('amat', pi0 + hd), 'Us'], w=pYt)
                    mm(pY[:, vs], amat[pi0 + hd][:, 384:512], vtok[:, vs], False, True, r=[('amat', pi0 + hd), 'trn'], w=pYt)
                pH, pHt = pq(4, 1, 1)
                mm(pH, BdT, Us, True, False, r=['trn', 'Us'], w=pHt)
                mm(pH, KdT, vtok, False, True, r=['trn'], w=pHt)
                for hd in range(2):
                    hs = slice(64 * hd, 64 * hd + 64)
                    stt(Hn[hd][hs, :], Hc[hd][hs, :], dec[hs, c:c + 1], pH[hs, 64 * hd:64 * hd + 64], ALU.mult, ALU.add,
                        r=[Hct, 'dec'] + pHt, w=[Hnt])
                if j == 0 and tb == 0 and c == 0:
                    dump('Hn0', Hn[0], [Hnt])
                    dump('Hm0', Hn[1], [Hnt])
                    stage_end('rwB2')
                cp('act', yb_, pY, r=pYt, w=['yb'])
                if j == 0 and tb == 0 and c < 2:
                    dump('Xs%d' % c, Xs, ['Xs'])
                    dump('Us%d' % c, Us, ['Us'])
                    dump('yb%d' % c, yb_, ['yb'])
                    dump('Hn%d' % c, Hn[0], [Hnt])
                    dump('Hm%d' % c, Hn[1], [Hnt])
                y3 = yb_.rearrange("p (h v) -> p h v", h=2)
                if os.environ.get('KDBG_RED'):
                    tr.op('dve', lambda e, o=st_[:, 0:2], i_=y3: e.tensor_reduce(o, i_, AX.X, ALU.add), ['yb'], ['st'])
                else:
                    for hd_ in range(2):
                        act(sq_[:, 64 * hd_:64 * hd_ + 64], yb_[:, 64 * hd_:64 * hd_ + 64], AF.Copy, r=['yb'],
                            w=['sq', 'st'], accum_out=st_[:, hd_:hd_ + 1])
                ts('dve', st_[:, 0:2], st_[:, 0:2], -1.0 / 64, None, ALU.mult, None, r=['st'], w=['st'])
                tt('dve', yc_.rearrange("p (h v) -> p h v", h=2), y3, st_[:, 0:2].unsqueeze(2).to_broadcast([128, 2, 64]),
                   ALU.add, r=['yb', 'st'], w=['yc'])
                if os.environ.get('KDBG_RED'):
                    act(sq_, yc_, AF.Square, r=['yc'], w=['sq'])
                else:
                    for hd_ in range(2):
                        act(sq_[:, 64 * hd_:64 * hd_ + 64], yc_[:, 64 * hd_:64 * hd_ + 64], AF.Square, r=['yc'],
                            w=['sq', 'st'], accum_out=st_[:, 2 + hd_:3 + hd_])
                if os.environ.get('KDBG_RED'):
                    tr.op('dve', lambda e, o=st_[:, 2:4], i_=sq_.rearrange("p (h v) -> p h v", h=2): e.tensor_reduce(
                        o, i_, AX.X, ALU.add), ['sq'], ['st'])
                act(st_[:, 2:4], st_[:, 2:4], AF.Sqrt, r=['st', 'epsc'], w=['st'], bias=epsc[:, 1:2], scale=1.0 / 64)
                tr.op('dve', lambda e, a_=st_[:, 2:4]: e.reciprocal(a_, a_), ['st'], ['st'])
                tt('dve', yc_.rearrange("p (h v) -> p h v", h=2), yc_.rearrange("p (h v) -> p h v", h=2),
                   st_[:, 2:4].unsqueeze(2).to_broadcast([128, 2, 64]), ALU.mult, r=['yc', 'st'], w=['yc'])
                if j == 0 and tb == 0 and c == 0:
                    stage_end('rwB3')
                SK = os.environ.get('KDBG_SKIP', '')
                if '1' not in SK:
                    tt('pool', yc_, yc_, lng[:, j * 128:(j + 1) * 128], ALU.mult, r=['yc', 'rowv'], w=['yc'])
                if '2' not in SK:
                    tt('pool', yc_, yc_, lnb[:, j * 128:(j + 1) * 128], ALU.add, r=['yc', 'rowv'], w=['yc'])
                if '3' not in SK:
                    tt('dve', sq_.rearrange("p (h v) -> p h v", h=2), vtok.rearrange("p (h v) -> p h v", h=2),
                       s_sb[:, 2 * c:2 * c + 2].unsqueeze(2).to_broadcast([128, 2, 64]), ALU.mult, r=['trn', 's_sb'],
                       w=['sq'])
                if '4' not in SK:
                    tt('dve', yc_, yc_, sq_, ALU.add, r=['yc', 'sq'], w=['yc'])
                if j == 0 and tb == 0 and c < 2:
                    dump('ypre%d' % c, yc_, ['yc'])
                if j == 0 and tb == 0 and c == 0:
                    stage_end('rwC')
                tt('dve', yfb, yc_, gtok[:, cs], ALU.mult, r=['yc', 'gtok'], w=['yfb'])
                pT, pTt = pq(7, 1, 1)
                pTb = pT.bitcast(BF16)[:, 0:128]
                tp(pTb, yfb, identb2, r=['yfb', 'identb2'], w=pTt)
                cp('act', ybT[:, cs], pTb, r=pTt, w=['ybT'])
            ydma.append(dma(ysrc_l[tb][512 + j * 128: 512 + (j + 1) * 128, :], ybT, r=['ybT'], w=[('ysrc', 4 + j, tb)]))
            if 'ybT%d' % tb in dbg_d and j == 0:
                tmpd2 = A['sg']
                cp('dve', tmpd2, ybT, r=['ybT'], w=[tk('sg')])
                dump('ybT%d' % tb, tmpd2, [tk('sg')])
            if j == 0 and tb == 1:
                stage_end('rwkv1')
    tr.fence()
    ar.release(mrw)

    mlr = ar.mark()
    lw = ar.f32(1024)
    lwv = lw.rearrange("p (g q n) -> p g q n", g=2, q=2)
    xp = [ar.f32(516), ar.f32(516)]
    uu = [ar.f32(512), ar.f32(512)]
    hh_ = [ar.f32(512), ar.f32(512)]
    hprev = ar.f32(2)
    R_ = ar.f32(512)
    I_ = ar.f32(512)
    A_ = ar.f32(512)
    M_ = ar.f32(512)
    G1 = ar.f32(512)
    G2 = ar.f32(512)
    yaT = ar.bf16(512)
    for h2 in range(2):
        bv, tok = load_w(288 + 1536 + 512 * h2, 512)
        for g in range(2):
            dma(lwv[:, g, :, :], lruw_d[g, h2, :, :].rearrange("(q p) n -> p q n", p=128), w=['lw'])
        for tb in range(4):
            tsl = slice(tb * 512, (tb + 1) * 512)
            for q in range(2):
                jt = 2 * h2 + q
                pb_ = 40 + 8 * jt
                p, ptk = proj(bv, tok, 128 * q, 128, tb)
                cp('act', xp[q][:, 3:515], p, r=[ptk], w=[('xp', q)])
                if tb > 0:
                    pp, pptk = proj_prev(bv, tok, 128 * q, 128, tb * 512 - 3, 3, 7, 0)
                    cp('act', xp[q][:, 0:3], pp, r=[pptk], w=[('xp', q)])
                else:
                    memset('pool', xp[q][:, 0:3], 0.0, w=[('xp', q)])
                ts('dve', uu[q], xp[q][:, 3:515], pc(pb_ + 3), pc(pb_ + 4), ALU.mult, ALU.add, r=[('xp', q), 'pcol'],
                   w=[('uu', q)])
                for kx in range(3):
                    stt(uu[q], xp[q][:, kx:kx + 512], pc(pb_ + kx), uu[q], ALU.mult, ALU.add,
                        r=[('xp', q), 'pcol', ('uu', q)], w=[('uu', q)])
            for qo in range(2):
                jt = 2 * h2 + qo
                pb_ = 40 + 8 * jt
                pr, prt = pq(4, 0, 4)
                pi, pit = pq(5, 0, 4)
                for q in range(2):
                    mm(pr, lwv[:, 0, q, qo * 128:(qo + 1) * 128], uu[q], q == 0, q == 1, r=['lw', ('uu', q)], w=prt)
                for q in range(2):
                    mm(pi, lwv[:, 1, q, qo * 128:(qo + 1) * 128], uu[q], q == 0, q == 1, r=['lw', ('uu', q)], w=pit)
                act(R_, pr, AF.Sigmoid, r=prt + ['pcol'], w=['R_'], bias=pc(pb_ + 5))
                act(I_, pi, AF.Sigmoid, r=pit + ['pcol'], w=['I_'], bias=pc(pb_ + 6))
                act(A_, R_, AF.Exp, r=['R_', 'dcol'], w=['A_'], scale=dcol[:, 24 + jt:25 + jt])
                act(M_, R_, AF.Exp, r=['R_', 'dcol'], w=['M_'], scale=dcol[:, 28 + jt:29 + jt])
                ts('dve', M_, M_, -1.0, 1.0, ALU.mult, ALU.add, r=['M_'], w=['M_'])
                ts('dve', M_, M_, 0.0, None, ALU.max, None, r=['M_'], w=['M_'])
                act(M_, M_, AF.Sqrt, r=['M_'], w=['M_'])
                if tb == 0:
                    memset('pool', M_[:, 0:1], 1.0, w=['M_'])
                tt('dve', I_, I_, M_, ALU.mult, r=['I_', 'M_'], w=['I_'])
                tt('pool', I_, I_, uu[qo], ALU.mult, r=['I_', ('uu', qo)], w=['I_'])
                if tb == 0:
                    init = 0.0
                    rr = ['A_', 'I_']
                else:
                    cp('pool', hprev[:, qo:qo + 1], hh_[qo][:, 511:512], r=[('hh', qo)], w=[('hprev', qo)])
                    init = hprev[:, qo:qo + 1]
                    rr = ['A_', 'I_', ('hprev', qo)]
                tr.op('dve', lambda e, o=hh_[qo], i0=init: e.tensor_tensor_scan(o, A_, I_, i0, ALU.mult, ALU.add),
                      rr, [('hh', qo)])
                p, ptk = proj(bv, tok, 256 + 128 * qo, 128, tb)
                cp('act', G1, p, r=[ptk], w=['G1'])
                act(G2, p, AF.Square, r=[ptk], w=['G2'])
                ts('dve', G2, G2, 0.044715, 1.0, ALU.mult, ALU.add, r=['G2'], w=['G2'])
                tt('pool', G2, G2, G1, ALU.mult, r=['G2', 'G1'], w=['G2'])
                act(G2, G2, AF.Sigmoid, r=['G2'], w=['G2'], scale=1.5957691216057308)
                tt('dve', G1, G1, G2, ALU.mult, r=['G1', 'G2'], w=['G1'])
                tt('dve', yaT, hh_[qo], G1, ALU.mult, r=[('hh', qo), 'G1'], w=['yaT'])
                ydma.append(dma(ysrc_l[tb][jt * 128:(jt + 1) * 128, :], yaT, r=['yaT'], w=[('ysrc', jt, tb)]))
                if ('yaT%d' % tb) in dbg_d and h2 == 0 and qo == 1:
                    cp('dve', G2, yaT, r=['yaT'], w=['G2'])
                    dump('yaT%d' % tb, G2, ['G2'])
    stage_end('mix')
    tr.fence()
    ar.release(mlr)
    ar.release(m_phase)

    for tb in range(4):
        toks_ = [('ysrc', a, tb) for a in range(8)]
        if os.environ.get('KDBG_NOCC'):
            dma(ydst_l[tb][0:1024, :], ysrc_l[tb], r=toks_, w=[('ydst', tb)], q='sp')
            dma(ydst_l[tb][1024:2048, :], ysrc_l[tb], r=toks_, w=[('ydst', tb)], q='sp')
        else:
            tr.op('pool', lambda e, a_=ysrc_l[tb], b_=ydst_l[tb]: e.collective_compute(
                "AllGather", ALU.bypass, replica_groups=[[0, 1], [2, 3], [4, 5], [6, 7]], ins=[a_], outs=[b_]),
                toks_, [('ydst', tb)], cc=True)
    x1 = ar.f32(8 * 2048)
    x1v = x1.rearrange("p (a n) -> p a n", a=8)
    vecA = ar.f32(2048)
    vecB = ar.f32(2048)
    vecC = ar.f32(2048)
    vecD = vecA
    tmpf = ar.f32(512)
    mtr = ar.mark()
    alloc_mb()
    mod_vec(2, vecA)
    mod_vec(4, vecB, gidx=1)
    mod_vec(3, vecC)
    tr.fence()
    ar.release(mtr)
    for a in range(8):
        dma(x1v[:, a, :], xown_d[a * 128:(a + 1) * 128, :], w=[('x1', a)])
    mys = ar.mark()
    ysel = ar.bf16(16 * 1024)
    yselv = ysel.rearrange("p (k t) -> p k t", k=16)
    tA = [ar.bf16(1024), ar.bf16(1024)]
    tB = [ar.bf16(1024), ar.bf16(1024)]
    for k in range(16):
        a_, b_ = tA[k % 2], tB[k % 2]
        for h_ in range(2):
            dma(a_[:, h_ * 512:(h_ + 1) * 512], ydst_l[h_][k * 128:(k + 1) * 128, :], r=[('ydst', h_)], w=[('tA', k % 2)])
            dma(b_[:, h_ * 512:(h_ + 1) * 512], ydst_l[2 + h_][k * 128:(k + 1) * 128, :], r=[('ydst', 2 + h_)],
                w=[('tB', k % 2)])
        act(a_, a_, AF.Copy, r=[('tA', k % 2), 'hsel'], w=[('tA', k % 2)], scale=hsel[:, 0:1])
        stt(yselv[:, k, :], b_, hsel[:, 1:2], a_, ALU.mult, ALU.add, r=[('tA', k % 2), ('tB', k % 2), 'hsel'],
            w=[('ysel', k)])
    ysel_all = [('ysel', k) for k in range(16)]
    if 'ysel' in dbg_d:
        tmpq = ar.f32(1024)
        cp('dve', tmpq, yselv[:, 5, :], r=ysel_all, w=['tmpq'])
        dump('ysel', tmpq, ['tmpq'])
        cp('dve', tmpq, yselv[:, 13, :], r=ysel_all + [('dbgout', 'ysel')], w=['tmpq'])
        dump('ysel2', tmpq, ['tmpq'])
    stage_end('xchg')
    wb2 = [ar.bf16(16 * 512), ar.bf16(16 * 512)]
    w2c = [0]

    def load_blk(src, off, nk, ncols, buf=None):
        if buf is None:
            buf = wb2[w2c[0] % 2]
            w2c[0] += 1
        tok_ = ('wb2', id(buf))
        load_flat(buf, src, off, nk * ncols, tok_)
        return buf[:, 0:nk * ncols].rearrange("p (k n) -> p k n", k=nk), tok_

    for cg in range(4):
        wv, wtok = load_blk(wout_d, cg * 8192, 16, 512)
        for a in range(8):
            bi = (cg * 8 + a) % 4
            for k in range(16):
                mm(ps[bi][:, :], yselv[:, k, a * 128:(a + 1) * 128], wv[:, k, :], k == 0, k == 15,
                   r=ysel_all + [wtok], w=[('ps', bi)])
            csl = slice(cg * 512, (cg + 1) * 512)
            tt('dve', tmpf, ps[bi][:, :], vecA[:, csl], ALU.mult, r=[('ps', bi), ('vec', id(vecA))], w=['tmpf'])
            tt('dve', x1v[:, a, csl], x1v[:, a, csl], tmpf, ALU.add, r=['tmpf', ('x1', a)], w=[('x1', a)])
    dump('x1a', x1v[:, 0, :], [('x1', 0)])
    dump('x1b', x1v[:, 7, :], [('x1', 7)])
    stage_end('wout')
    tr.fence()
    ar.release(mys)
    alloc_mb()
    mod_vec(5, vecD)
    tr.fence()
    ar.release(mys)

    h2T = ar.bf16(16 * 1024)
    h2Tv = h2T.rearrange("p (k t) -> p k t", k=16)
    actT = ar.bf16(4 * 1024)
    actv = actT.rearrange("p (f t) -> p f t", f=4)
    wg = [ar.bf16(16 * 256), ar.bf16(16 * 256)]
    wd = [ar.bf16(4 * 2048)] * 2
    hf2 = ar.f32(2048)
    hb2 = ar.bf16(2048)
    junk2 = hb2
    st2 = ar.f32(32)

    def rms_rstd(a, col):
        act(junk2, x1v[:, a, :], AF.Square, r=[('x1', a)], w=['hb2', ('st2', col)], accum_out=st2[:, col:col + 1])
        act(st2[:, col:col + 1], st2[:, col:col + 1], AF.Sqrt, r=[('st2', col), 'epsc'], w=[('st2', col)],
            bias=epsc[:, 0:1], scale=1.0 / D)
        tr.op('dve', lambda e, a_=st2[:, col:col + 1]: e.reciprocal(a_, a_), [('st2', col)], [('st2', col)])

    for a in range(8):
        rms_rstd(a, a)
        stt(hf2, x1v[:, a, :], st2[:, a:a + 1], vecB, ALU.mult, ALU.mult, r=[('x1', a), ('st2', a), ('vec', id(vecB))],
            w=['hf2'])
        tt('pool', hb2, hf2, vecC, ALU.add, r=['hf2', ('vec', id(vecC))], w=['hb2'])
        for half in range(2):
            bi = 4 + (a * 2 + half) % 4
            pbv = ps[bi][:, :].bitcast(BF16)
            for q in range(8):
                dc = half * 8 + q
                tp(pbv[:, q * 128:(q + 1) * 128], hb2[:, dc * 128:(dc + 1) * 128], identb, r=['hb2', 'identb'],
                   w=[('ps', bi)])
            cp('act' if half == 0 else 'dve', h2Tv[:, half * 8:(half + 1) * 8, a * 128:(a + 1) * 128],
               pbv.rearrange("p (q t) -> p q t", q=8), r=[('ps', bi)], w=[('h2T', a)])
    h2T_all = [('h2T', a) for a in range(8)]

    tmpg = ar.f32(512)
    tr.fence()
    wg = wg + [hf2.bitcast(BF16), vecB.bitcast(BF16), vecC.bitcast(BF16)]
    for fg in range(11):
        dv, dtok = load_blk(wdown_d, fg * 8192, 4, 2048, buf=wd[fg % 2])
        for ft in range(4):
            f = fg * 4 + ft
            gb = wg[f % len(wg)]
            gv_, gtok_ = load_blk(wgu_d, f * 4096, 16, 256, buf=gb)
            for tg in range(2):
                bg, bu = (tg * 2) % 4, (tg * 2 + 1) % 4
                for k in range(16):
                    mm(ps[bg][:, :], gv_[:, k, 0:128], h2Tv[:, k, tg * 512:(tg + 1) * 512], k == 0, k == 15,
                       r=[gtok_] + h2T_all, w=[('ps', bg)])
                for k in range(16):
                    mm(ps[bu][:, :], gv_[:, k, 128:256], h2Tv[:, k, tg * 512:(tg + 1) * 512], k == 0, k == 15,
                       r=[gtok_] + h2T_all, w=[('ps', bu)])
                act(tmpg, ps[bg][:, :], AF.Silu, r=[('ps', bg)], w=['tmpg'])
                tt('dve', actv[:, ft, tg * 512:(tg + 1) * 512], tmpg, ps[bu][:, :], ALU.mult, r=['tmpg', ('ps', bu)],
                   w=[('actT', ft)])
        for a in range(8):
            for cg in range(4):
                bi = 4 + (a * 4 + cg) % 4
                for ft in range(4):
                    mm(ps[bi][:, :], actv[:, ft, a * 128:(a + 1) * 128], dv[:, ft, cg * 512:(cg + 1) * 512], ft == 0,
                       ft == 3, r=[('actT', ft), dtok], w=[('ps', bi)])
                csl = slice(cg * 512, (cg + 1) * 512)
                tt('dve', tmpf, ps[bi][:, :], vecD[:, csl], ALU.mult, r=[('ps', bi), ('vec', id(vecD))], w=['tmpf'])
                tt('dve', x1v[:, a, csl], x1v[:, a, csl], tmpf, ALU.add, r=['tmpf', ('x1', a)], w=[('x1', a)])
    tr.fence()
    dma(vecB, gvec_d[2, :].partition_broadcast(128), w=[('vec', id(vecB))])
    outs = []
    for a in range(8):
        rms_rstd(a, 8 + a)
        stt(hf2, x1v[:, a, :], st2[:, 8 + a:9 + a], vecB, ALU.mult, ALU.mult,
            r=[('x1', a), ('st2', 8 + a), ('vec', id(vecB))], w=['hf2'])
        outs.append(dma(out_d[a * 128:(a + 1) * 128, :], hf2, r=['hf2'], w=[('out', a)], q='sp'))
    fin_deps = list(outs)
    for nm in dbg_d:
        pass
    blk = es.enter_context(nc.Block())
    tr.emit(nc, es, blk, fin_deps + tr.dbg_ops)
    es.close()
    return nc


def _perm_cols(hh):
    cols = []
    rw = 2048
    wl0, al0, gl0 = rw + 3072, rw + 3072 + 64, rw + 3072 + 128
    cols += list(range(wl0, wl0 + 64)) + list(range(al0, al0 + 64)) + list(range(gl0, gl0 + 160))
    for j in range(4):
        c = hh * 512 + j * 128
        for q in range(3):
            cols += list(range(rw + q * 1024 + c, rw + q * 1024 + c + 128))
    for h2 in range(2):
        c = hh * 512 + h2 * 256
        cols += list(range(c, c + 256)) + list(range(1024 + c, 1024 + c + 256))
    return np.array(cols)


_CACHE = {}


def kernel(x, c, w_ada, b_ada, norm_mix_g, w_in, conv_w, conv_b, lru_wa, lru_ba, lru_wx, lru_bx, lru_lambda,
           rwkv_mu, rwkv_w0, rwkv_w2, rwkv_a0, rwkv_a2, rwkv_g2, rwkv_k_k, rwkv_k_a, rwkv_r_k, rwkv_ln_g,
           rwkv_ln_b, w_out, norm_ffn_g, w_gu, w_down, final_norm_g, _dbg=None, _stage=None, _ncores=8):
    f = lambda a: np.ascontiguousarray(np.asarray(a, dtype=np.float32))
    x, c = f(x), f(c)
    if 'nc' not in _CACHE or _dbg is not None:
        _CACHE['nc'] = build(_dbg, _stage)
    nc = _CACHE['nc']
    ident = np.eye(128, dtype=np.float32)
    masks = np.zeros((128, 4, 128), np.float32)
    masks[:, 0] = np.triu(np.ones((128, 128)), 1)
    masks[:, 1] = np.triu(np.ones((128, 128)), 0)
    masks[:, 2] = np.tril(np.ones((128, 128)), -1)
    masks[0:64, 3, 0:64] = 1
    masks[64:128, 3, 64:128] = 1
    mu = f(rwkv_mu)[0]
    wout_perm = np.concatenate([np.arange(0, 512), np.arange(1024, 1536), np.arange(512, 1024), np.arange(1536, 2048)])
    wout_p = f(w_out)[0][wout_perm]
    wgu_ = f(w_gu)[0]
    g_ = wgu_[:, 0:DFF].reshape(16, 128, 44, 128)
    u_ = wgu_[:, DFF:2 * DFF].reshape(16, 128, 44, 128)
    wgu = np.ascontiguousarray(np.concatenate([g_, u_], axis=3).transpose(1, 2, 0, 3).reshape(128, 44 * 4096))
    wdown = np.ascontiguousarray(f(w_down)[0].reshape(11, 4, 128, 2048).transpose(2, 0, 1, 3).reshape(128, 11 * 8192))
    wout_p = np.ascontiguousarray(wout_p.reshape(16, 128, 4, 512).transpose(1, 2, 0, 3).reshape(128, 4 * 8192))
    wada = f(w_ada)[0]
    bada = f(b_ada)[0]
    gvec = np.stack([f(norm_mix_g)[0], f(norm_ffn_g)[0], f(final_norm_g)])
    in_maps = []
    for r in range(8):
        b, hh = r // 2, r % 2
        cols = _perm_cols(hh)
        winc = f(w_in)[0][:, cols]
        blocks = []
        for (c0_, n_) in [(0, 288)] + [(288 + 384 * j_, 384) for j_ in range(4)] + [(1824 + 512 * h_, 512) for h_ in range(2)]:
            blocks.append(winc[:, c0_:c0_ + n_].reshape(16, 128, n_).transpose(1, 0, 2).reshape(128, 16 * n_))
        win = np.ascontiguousarray(np.concatenate(blocks, axis=1))
        pcol = np.zeros((128, 80), np.float32)
        rkm = np.zeros((128, 8), np.float32)
        for j in range(4):
            ch = hh * 512 + j * 128 + np.arange(128)
            pcol[:, 8 * j + 0] = mu[ch]
            pcol[:, 8 * j + 1] = mu[1024 + ch]
            pcol[:, 8 * j + 2] = mu[2048 + ch]
            pcol[:, 8 * j + 3] = f(rwkv_k_k)[0][ch]
            pcol[:, 8 * j + 4] = f(rwkv_k_a)[0][ch]
            pcol[:, 8 * j + 5] = f(rwkv_w0)[0][ch]
            pcol[:, 8 * j + 6] = f(rwkv_a0)[0][ch]
            rkf = f(rwkv_r_k)[0].reshape(-1)[ch]
            rkm[0:64, 2 * j] = rkf[0:64]
            rkm[64:128, 2 * j + 1] = rkf[64:128]
            pcol[:, 40 + 8 * j + 0:40 + 8 * j + 4] = f(conv_w)[0][:, ch].T
            pcol[:, 40 + 8 * j + 4] = f(conv_b)[0][ch]
            pcol[:, 40 + 8 * j + 5] = f(lru_ba)[0][ch]
            pcol[:, 40 + 8 * j + 6] = f(lru_bx)[0][ch]
            pcol[:, 40 + 8 * j + 7] = f(lru_lambda)[0][ch]
        pcol[:, 32] = mu[3072:3200]
        pcol[:, 33] = mu[3200:3328]
        pcol[0:32, 34] = mu[3328:3360]
        chs = hh * 512 + np.arange(512)
        rowv = np.stack([f(rwkv_ln_g)[0][chs], f(rwkv_ln_b)[0][chs]])
        wa2 = np.concatenate([f(rwkv_w2)[0][:, chs], f(rwkv_a2)[0][:, chs]], axis=0)
        g2 = np.ascontiguousarray(f(rwkv_g2)[0][:, chs])
        lruw = np.stack([f(lru_wa)[0][2 * hh:2 * hh + 2], f(lru_wx)[0][2 * hh:2 * hh + 2]])
        hsel = np.zeros((128, 2), np.float32)
        hsel[:, hh] = 1.0
        m = dict(x=x[b], xown=np.ascontiguousarray(x[b, hh * 1024:(hh + 1) * 1024]),
                 cb=np.ascontiguousarray(c[b].reshape(128, 16)), wada=wada, bada=bada, gvec=gvec, win=win, pcol=pcol,
                 rowv=np.ascontiguousarray(rowv), rkm=rkm, wa2=np.ascontiguousarray(wa2), g2=g2,
                 lruw=np.ascontiguousarray(lruw), wout=wout_p, wgu=wgu, wdown=wdown, ident=ident, masks=masks, hsel=hsel)
        in_maps.append(m)
    res = run_bass_kernel_spmd(nc, in_maps[:_ncores], core_ids=list(range(_ncores)))
    if _stage is not None:
        return None, res
    out = np.zeros((4, 2048, 2048), np.float32)
    for r in range(8):
        b, hh = r // 2, r % 2
        out[b, hh * 1024:(hh + 1) * 1024] = res.results[r]["out"]
    if _dbg is not None:
        return out, res
    return out
```

```python
import os
import numpy as np
from contextlib import ExitStack
import concourse.bass as bass
import concourse.mybir as mybir
from concourse.bass_utils import run_bass_kernel_spmd

F32 = mybir.dt.float32
BF16 = mybir.dt.bfloat16
AF = mybir.ActivationFunctionType
ALU = mybir.AluOpType
AX = mybir.AxisListType

D = 2048
T = 2048
TO = 1024
DFF = 5632
NWIN = 2848
C0 = float(np.exp(-0.5))
RMS_EPS = 1e-6
GN_EPS = 64e-5
L2_EPS = 1e-12
NDMASEM = 6


class Tr:
    def __init__(self):
        self.ops = []
        self.lw = {}
        self.rd = {}
        self.fenced = 0
        self.dbg_ops = []

    def op(self, eng, fn, r=(), w=(), dma=False, cc=False):
        i = len(self.ops)
        deps = set()
        psr = [t for t in r if isinstance(t, tuple) and t and t[0] == 'ps']
        if psr:
            r = [t for t in r if t not in psr]
            w = list(w) + psr
        for t in r:
            x = self.lw.get(t)
            if x is not None:
                deps.add(x)
        for t in w:
            x = self.lw.get(t)
            if x is not None:
                deps.add(x)
            for y in self.rd.get(t, ()):
                deps.add(y)
        deps.discard(i)
        for t in r:
            self.rd.setdefault(t, []).append(i)
        for t in w:
            self.lw[t] = i
            self.rd[t] = []
        self.ops.append(dict(eng=eng, fn=fn, deps=deps, dma=dma, cc=cc))
        return i

    def fence(self):
        last = {}
        dmas = []
        for i, o in enumerate(self.ops):
            if o['dma'] or o['cc']:
                dmas.append(i)
            elif o['fn'] is not None:
                last[o['eng']] = i
        deps = set(last.values()) | set(dmas[self.fenced:])
        self.fenced = len(dmas)
        for e in ['pe', 'act', 'dve', 'pool', 'sp']:
            i = self.op(e, None, (), ())
            self.ops[i]['deps'] = set(deps)

    def emit(self, nc, es, block, final_deps):
        ops = self.ops
        engs = ['pe', 'act', 'dve', 'pool', 'sp']
        csem = {e: es.enter_context(nc.semaphore("c_" + e)) for e in engs}
        dsem = {e: [es.enter_context(nc.semaphore("d_%s%d" % (e, k))) for k in range(NDMASEM)]
                for e in ['act', 'pool', 'sp']}
        ccsem = es.enter_context(nc.semaphore("ccs"))
        fin = self.op('sp', None, (), ())
        ops[fin]['deps'] = set(final_deps)
        used = set()
        for o in ops:
            for d in o['deps']:
                used.add(d)
        ccnt = {e: 0 for e in engs}
        dcnt = {e: 0 for e in dsem}
        dval = {}
        cccount = 0
        for i, o in enumerate(ops):
            e = o['eng']
            if o['cc']:
                cccount += 1
                o['sem'] = ccsem
                o['val'] = cccount
            elif o['dma']:
                k = dcnt[e] % NDMASEM
                dcnt[e] += 1
                s = dsem[e][k]
                prev = dval.get((e, k), 0)
                o['sem'] = s
                o['prev'] = prev
                o['val'] = prev + 16
                dval[(e, k)] = prev + 16
            elif o['fn'] is not None and i in used:
                ccnt[e] += 1
                o['sem'] = csem[e]
                o['val'] = ccnt[e]
            else:
                o['sem'] = None
        per = {e: [] for e in engs}
        for i, o in enumerate(ops):
            per[o['eng']].append(i)

        def run(e, eng):
            waited = {}

            def wait(sem, val):
                key = sem.num if hasattr(sem, 'num') else id(sem)
                if waited.get(key, 0) >= val:
                    return
                eng.wait_ge(sem, val)
                waited[key] = val

            for i in per[e]:
                o = ops[i]
                for d in sorted(o['deps']):
                    p = ops[d]
                    if p['sem'] is None:
                        continue
                    if p['eng'] == e and e == 'pe' and not p['dma'] and not p['cc']:
                        continue
                    wait(p['sem'], p['val'])
                if o['dma'] and not o['cc'] and o['prev'] > 0:
                    wait(o['sem'], o['prev'])
                if o['fn'] is None:
                    continue
                ins = o['fn'](eng)
                if o['sem'] is not None:
                    if o['cc']:
                        ins.then_inc(o['sem'], 1)
                    elif o['dma']:
                        ins.then_inc(o['sem'], 16)
                    else:
                        ins.then_inc(o['sem'], 1)

        @block.tensor
        def _(eng):
            run('pe', eng)

        @block.scalar
        def _(eng):
            run('act', eng)

        @block.vector
        def _(eng):
            run('dve', eng)

        @block.gpsimd
        def _(eng):
            run('pool', eng)

        @block.sync
        def _(eng):
            run('sp', eng)


class Arena:
    def __init__(self, ap_f32, nwords):
        self.ap = ap_f32
        self.n = nwords
        self.top = 0
        self.hi = 0

    def mark(self):
        return self.top

    def release(self, m):
        self.top = m

    def f32(self, cols, parts=128):
        a = self.top
        self.top += cols
        self.hi = max(self.hi, self.top)
        assert self.top <= self.n, ("SBUF arena overflow", self.top, self.n)
        return self.ap[0:parts, a:a + cols]

    def bf16(self, cols, parts=128):
        w = (cols + 1) // 2
        a = self.top
        self.top += w
        self.hi = max(self.hi, self.top)
        assert self.top <= self.n, ("SBUF arena overflow", self.top, self.n)
        return self.ap[0:parts, a:a + w].bitcast(BF16)[:, 0:cols]


class _Done(Exception):
    def __init__(self, nc):
        self.nc = nc


def build(dbg=None, stage=None):
    try:
        return build_nc(dbg, stage)
    except _Done as d:
        return d.nc


def build_nc(dbg=None, stage=None):
    nc = bass.Bass("TRN2", target_bir_lowering=False)
    es = ExitStack()
    tr = Tr()

    def stage_end(name):
        if stage == name:
            blk_ = es.enter_context(nc.Block())
            tr.emit(nc, es, blk_, list(tr.dbg_ops))
            es.close()
            raise _Done(nc)

    def din(name, shape, dt=F32):
        return nc.dram_tensor(name, list(shape), dt, kind="ExternalInput").ap()

    x_d = din("x", [T, D])
    xown_d = din("xown", [TO, D])
    cb_d = din("cb", [128, 16])
    wada_d = din("wada", [D, 6 * D])
    bada_d = din("bada", [6 * D])
    gvec_d = din("gvec", [3, D])
    win_d = din("win", [128, 16 * NWIN])
    pcol_d = din("pcol", [128, 80])
    rowv_d = din("rowv", [2, 512])
    rk_d = din("rkm", [128, 8])
    wa2_d = din("wa2", [128, 512])
    g2_d = din("g2", [160, 512])
    lruw_d = din("lruw", [2, 2, 256, 256])
    wout_d = din("wout", [128, 4 * 8192])
    wgu_d = din("wgu", [128, 44 * 4096])
    wdown_d = din("wdown", [128, 11 * 8192])
    ident_d = din("ident", [128, 128])
    masks_d = din("masks", [128, 4, 128])
    hsel_d = din("hsel", [128, 2])
    out_d = nc.dram_tensor("out", [TO, D], F32, kind="ExternalOutput").ap()
    ysrc_l = [nc.dram_tensor("ysrc%d" % i, [1024, 512], BF16, kind="Internal").ap() for i in range(4)]
    ydst_l = [nc.dram_tensor("ydst%d" % i, [2048, 512], BF16, kind="Internal").ap() for i in range(4)]
    dbg_d = {}
    if dbg:
        for nm, shp in dbg.items():
            dbg_d[nm] = nc.dram_tensor("dbg_" + nm, list(shp), F32, kind="ExternalOutput").ap()

    NW = 53000
    arena_t = es.enter_context(nc.sbuf_tensor("arena", [128, NW], F32))
    ar = Arena(arena_t, NW)
    ps = [es.enter_context(nc.psum_tensor("ps%d" % i, [128, 512], F32)) for i in range(8)]

    dmaq = ['sp', 'act']
    dctr = [0]

    def dma(out, in_, r=(), w=(), q=None, **kw):
        if q is None:
            q = dmaq[dctr[0] % len(dmaq)]
            dctr[0] += 1
        return tr.op(q, lambda e: e.dma_start(out=out, in_=in_, **kw), r, w, dma=True)

    def mm(out, lhsT, rhs, start, stop, r=(), w=()):
        return tr.op('pe', lambda e: e.matmul(out, lhsT, rhs, start=start, stop=stop), r, w)

    def tp(out, in_, idn, r=(), w=()):
        return tr.op('pe', lambda e: e.transpose(out, in_, idn), r, w)

    def act(out, in_, func, r=(), w=(), bias=None, scale=None, accum_out=None):
        kw = {}
        if bias is not None:
            kw['bias'] = bias
        if scale is not None:
            kw['scale'] = scale
        if accum_out is not None:
            kw['accum_out'] = accum_out
        return tr.op('act', lambda e: e.activation(out, in_, func, **kw), r, w)

    def ts(eng, out, in0, s1, s2, op0, op1, r=(), w=()):
        if op1 is None:
            return tr.op(eng, lambda e: e.tensor_scalar(out, in0, s1, None, op0), r, w)
        return tr.op(eng, lambda e: e.tensor_scalar(out, in0, s1, s2, op0, op1), r, w)

    def tt(eng, out, in0, in1, op, r=(), w=()):
        return tr.op(eng, lambda e: e.tensor_tensor(out, in0, in1, op), r, w)

    def stt(out, in0, sc, in1, op0, op1, r=(), w=()):
        return tr.op('dve', lambda e: e.scalar_tensor_tensor(out, in0, sc, in1, op0, op1), r, w)

    def cp(eng, out, in_, r=(), w=()):
        if eng == 'act':
            return tr.op('act', lambda e: e.activation(out, in_, AF.Copy), r, w)
        return tr.op(eng, lambda e: e.tensor_copy(out, in_), r, w)

    def memset(eng, ap, val, w=()):
        return tr.op(eng, lambda e: e.memset(ap, val), (), w)

    def dump(name, ap, r):
        if name in dbg_d:
            i_ = dma(dbg_d[name], ap, r=r, w=[('dbgout', name)], q='sp')
            tr.dbg_ops.append(i_)
            return i_
        return None

    ident = ar.f32(128)
    masks = ar.f32(512)
    pcol = ar.f32(80)
    rkm = ar.f32(8)
    hsel = ar.f32(2)
    ones = ar.f32(128)
    epsc = ar.f32(4)
    wa2 = ar.f32(512)
    g2a = ar.f32(512)
    g2b = ar.f32(512, 32)
    rowv = ar.f32(1024)
    cb = ar.f32(16)
    identb = ar.bf16(128)
    identb2 = identb
    dma(ident, ident_d, w=['ident'])
    dma(masks, masks_d.rearrange("p a b -> p (a b)"), w=['masks'])
    dma(pcol, pcol_d, w=['pcol'])
    dma(rkm, rk_d, w=['rkm'])
    dma(hsel, hsel_d, w=['hsel'])
    dma(wa2, wa2_d, w=['wa2'])
    dma(g2a, g2_d[0:128, :], w=['g2'])
    dma(g2b, g2_d[128:160, :], w=['g2'])
    dma(rowv, rowv_d.rearrange("a n -> (a n)").partition_broadcast(128), w=['rowv'])
    dma(cb, cb_d, w=['cb'])
    memset('pool', ones, 1.0, w=['ones'])
    memset('pool', epsc[:, 0:1], RMS_EPS, w=['epsc'])
    memset('pool', epsc[:, 1:2], GN_EPS, w=['epsc'])
    memset('pool', epsc[:, 2:3], 0.0, w=['epsc'])
    MUS = masks[:, 0:128]
    MUI = masks[:, 128:256]
    MLS = masks[:, 256:384]
    BLK = masks[:, 384:512]
    lng = rowv[:, 0:512]
    lnb = rowv[:, 512:1024]

    def pc(i):
        return pcol[:, i:i + 1]

    dcol = ar.f32(64)
    mu_idx = [8 * j + q for j in range(4) for q in range(3)] + [32, 33, 34]
    for n, mi in enumerate(mu_idx):
        ts('pool', dcol[:, n:n + 1], pc(mi), -1.0, 1.0, ALU.mult, ALU.add, r=['pcol'], w=['dcol'])

    def omm(mi):
        return dcol[:, mu_idx.index(mi):mu_idx.index(mi) + 1]

    for j in range(4):
        ts('pool', dcol[:, 16 + j:17 + j], pc(8 * j + 4), -1.0, 1.0, ALU.mult, ALU.add, r=['pcol'], w=['dcol'])
        act(dcol[:, 32 + j:33 + j], pc(40 + 8 * j + 7), AF.Exp, r=['pcol'], w=['dcol'], scale=-1.0)
        act(dcol[:, 36 + j:37 + j], dcol[:, 32 + j:33 + j], AF.Ln, r=['dcol'], w=['dcol'], bias=1.0)
        ts('pool', dcol[:, 24 + j:25 + j], dcol[:, 36 + j:37 + j], -8.0, None, ALU.mult, None, r=['dcol'], w=['dcol'])
        ts('pool', dcol[:, 28 + j:29 + j], dcol[:, 36 + j:37 + j], -16.0, None, ALU.mult, None, r=['dcol'], w=['dcol'])

    cact = ar.f32(16)
    act(cact, cb, AF.Silu, r=['cb'], w=['cact'])
    cbc = ar.f32(16 * 128)
    cp('dve', cbc.rearrange("p (k m) -> p k m", k=16), cact.unsqueeze(2).to_broadcast([128, 16, 128]),
       r=['cact'], w=['cbc'])
    m_phase = ar.mark()
    MB = {}

    def alloc_mb():
        MB['bad'] = ar.f32(2048)
        MB['gv'] = ar.f32(2048)
        MB['wst'] = [ar.f32(2 * 2048), ar.f32(2 * 2048)]
    wada_v = wada_d.rearrange("(p k) n -> p k n", k=16)

    def mod_chunk(jm, evac):
        nst = 0
        for kq in range(8):
            b = MB['wst'][nst % 2]
            nst += 1
            dma(b.rearrange("p (k n) -> p k n", k=2), wada_v[:, kq * 2:(kq + 1) * 2, jm * 2048:(jm + 1) * 2048],
                w=[('wst', id(b))])
            for kk in range(2):
                k = kq * 2 + kk
                for q in range(4):
                    mm(ps[q][:, :], cbc[:, k * 128:(k + 1) * 128], b[:, kk * 2048 + q * 512: kk * 2048 + (q + 1) * 512],
                       k == 0, k == 15, r=['cbc', ('wst', id(b))], w=[('ps', q)])
        for q in range(4):
            evac(q, ps[q][:, :])

    def mod_vec(jm, dst, gidx=None):
        bad = MB['bad']
        gv = MB['gv']
        dma(bad, bada_d[jm * 2048:(jm + 1) * 2048].partition_broadcast(128), w=['bad'])
        if gidx is not None:
            dma(gv, gvec_d[gidx, :].partition_broadcast(128), w=['gv'])

        def ev(q, p):
            sl = slice(q * 512, (q + 1) * 512)
            tt('dve', dst[:, sl], p, bad[:, sl], ALU.add, r=[('ps', q), 'bad'], w=[('vec', id(dst))])
            if gidx is not None:
                stt(dst[:, sl], dst[:, sl], 1.0, gv[:, sl], ALU.add, ALU.mult, r=[('vec', id(dst)), 'gv'],
                    w=[('vec', id(dst))])
        mod_chunk(jm, ev)

    hT = ar.f32(16 * 2048 // 2)
    hTb = hT.bitcast(BF16).rearrange("p (k t) -> p k t", k=16)
    p1 = ar.mark()
    alloc_mb()
    gm1 = ar.f32(2048)
    shm = ar.f32(2048)
    mod_vec(1, gm1, gidx=0)
    mod_vec(0, shm)
    xs = [ar.f32(2048), ar.f32(2048)]
    hf = ar.f32(2048)
    hb = ar.bf16(2048)
    junk = ar.bf16(2048)
    ssq = ar.f32(16)
    rst = ar.f32(16)
    cp('dve', identb, ident, r=['ident'], w=['identb', 'identb2'])
    for tt_i in range(16):
        xb = xs[tt_i % 2]
        xt = ('xs', tt_i % 2)
        dma(xb, x_d[tt_i * 128:(tt_i + 1) * 128, :], w=[xt], q='sp')
        act(junk, xb, AF.Square, r=[xt], w=['junk', ('ssq', tt_i)], accum_out=ssq[:, tt_i:tt_i + 1])
        act(rst[:, tt_i:tt_i + 1], ssq[:, tt_i:tt_i + 1], AF.Sqrt, r=[('ssq', tt_i), 'epsc'], w=[('rst', tt_i)],
            bias=epsc[:, 0:1], scale=1.0 / D)
        tr.op('dve', lambda e, a=rst[:, tt_i:tt_i + 1]: e.reciprocal(a, a), [('rst', tt_i)], [('rst', tt_i)])
        stt(hf, xb, rst[:, tt_i:tt_i + 1], gm1, ALU.mult, ALU.mult, r=[xt, ('rst', tt_i), ('vec', id(gm1))], w=['hf'])
        tt('pool', hb, hf, shm, ALU.add, r=['hf', ('vec', id(shm))], w=['hb'])
        for half in range(2):
            pb = ps[4 + (tt_i * 2 + half) % 4]
            pt = ('ps', 4 + (tt_i * 2 + half) % 4)
            pbv = pb[:, :].bitcast(BF16)
            for q in range(8):
                dc = half * 8 + q
                tp(pbv[:, q * 128:(q + 1) * 128], hb[:, dc * 128:(dc + 1) * 128], identb, r=['hb', 'identb'], w=[pt])
            cp('act' if half == 0 else 'dve', hTb[:, half * 8:(half + 1) * 8, tt_i * 128:(tt_i + 1) * 128],
               pbv.rearrange("p (q t) -> p q t", q=8), r=[pt], w=[('hT', tt_i)])
    hT_all = [('hT', i) for i in range(16)]
    if 'hT' in dbg_d:
        tmpd = ar.f32(2048)
        cp('dve', tmpd, hTb[:, 3, :], r=hT_all, w=['tmpd'])
        dump('hT', tmpd, ['tmpd'])
    dump('gm1', gm1, [('vec', id(gm1))])
    stage_end('p1')
    tr.fence()
    ar.release(p1)

    wbufs = [ar.bf16(16 * 512), ar.bf16(16 * 512)]
    wctr = [0]

    def load_flat(dst, src, off, n, tok, piece=2048):
        for a in range(0, n, piece):
            b_ = min(n, a + piece)
            dma(dst[:, a:b_], src[:, off + a: off + b_], w=[tok], q='pool', max_dma_last_dim=8192)

    def load_w(c0, ncols):
        b = wbufs[wctr[0] % 2]
        wctr[0] += 1
        bv = b[:, 0:16 * ncols].rearrange("p (k n) -> p k n", k=16)
        tok = ('wbuf', id(b))
        load_flat(b, win_d, 16 * c0, 16 * ncols, tok)
        return bv, tok

    psrr = [0]

    def proj(bv, tok, cofs, ncol, tg, tlen=512):
        bi = psrr[0] % 4
        psrr[0] += 1
        p = ps[bi]
        for k in range(16):
            mm(p[0:ncol, 0:tlen], bv[:, k, cofs:cofs + ncol], hTb[:, k, tg * 512: tg * 512 + tlen], k == 0, k == 15,
               r=[tok] + hT_all, w=[('ps', bi)])
        return p[0:ncol, 0:tlen], ('ps', bi)

    pmu = ar.f32(516)
    memset('pool', pmu[:, 0:1], 0.0, w=['pmu'])

    def lerp_evac(p, ptok, nrow, mui, dst, dtok, tg):
        if tg > 0:
            cp('pool', pmu[0:nrow, 0:1], pmu[0:nrow, 512:513], r=['pmu'], w=['pmu'])
        else:
            memset('pool', pmu[0:nrow, 0:1], 0.0, w=['pmu'])
        act(pmu[0:nrow, 1:513], p, AF.Copy, r=[ptok, 'pcol'], w=['pmu'], scale=pc(mui)[0:nrow, :])
        stt(dst[0:nrow, tg * 512:(tg + 1) * 512], p, omm(mui)[0:nrow, :], pmu[0:nrow, 0:512], ALU.mult, ALU.add,
            r=[ptok, 'pmu', 'dcol'], w=[dtok])

    lwin = ar.f32(2048)
    glB = ar.f32(2048)
    glC = ar.f32(2048, 32)
    bv, tok = load_w(0, 288)
    for tg in range(4):
        p, ptk = proj(bv, tok, 0, 128, tg)
        lerp_evac(p, ptk, 128, 32, lwin, ('lwin', tg), tg)
        act(lwin[0:64, tg * 512:(tg + 1) * 512], lwin[0:64, tg * 512:(tg + 1) * 512], AF.Tanh, r=[('lwin', tg)],
            w=[('lwin', tg)])
    for tg in range(4):
        p, ptk = proj(bv, tok, 128, 128, tg)
        lerp_evac(p, ptk, 128, 33, glB, ('glB', tg), tg)
        act(glB[:, tg * 512:(tg + 1) * 512], glB[:, tg * 512:(tg + 1) * 512], AF.Sigmoid, r=[('glB', tg)],
            w=[('glB', tg)])
    for tg in range(4):
        p, ptk = proj(bv, tok, 256, 32, tg)
        lerp_evac(p, ptk, 32, 34, glC, ('glC', tg), tg)
        act(glC[:, tg * 512:(tg + 1) * 512], glC[:, tg * 512:(tg + 1) * 512], AF.Sigmoid, r=[('glC', tg)],
            w=[('glC', tg)])
    dump('lwin', lwin, [('lwin', g) for g in range(4)])
    dump('glB', glB, [('glB', g) for g in range(4)])
    stage_end('lora')

    def pq(bank, q0, nq=1, rows=128):
        return ps[bank][0:rows, q0 * 128:(q0 + nq) * 128], [('ps', bank)]

    def proj_prev(bv, tok, cofs, ncol, tcol, width, bank, col):
        for k in range(16):
            mm(ps[bank][0:ncol, col:col + width], bv[:, k, cofs:cofs + ncol], hTb[:, k, tcol:tcol + width], k == 0,
               k == 15, r=[tok] + hT_all, w=[('ps', bank)])
        return ps[bank][0:ncol, col:col + width], ('ps', bank)

    mask4 = ar.f32(512)
    cp('pool', mask4[:, 0:128], MUS, r=['masks'], w=['mask4'])
    cp('pool', mask4[:, 128:256], MUI, r=['masks'], w=['mask4'])
    cp('pool', mask4[:, 256:384], MUS, r=['masks'], w=['mask4'])
    cp('pool', mask4[:, 384:512], MUI, r=['masks'], w=['mask4'])

    ydma = []
    mrw = ar.mark()
    Hst = [[ar.f32(64), ar.f32(64)], [ar.f32(64), ar.f32(64)]]
    btp = [ar.f32(512), ar.f32(512)]
    ktp = [ar.f32(512), ar.f32(512)]
    for z_ in btp + ktp:
        memset('pool', z_, 0.0, w=[('pad', id(z_))])
    hcnt = [0]
    pqs = [[ar.f32(384), ar.f32(384)] for _ in range(4)]
    amat = [ar.f32(512) for _ in range(4)]
    trn2 = [ar.f32(384), ar.f32(384)]
    yb2 = [ar.f32(128), ar.f32(128)]
    pend = []
    Xs = ar.f32(128)
    Us = ar.f32(128)
    gtok = ar.f32(512)
    s_sb = ar.f32(8)
    ybT = ar.bf16(512)
    yb_ = ar.f32(128)
    yc_ = ar.f32(128)
    sq_ = ar.f32(128)
    st_ = ar.f32(8)
    yfb = ar.bf16(128)
    dec = ar.f32(4)
    A = {}
    for nm in ['r', 'k', 'v', 'sg', 'L', 'a', 'kk', 'EL', 'tmp']:
        A[nm] = ar.f32(512)
    tk = lambda nm: ('rw', nm)
    for j in range(4):
        bv, tok = load_w(288 + 384 * j, 384)
        for par_ in range(2):
            for hd_ in range(2):
                memset('pool', Hst[par_][hd_], 0.0, w=[('H', par_)])
        for tb in range(4):
            tsl = slice(tb * 512, (tb + 1) * 512)
            for q, nm in enumerate(['r', 'k', 'v']):
                p, ptk = proj(bv, tok, 128 * q, 128, tb)
                if tb > 0:
                    pp, pptk = proj_prev(bv, tok, 128 * q, 128, tb * 512 - 1, 1, 7, 0)
                    act(pmu[:, 0:1], pp, AF.Copy, r=[pptk, 'pcol'], w=['pmu'], scale=pc(8 * j + q))
                else:
                    memset('pool', pmu[:, 0:1], 0.0, w=['pmu'])
                act(pmu[:, 1:513], p, AF.Copy, r=[ptk, 'pcol'], w=['pmu'], scale=pc(8 * j + q))
                stt(A[nm], p, omm(8 * j + q), pmu[:, 0:512], ALU.mult, ALU.add, r=[ptk, 'pmu', 'dcol'], w=[tk(nm)])
            pw, pwt = pq(4, 0, 4)
            mm(pw, wa2[0:64, j * 128:(j + 1) * 128], lwin[0:64, tsl], True, True, r=['wa2', ('lwin', tb)], w=pwt)
            act(A['sg'], pw, AF.Sigmoid, r=pwt + ['pcol'], w=[tk('sg')], bias=pc(8 * j + 5))
            pa, pat = pq(5, 0, 4)
            mm(pa, wa2[64:128, j * 128:(j + 1) * 128], lwin[64:128, tsl], True, True, r=['wa2', ('lwin', tb)], w=pat)
            act(A['a'], pa, AF.Sigmoid, r=pat + ['pcol'], w=[tk('a')], bias=pc(8 * j + 6))
            for c in range(4):
                cs = slice(c * 128, (c + 1) * 128)
                tr.op('dve', lambda e, o=A['L'][:, cs], d1=A['sg'][:, cs]: e.tensor_tensor_scan(
                    o, ones, d1, 0.0, ALU.mult, ALU.add), [tk('sg'), 'ones'], [tk('L')])
            tt('pool', A['tmp'], A['L'], A['sg'], ALU.subtract, r=[tk('L'), tk('sg')], w=[tk('tmp')])
            act(A['tmp'], A['tmp'], AF.Exp, r=[tk('tmp')], w=[tk('tmp')], scale=-C0)
            act(A['EL'], A['L'], AF.Exp, r=[tk('L')], w=[tk('EL')], scale=-C0)
            act(A['L'], A['L'], AF.Exp, r=[tk('L')], w=[tk('L')], scale=C0)
            ts('pool', A['kk'], A['k'], pc(8 * j + 3), None, ALU.mult, None, r=[tk('k'), 'pcol'], w=[tk('kk')])
            act(A['sg'], A['kk'], AF.Square, r=[tk('kk')], w=[tk('sg')])
            pn, pnt = pq(6, 0, 4)
            mm(pn, BLK, A['sg'], True, True, r=['masks', tk('sg')], w=pnt)
            act(A['sg'], pn, AF.Sqrt, r=pnt, w=[tk('sg')])
            ts('dve', A['sg'], A['sg'], L2_EPS, None, ALU.max, None, r=[tk('sg')], w=[tk('sg')])
            tr.op('dve', lambda e, a_=A['sg']: e.reciprocal(a_, a_), [tk('sg')], [tk('sg')])
            tt('dve', A['kk'], A['kk'], A['sg'], ALU.mult, r=[tk('kk'), tk('sg')], w=[tk('kk')])
            ts('dve', A['sg'], A['a'], pc(8 * j + 4), dcol[:, 16 + j:17 + j], ALU.mult, ALU.add,
               r=[tk('a'), 'pcol', 'dcol'], w=[tk('sg')])
            tt('dve', A['k'], A['k'], A['sg'], ALU.mult, r=[tk('k'), tk('sg')], w=[tk('k')])
            tt('pool', A['sg'], A['r'], A['k'], ALU.mult, r=[tk('r'), tk('k')], w=[tk('sg')])
            psS, psSt = pq(7, 1, 1)
            for c in range(4):
                mm(psS[:, 2 * c:2 * c + 2], A['sg'][:, c * 128:(c + 1) * 128], rkm[:, 2 * j:2 * j + 2], True, True,
                   r=[tk('sg'), 'rkm'], w=psSt)
            cp('act', s_sb, psS[:, 0:8], r=psSt, w=['s_sb'])
            tt('pool', A['a'], A['kk'], A['a'], ALU.mult, r=[tk('kk'), tk('a')], w=[tk('a')])
            tt('dve', A['r'], A['r'], A['EL'], ALU.mult, r=[tk('r'), tk('EL')], w=[tk('r')])
            tt('pool', A['k'], A['k'], A['L'], ALU.mult, r=[tk('k'), tk('L')], w=[tk('k')])
            tt('dve', A['a'], A['a'], A['L'], ALU.mult, r=[tk('a'), tk('L')], w=[tk('a')])
            stt(A['kk'], A['kk'], -1.0, A['tmp'], ALU.mult, ALU.mult, r=[tk('kk'), tk('tmp')], w=[tk('kk')])
            cp('pool', dec, A['EL'].rearrange("p (c t) -> p c t", c=4)[:, :, 127], r=[tk('EL')], w=['dec'])
            decb = dec.unsqueeze(2).to_broadcast([128, 4, 128])
            tt('dve', A['L'].rearrange("p (c t) -> p c t", c=4), A['a'].rearrange("p (c t) -> p c t", c=4), decb,
               ALU.mult, r=[tk('a'), 'dec', tk('L')], w=[tk('L')])
            tt('pool', A['tmp'].rearrange("p (c t) -> p c t", c=4), A['k'].rearrange("p (c t) -> p c t", c=4), decb,
               ALU.mult, r=[tk('k'), 'dec', tk('tmp')], w=[tk('tmp')])
            rt_, kt_, bt_, at_, Bd_, Kd_ = A['r'], A['k'], A['a'], A['kk'], A['L'], A['tmp']
            for hd_ in range(2):
                hs_ = slice(64 * hd_, 64 * hd_ + 64)
                cp('pool', btp[hd_][hs_, :], bt_[hs_, :], r=[tk('a')], w=[('pad', id(btp[hd_]))])
                cp('act', ktp[hd_][hs_, :], kt_[hs_, :], r=[tk('k')], w=[('pad', id(ktp[hd_]))])
            if j == 0 and tb < 2:
                for nm_, ap_, t_ in [('rt', rt_, 'r'), ('kt', kt_, 'k'), ('bt', bt_, 'a'), ('at', at_, 'kk'),
                                     ('Bd', Bd_, 'L'), ('Kd', Kd_, 'tmp'), ('vv', A['v'], 'v')]:
                    dump('%s%d' % (nm_, tb), ap_, [tk(t_)])
            if j == 0 and tb == 0:
                stage_end('rwA')
            for c in range(4):
                cs = slice(tb * 512 + c * 128, tb * 512 + (c + 1) * 128)
                pg, pgt = pq(4, c, 1)
                mm(pg, glB[:, cs], g2a[:, j * 128:(j + 1) * 128], True, False, r=[('glB', tb), 'g2'], w=pgt)
                mm(pg, glC[0:32, cs], g2b[0:32, j * 128:(j + 1) * 128], False, True, r=[('glC', tb), 'g2'], w=pgt)
            pg4, pg4t = pq(4, 0, 4)
            cp('act', gtok, pg4, r=pg4t, w=['gtok'])
            if j == 0 and tb == 0:
                stage_end('rwA1')
            for c in range(4):
                cs = slice(c * 128, (c + 1) * 128)
                ptr_, ptrt = pq(4, 0, 3)
                tp(ptr_[:, 0:128], A['v'][:, cs], ident, r=[tk('v'), 'ident'], w=ptrt)
                tp(ptr_[:, 128:256], Bd_[:, cs], ident, r=[tk('L'), 'ident'], w=ptrt)
                tp(ptr_[:, 256:384], Kd_[:, cs], ident, r=[tk('tmp'), 'ident'], w=ptrt)
                trn_c, trnt = trn2[c % 2], ('trn', c % 2)
                cp('act', trn_c, ptr_, r=ptrt, w=[trnt])
                vtok, BdT, KdT = trn_c[:, 0:128], trn_c[:, 128:256], trn_c[:, 256:384]
                if j == 0 and tb == 0 and c == 0:
                    stage_end('rwA2')
                if c % 2 == 0:
                    probs = [(cc_, hd_) for cc_ in (c, c + 1) for hd_ in range(2)]
                    PB = [5, 6, 0, 1]
                    for pi, (cc_, hd) in enumerate(probs):
                        ccs = slice(cc_ * 128, (cc_ + 1) * 128)
                        pA, pAt = pq(PB[pi], 0, 4)
                        bpt, kpt = ('pad', id(btp[hd])), ('pad', id(ktp[hd]))
                        mm(pA[:, 0:128], btp[hd][:, ccs], at_[:, ccs], True, True, r=[bpt, tk('kk')], w=pAt)
                        mm(pA[:, 128:256], btp[hd][:, ccs], rt_[:, ccs], True, True, r=[bpt, tk('r')], w=pAt)
                        mm(pA[:, 256:384], ktp[hd][:, ccs], at_[:, ccs], True, True, r=[kpt, tk('kk')], w=pAt)
                        mm(pA[:, 384:512], ktp[hd][:, ccs], rt_[:, ccs], True, True, r=[kpt, tk('r')], w=pAt)
                    pQa, pQt = pq(2, 0, 4)
                    for pi, (cc_, hd) in enumerate(probs):
                        ccs = slice(cc_ * 128, (cc_ + 1) * 128)
                        mm(pQa[:, pi * 128:(pi + 1) * 128], at_[:, ccs], btp[hd][:, ccs], True, True,
                           r=[('pad', id(btp[hd])), tk('kk')], w=pQt)
                    for pi in range(4):
                        pA, pAt = pq(PB[pi], 0, 4)
                        tt('dve', amat[pi], pA, mask4, ALU.mult, r=pAt + ['mask4'], w=[('amat', pi)])
                    for pi in range(4):
                        b0 = pqs[pi][0]
                        cp('act', b0[:, 0:128], amat[pi][:, 0:128], r=[('amat', pi)], w=[('pqs', pi, 0)])
                        tt('dve', b0[:, 128:256], pQa[:, pi * 128:(pi + 1) * 128], MLS, ALU.mult, r=pQt + ['masks'],
                           w=[('pqs', pi, 0)])
                        tt('pool', b0[:, 256:384], amat[pi][:, 0:128], ident, ALU.add, r=[('amat', pi), 'ident'],
                           w=[('pqs', pi, 0)])
                    for it in range(6):
                        for pi in range(4):
                            cur = pqs[pi][it % 2]
                            ct = ('pqs', pi, it % 2)
                            pI, pIt = pq(PB[pi], 0, 3)
                            mm(pI[:, 0:128], cur[:, 0:128], cur[:, 128:256], True, True, r=[ct], w=pIt)
                            if it < 5:
                                mm(pI[:, 128:256], cur[:, 128:256], cur[:, 0:128], True, True, r=[ct], w=pIt)
                        for pi in range(4):
                            nxt = pqs[pi][(it + 1) % 2]
                            nt = ('pqs', pi, (it + 1) % 2)
                            pI, pIt = pq(PB[pi], 0, 3)
                            n2 = 256 if it < 5 else 128
                            if it < 5:
                                cp('act', nxt[:, 128:256], pI[:, 0:128], r=pIt, w=[nt])
                                cp('act', nxt[:, 0:128], pI[:, 128:256], r=pIt, w=[nt])
                            else:
                                cp('act', nxt[:, 128:256], pI[:, 0:128], r=pIt, w=[nt])
                        for pi in range(4):
                            cur, nxt = pqs[pi][it % 2], pqs[pi][(it + 1) % 2]
                            ct, nt = ('pqs', pi, it % 2), ('pqs', pi, (it + 1) % 2)
                            pI, pIt = pq(PB[pi], 0, 3)
                            mm(pI[:, 256:384], nxt[:, 128:256], cur[:, 256:384], True, True, r=[ct, nt], w=pIt)
                        for pi in range(4):
                            cur, nxt = pqs[pi][it % 2], pqs[pi][(it + 1) % 2]
                            ct, nt = ('pqs', pi, it % 2), ('pqs', pi, (it + 1) % 2)
                            pI, pIt = pq(PB[pi], 0, 3)
                            tt('dve', nxt[:, 256:384], pI[:, 256:384], cur[:, 256:384], ALU.add, r=pIt + [ct], w=[nt])
                pi0 = 2 * (c % 2)
                if j == 0 and tb == 0 and c < 2:
                    dump('amat%d' % c, amat[pi0], [('amat', pi0)])
                    dump('Tt%d' % c, pqs[pi0][0][:, 256:384], [('pqs', pi0, 0)])
                if j == 0 and tb == 0 and c == 0:
                    stage_end('rwB')
                Tt = [pqs[pi0 + hd][0][:, 256:384] for hd in range(2)]
                Ttt = [('pqs', pi0 + hd, 0) for hd in range(2)]
                Hc = Hst[hcnt[0] % 2]
                Hn = Hst[(hcnt[0] + 1) % 2]
                Hcp = Hc
                Hct, Hnt = ('H', hcnt[0] % 2), ('H', (hcnt[0] + 1) % 2)
                hcnt[0] += 1
                pX, pXt = pq(4, 3, 1)
                for hd in range(2):
                    hs = slice(64 * hd, 64 * hd + 64)
                    vs = slice(64 * hd, 64 * hd + 64)
                    mm(pX[:, vs], at_[:, cs], Hc[hd], True, False, r=[tk('kk'), Hct], w=pXt)
                    mm(pX[:, vs], amat[pi0 + hd][:, 256:384], vtok[:, vs], False, True, r=[('amat', pi0 + hd), trnt], w=pXt)
                cp('act', Xs, pX, r=pXt, w=['Xs'])
                pU, pUt = pq(7, 0, 1)
                for hd in range(2):
                    vs = slice(64 * hd, 64 * hd + 64)
                    mm(pU[:, vs], Tt[hd], Xs[:, vs], True, True, r=[Ttt[hd], 'Xs'], w=pUt)
                cp('act', Us, pU, r=pUt, w=['Us'])
                if j == 0 and tb == 0 and c == 0:
                    dump('Xs0', Xs, ['Xs'])
                    dump('Us0', Us, ['Us'])
                    stage_end('rwB1')
                pY, pYt = pq(4, 0, 1)
                for hd in range(2):
                    hs = slice(64 * hd, 64 * hd + 64)
                    vs = slice(64 * hd, 64 * hd + 64)
                    mm(pY[:, vs], rt_[:, cs], Hc[hd], True, False, r=[tk('r'), Hct], w=pYt)
                    mm(pY[:, vs], amat[pi0 + hd][:, 128:256], Us[:, vs], False, False, r=[('amat', pi0 + hd), 'Us'], w=pYt)
                    mm(pY[:, vs], amat[pi0 + hd][:, 384:512], vtok[:, vs], False, True, r=[('amat', pi0 + hd), trnt], w=pYt)
                pH, pHt = pq(4, 1, 1)
                mm(pH, BdT, Us, True, False, r=[trnt, 'Us'], w=pHt)
                mm(pH, KdT, vtok, False, True, r=[trnt], w=pHt)
                for hd in range(2):
                    hs = slice(64 * hd, 64 * hd + 64)
                    stt(Hn[hd][hs, :], Hc[hd][hs, :], dec[hs, c:c + 1], pH[hs, 64 * hd:64 * hd + 64], ALU.mult, ALU.add,
                        r=[Hct, 'dec'] + pHt, w=[Hnt])
                if j == 0 and tb == 0 and c == 0:
                    dump('Hn0', Hn[0], [Hnt])
                    dump('Hm0', Hn[1], [Hnt])
                    stage_end('rwB2')
                ybuf, ybt = yb2[c % 2], ('yb', c % 2)
                cp('act', ybuf, pY, r=pYt, w=[ybt])

                def make_post(c=c, cs=cs, ybuf=ybuf, ybt=ybt, vtok=vtok, trnt=trnt, j=j):
                    def post():
                        y3 = ybuf.rearrange("p (h v) -> p h v", h=2)
                        for hd_ in range(2):
                            act(sq_[:, 64 * hd_:64 * hd_ + 64], ybuf[:, 64 * hd_:64 * hd_ + 64], AF.Copy, r=[ybt],
                                w=['sq', 'st'], accum_out=st_[:, hd_:hd_ + 1])
                        ts('dve', st_[:, 0:2], st_[:, 0:2], -1.0 / 64, None, ALU.mult, None, r=['st'], w=['st'])
                        tt('dve', yc_.rearrange("p (h v) -> p h v", h=2), y3,
                           st_[:, 0:2].unsqueeze(2).to_broadcast([128, 2, 64]), ALU.add, r=[ybt, 'st'], w=['yc'])
                        for hd_ in range(2):
                            act(sq_[:, 64 * hd_:64 * hd_ + 64], yc_[:, 64 * hd_:64 * hd_ + 64], AF.Square, r=['yc'],
                                w=['sq', 'st'], accum_out=st_[:, 2 + hd_:3 + hd_])
                        act(st_[:, 2:4], st_[:, 2:4], AF.Sqrt, r=['st', 'epsc'], w=['st'], bias=epsc[:, 1:2],
                            scale=1.0 / 64)
                        tr.op('dve', lambda e, a_=st_[:, 2:4]: e.reciprocal(a_, a_), ['st'], ['st'])
                        tt('dve', yc_.rearrange("p (h v) -> p h v", h=2), yc_.rearrange("p (h v) -> p h v", h=2),
                           st_[:, 2:4].unsqueeze(2).to_broadcast([128, 2, 64]), ALU.mult, r=['yc', 'st'], w=['yc'])
                        tt('pool', yc_, yc_, lng[:, j * 128:(j + 1) * 128], ALU.mult, r=['yc', 'rowv'], w=['yc'])
                        tt('pool', yc_, yc_, lnb[:, j * 128:(j + 1) * 128], ALU.add, r=['yc', 'rowv'], w=['yc'])
                        tt('dve', sq_.rearrange("p (h v) -> p h v", h=2), vtok.rearrange("p (h v) -> p h v", h=2),
                           s_sb[:, 2 * c:2 * c + 2].unsqueeze(2).to_broadcast([128, 2, 64]), ALU.mult,
                           r=[trnt, 's_sb'], w=['sq'])
                        tt('dve', yc_, yc_, sq_, ALU.add, r=['yc', 'sq'], w=['yc'])
                        tt('dve', yfb, yc_, gtok[:, cs], ALU.mult, r=['yc', 'gtok'], w=['yfb'])
                        pT, pTt = pq(7, 1, 1)
                        pTb = pT.bitcast(BF16)[:, 0:128]
                        tp(pTb, yfb, identb2, r=['yfb', 'identb2'], w=pTt)
                        cp('act', ybT[:, cs], pTb, r=pTt, w=['ybT'])
                    return post
                pend.append(make_post())
                if len(pend) > 1:
                    pend.pop(0)()
            while pend:
                pend.pop(0)()
            ydma.append(dma(ysrc_l[tb][512 + j * 128: 512 + (j + 1) * 128, :], ybT, r=['ybT'], w=[('ysrc', 4 + j, tb)]))
            if 'ybT%d' % tb in dbg_d and j == 0:
                tmpd2 = A['sg']
                cp('dve', tmpd2, ybT, r=['ybT'], w=[tk('sg')])
                dump('ybT%d' % tb, tmpd2, [tk('sg')])
            if j == 0 and tb == 1:
                stage_end('rwkv1')
    tr.fence()
    ar.release(mrw)

    mlr = ar.mark()
    lw = ar.f32(1024)
    lwv = lw.rearrange("p (g q n) -> p g q n", g=2, q=2)
    xp = [ar.f32(516), ar.f32(516)]
    uu = [ar.f32(512), ar.f32(512)]
    hh_ = [ar.f32(512), ar.f32(512)]
    hprev = ar.f32(2)
    R_ = ar.f32(512)
    I_ = ar.f32(512)
    A_ = ar.f32(512)
    M_ = ar.f32(512)
    G1 = ar.f32(512)
    G2 = ar.f32(512)
    yaT = ar.bf16(512)
    for h2 in range(2):
        bv, tok = load_w(288 + 1536 + 512 * h2, 512)
        for g in range(2):
            dma(lwv[:, g, :, :], lruw_d[g, h2, :, :].rearrange("(q p) n -> p q n", p=128), w=['lw'])
        for tb in range(4):
            tsl = slice(tb * 512, (tb + 1) * 512)
            for q in range(2):
                jt = 2 * h2 + q
                pb_ = 40 + 8 * jt
                p, ptk = proj(bv, tok, 128 * q, 128, tb)
                cp('act', xp[q][:, 3:515], p, r=[ptk], w=[('xp', q)])
                if tb > 0:
                    pp, pptk = proj_prev(bv, tok, 128 * q, 128, tb * 512 - 3, 3, 7, 0)
                    cp('act', xp[q][:, 0:3], pp, r=[pptk], w=[('xp', q)])
                else:
                    memset('pool', xp[q][:, 0:3], 0.0, w=[('xp', q)])
                ts('dve', uu[q], xp[q][:, 3:515], pc(pb_ + 3), pc(pb_ + 4), ALU.mult, ALU.add, r=[('xp', q), 'pcol'],
                   w=[('uu', q)])
                for kx in range(3):
                    stt(uu[q], xp[q][:, kx:kx + 512], pc(pb_ + kx), uu[q], ALU.mult, ALU.add,
                        r=[('xp', q), 'pcol', ('uu', q)], w=[('uu', q)])
            for qo in range(2):
                jt = 2 * h2 + qo
                pb_ = 40 + 8 * jt
                pr, prt = pq(4, 0, 4)
                pi, pit = pq(5, 0, 4)
                for q in range(2):
                    mm(pr, lwv[:, 0, q, qo * 128:(qo + 1) * 128], uu[q], q == 0, q == 1, r=['lw', ('uu', q)], w=prt)
                for q in range(2):
                    mm(pi, lwv[:, 1, q, qo * 128:(qo + 1) * 128], uu[q], q == 0, q == 1, r=['lw', ('uu', q)], w=pit)
                act(R_, pr, AF.Sigmoid, r=prt + ['pcol'], w=['R_'], bias=pc(pb_ + 5))
                act(I_, pi, AF.Sigmoid, r=pit + ['pcol'], w=['I_'], bias=pc(pb_ + 6))
                act(A_, R_, AF.Exp, r=['R_', 'dcol'], w=['A_'], scale=dcol[:, 24 + jt:25 + jt])
                act(M_, R_, AF.Exp, r=['R_', 'dcol'], w=['M_'], scale=dcol[:, 28 + jt:29 + jt])
                ts('dve', M_, M_, -1.0, 1.0, ALU.mult, ALU.add, r=['M_'], w=['M_'])
                ts('dve', M_, M_, 0.0, None, ALU.max, None, r=['M_'], w=['M_'])
                act(M_, M_, AF.Sqrt, r=['M_'], w=['M_'])
                if tb == 0:
                    memset('pool', M_[:, 0:1], 1.0, w=['M_'])
                tt('dve', I_, I_, M_, ALU.mult, r=['I_', 'M_'], w=['I_'])
                tt('pool', I_, I_, uu[qo], ALU.mult, r=['I_', ('uu', qo)], w=['I_'])
                if tb == 0:
                    init = 0.0
                    rr = ['A_', 'I_']
                else:
                    cp('pool', hprev[:, qo:qo + 1], hh_[qo][:, 511:512], r=[('hh', qo)], w=[('hprev', qo)])
                    init = hprev[:, qo:qo + 1]
                    rr = ['A_', 'I_', ('hprev', qo)]
                tr.op('dve', lambda e, o=hh_[qo], i0=init: e.tensor_tensor_scan(o, A_, I_, i0, ALU.mult, ALU.add),
                      rr, [('hh', qo)])
                p, ptk = proj(bv, tok, 256 + 128 * qo, 128, tb)
                cp('act', G1, p, r=[ptk], w=['G1'])
                act(G2, p, AF.Square, r=[ptk], w=['G2'])
                ts('dve', G2, G2, 0.044715, 1.0, ALU.mult, ALU.add, r=['G2'], w=['G2'])
                tt('pool', G2, G2, G1, ALU.mult, r=['G2', 'G1'], w=['G2'])
                act(G2, G2, AF.Sigmoid, r=['G2'], w=['G2'], scale=1.5957691216057308)
                tt('dve', G1, G1, G2, ALU.mult, r=['G1', 'G2'], w=['G1'])
                tt('dve', yaT, hh_[qo], G1, ALU.mult, r=[('hh', qo), 'G1'], w=['yaT'])
                ydma.append(dma(ysrc_l[tb][jt * 128:(jt + 1) * 128, :], yaT, r=['yaT'], w=[('ysrc', jt, tb)]))
                if ('yaT%d' % tb) in dbg_d and h2 == 0 and qo == 1:
                    cp('dve', G2, yaT, r=['yaT'], w=['G2'])
                    dump('yaT%d' % tb, G2, ['G2'])
    stage_end('mix')
    tr.fence()
    ar.release(mlr)
    ar.release(m_phase)

    for tb in range(4):
        toks_ = [('ysrc', a, tb) for a in range(8)]
        if os.environ.get('KDBG_NOCC'):
            dma(ydst_l[tb][0:1024, :], ysrc_l[tb], r=toks_, w=[('ydst', tb)], q='sp')
            dma(ydst_l[tb][1024:2048, :], ysrc_l[tb], r=toks_, w=[('ydst', tb)], q='sp')
        else:
            tr.op('pool', lambda e, a_=ysrc_l[tb], b_=ydst_l[tb]: e.collective_compute(
                "AllGather", ALU.bypass, replica_groups=[[0, 1], [2, 3], [4, 5], [6, 7]], ins=[a_], outs=[b_]),
                toks_, [('ydst', tb)], cc=True)
    x1 = ar.f32(8 * 2048)
    x1v = x1.rearrange("p (a n) -> p a n", a=8)
    vecA = ar.f32(2048)
    vecB = ar.f32(2048)
    vecC = ar.f32(2048)
    vecD = vecA
    tmpf = ar.f32(512)
    mtr = ar.mark()
    alloc_mb()
    mod_vec(2, vecA)
    mod_vec(4, vecB, gidx=1)
    mod_vec(3, vecC)
    tr.fence()
    ar.release(mtr)
    for a in range(8):
        dma(x1v[:, a, :], xown_d[a * 128:(a + 1) * 128, :], w=[('x1', a)])
    mys = ar.mark()
    ysel = ar.bf16(16 * 1024)
    yselv = ysel.rearrange("p (k t) -> p k t", k=16)
    tA = [ar.bf16(1024), ar.bf16(1024)]
    tB = [ar.bf16(1024), ar.bf16(1024)]
    for k in range(16):
        a_, b_ = tA[k % 2], tB[k % 2]
        for h_ in range(2):
            dma(a_[:, h_ * 512:(h_ + 1) * 512], ydst_l[h_][k * 128:(k + 1) * 128, :], r=[('ydst', h_)], w=[('tA', k % 2)])
            dma(b_[:, h_ * 512:(h_ + 1) * 512], ydst_l[2 + h_][k * 128:(k + 1) * 128, :], r=[('ydst', 2 + h_)],
                w=[('tB', k % 2)])
        act(a_, a_, AF.Copy, r=[('tA', k % 2), 'hsel'], w=[('tA', k % 2)], scale=hsel[:, 0:1])
        stt(yselv[:, k, :], b_, hsel[:, 1:2], a_, ALU.mult, ALU.add, r=[('tA', k % 2), ('tB', k % 2), 'hsel'],
            w=[('ysel', k)])
    ysel_all = [('ysel', k) for k in range(16)]
    if 'ysel' in dbg_d:
        tmpq = ar.f32(1024)
        cp('dve', tmpq, yselv[:, 5, :], r=ysel_all, w=['tmpq'])
        dump('ysel', tmpq, ['tmpq'])
        cp('dve', tmpq, yselv[:, 13, :], r=ysel_all + [('dbgout', 'ysel')], w=['tmpq'])
        dump('ysel2', tmpq, ['tmpq'])
    stage_end('xchg')
    wb2 = [ar.bf16(16 * 512), ar.bf16(16 * 512)]
    w2c = [0]

    def load_blk(src, off, nk, ncols, buf=None):
        if buf is None:
            buf = wb2[w2c[0] % 2]
            w2c[0] += 1
        tok_ = ('wb2', id(buf))
        load_flat(buf, src, off, nk * ncols, tok_)
        return buf[:, 0:nk * ncols].rearrange("p (k n) -> p k n", k=nk), tok_

    for cg in range(4):
        wv, wtok = load_blk(wout_d, cg * 8192, 16, 512)
        for a in range(8):
            bi = (cg * 8 + a) % 4
            for k in range(16):
                mm(ps[bi][:, :], yselv[:, k, a * 128:(a + 1) * 128], wv[:, k, :], k == 0, k == 15,
                   r=ysel_all + [wtok], w=[('ps', bi)])
            csl = slice(cg * 512, (cg + 1) * 512)
            tt('dve', tmpf, ps[bi][:, :], vecA[:, csl], ALU.mult, r=[('ps', bi), ('vec', id(vecA))], w=['tmpf'])
            tt('dve', x1v[:, a, csl], x1v[:, a, csl], tmpf, ALU.add, r=['tmpf', ('x1', a)], w=[('x1', a)])
    dump('x1a', x1v[:, 0, :], [('x1', 0)])
    dump('x1b', x1v[:, 7, :], [('x1', 7)])
    stage_end('wout')
    tr.fence()
    ar.release(mys)
    alloc_mb()
    mod_vec(5, vecD)
    tr.fence()
    ar.release(mys)

    h2T = ar.bf16(16 * 1024)
    h2Tv = h2T.rearrange("p (k t) -> p k t", k=16)
    actT = ar.bf16(4 * 1024)
    actv = actT.rearrange("p (f t) -> p f t", f=4)
    wg = [ar.bf16(16 * 256), ar.bf16(16 * 256)]
    wd = [ar.bf16(4 * 2048)] * 2
    hf2 = ar.f32(2048)
    hb2 = ar.bf16(2048)
    junk2 = hb2
    st2 = ar.f32(32)

    def rms_rstd(a, col):
        act(junk2, x1v[:, a, :], AF.Square, r=[('x1', a)], w=['hb2', ('st2', col)], accum_out=st2[:, col:col + 1])
        act(st2[:, col:col + 1], st2[:, col:col + 1], AF.Sqrt, r=[('st2', col), 'epsc'], w=[('st2', col)],
            bias=epsc[:, 0:1], scale=1.0 / D)
        tr.op('dve', lambda e, a_=st2[:, col:col + 1]: e.reciprocal(a_, a_), [('st2', col)], [('st2', col)])

    for a in range(8):
        rms_rstd(a, a)
        stt(hf2, x1v[:, a, :], st2[:, a:a + 1], vecB, ALU.mult, ALU.mult, r=[('x1', a), ('st2', a), ('vec', id(vecB))],
            w=['hf2'])
        tt('pool', hb2, hf2, vecC, ALU.add, r=['hf2', ('vec', id(vecC))], w=['hb2'])
        for half in range(2):
            bi = 4 + (a * 2 + half) % 4
            pbv = ps[bi][:, :].bitcast(BF16)
            for q in range(8):
                dc = half * 8 + q
                tp(pbv[:, q * 128:(q + 1) * 128], hb2[:, dc * 128:(dc + 1) * 128], identb, r=['hb2', 'identb'],
                   w=[('ps', bi)])
            cp('act' if half == 0 else 'dve', h2Tv[:, half * 8:(half + 1) * 8, a * 128:(a + 1) * 128],
               pbv.rearrange("p (q t) -> p q t", q=8), r=[('ps', bi)], w=[('h2T', a)])
    h2T_all = [('h2T', a) for a in range(8)]

    tmpg = ar.f32(512)
    tr.fence()
    wg = wg + [hf2.bitcast(BF16), vecB.bitcast(BF16), vecC.bitcast(BF16)]
    for fg in range(11):
        dv, dtok = load_blk(wdown_d, fg * 8192, 4, 2048, buf=wd[fg % 2])
        for ft in range(4):
            f = fg * 4 + ft
            gb = wg[f % len(wg)]
            gv_, gtok_ = load_blk(wgu_d, f * 4096, 16, 256, buf=gb)
            for tg in range(2):
                bg, bu = (tg * 2) % 4, (tg * 2 + 1) % 4
                for k in range(16):
                    mm(ps[bg][:, :], gv_[:, k, 0:128], h2Tv[:, k, tg * 512:(tg + 1) * 512], k == 0, k == 15,
                       r=[gtok_] + h2T_all, w=[('ps', bg)])
                for k in range(16):
                    mm(ps[bu][:, :], gv_[:, k, 128:256], h2Tv[:, k, tg * 512:(tg + 1) * 512], k == 0, k == 15,
                       r=[gtok_] + h2T_all, w=[('ps', bu)])
                act(tmpg, ps[bg][:, :], AF.Silu, r=[('ps', bg)], w=['tmpg'])
                tt('dve', actv[:, ft, tg * 512:(tg + 1) * 512], tmpg, ps[bu][:, :], ALU.mult, r=['tmpg', ('ps', bu)],
                   w=[('actT', ft)])
        for a in range(8):
            for cg in range(4):
                bi = 4 + (a * 4 + cg) % 4
                for ft in range(4):
                    mm(ps[bi][:, :], actv[:, ft, a * 128:(a + 1) * 128], dv[:, ft, cg * 512:(cg + 1) * 512], ft == 0,
                       ft == 3, r=[('actT', ft), dtok], w=[('ps', bi)])
                csl = slice(cg * 512, (cg + 1) * 512)
                tt('dve', tmpf, ps[bi][:, :], vecD[:, csl], ALU.mult, r=[('ps', bi), ('vec', id(vecD))], w=['tmpf'])
                tt('dve', x1v[:, a, csl], x1v[:, a, csl], tmpf, ALU.add, r=['tmpf', ('x1', a)], w=[('x1', a)])
    tr.fence()
    dma(vecB, gvec_d[2, :].partition_broadcast(128), w=[('vec', id(vecB))])
    outs = []
    for a in range(8):
        rms_rstd(a, 8 + a)
        stt(hf2, x1v[:, a, :], st2[:, 8 + a:9 + a], vecB, ALU.mult, ALU.mult,
            r=[('x1', a), ('st2', 8 + a), ('vec', id(vecB))], w=['hf2'])
        outs.append(dma(out_d[a * 128:(a + 1) * 128, :], hf2, r=['hf2'], w=[('out', a)], q='sp'))
    fin_deps = list(outs)
    for nm in dbg_d:
        pass
    blk = es.enter_context(nc.Block())
    tr.emit(nc, es, blk, fin_deps + tr.dbg_ops)
    es.close()
    return nc


def _perm_cols(hh):
    cols = []
    rw = 2048
    wl0, al0, gl0 = rw + 3072, rw + 3072 + 64, rw + 3072 + 128
    cols += list(range(wl0, wl0 + 64)) + list(range(al0, al0 + 64)) + list(range(gl0, gl0 + 160))
    for j in range(4):
        c = hh * 512 + j * 128
        for q in range(3):
            cols += list(range(rw + q * 1024 + c, rw + q * 1024 + c + 128))
    for h2 in range(2):
        c = hh * 512 + h2 * 256
        cols += list(range(c, c + 256)) + list(range(1024 + c, 1024 + c + 256))
    return np.array(cols)


_CACHE = {}


def kernel(x, c, w_ada, b_ada, norm_mix_g, w_in, conv_w, conv_b, lru_wa, lru_ba, lru_wx, lru_bx, lru_lambda,
           rwkv_mu, rwkv_w0, rwkv_w2, rwkv_a0, rwkv_a2, rwkv_g2, rwkv_k_k, rwkv_k_a, rwkv_r_k, rwkv_ln_g,
           rwkv_ln_b, w_out, norm_ffn_g, w_gu, w_down, final_norm_g, _dbg=None, _stage=None, _ncores=8):
    f = lambda a: np.ascontiguousarray(np.asarray(a, dtype=np.float32))
    x, c = f(x), f(c)
    if 'nc' not in _CACHE or _dbg is not None:
        _CACHE['nc'] = build(_dbg, _stage)
    nc = _CACHE['nc']
    ident = np.eye(128, dtype=np.float32)
    masks = np.zeros((128, 4, 128), np.float32)
    masks[:, 0] = np.triu(np.ones((128, 128)), 1)
    masks[:, 1] = np.triu(np.ones((128, 128)), 0)
    masks[:, 2] = np.tril(np.ones((128, 128)), -1)
    masks[0:64, 3, 0:64] = 1
    masks[64:128, 3, 64:128] = 1
    mu = f(rwkv_mu)[0]
    wout_perm = np.concatenate([np.arange(0, 512), np.arange(1024, 1536), np.arange(512, 1024), np.arange(1536, 2048)])
    wout_p = f(w_out)[0][wout_perm]
    wgu_ = f(w_gu)[0]
    g_ = wgu_[:, 0:DFF].reshape(16, 128, 44, 128)
    u_ = wgu_[:, DFF:2 * DFF].reshape(16, 128, 44, 128)
    wgu = np.ascontiguousarray(np.concatenate([g_, u_], axis=3).transpose(1, 2, 0, 3).reshape(128, 44 * 4096))
    wdown = np.ascontiguousarray(f(w_down)[0].reshape(11, 4, 128, 2048).transpose(2, 0, 1, 3).reshape(128, 11 * 8192))
    wout_p = np.ascontiguousarray(wout_p.reshape(16, 128, 4, 512).transpose(1, 2, 0, 3).reshape(128, 4 * 8192))
    wada = f(w_ada)[0]
    bada = f(b_ada)[0]
    gvec = np.stack([f(norm_mix_g)[0], f(norm_ffn_g)[0], f(final_norm_g)])
    in_maps = []
    for r in range(8):
        b, hh = r // 2, r % 2
        cols = _perm_cols(hh)
        winc = f(w_in)[0][:, cols]
        blocks = []
        for (c0_, n_) in [(0, 288)] + [(288 + 384 * j_, 384) for j_ in range(4)] + [(1824 + 512 * h_, 512) for h_ in range(2)]:
            blocks.append(winc[:, c0_:c0_ + n_].reshape(16, 128, n_).transpose(1, 0, 2).reshape(128, 16 * n_))
        win = np.ascontiguousarray(np.concatenate(blocks, axis=1))
        pcol = np.zeros((128, 80), np.float32)
        rkm = np.zeros((128, 8), np.float32)
        for j in range(4):
            ch = hh * 512 + j * 128 + np.arange(128)
            pcol[:, 8 * j + 0] = mu[ch]
            pcol[:, 8 * j + 1] = mu[1024 + ch]
            pcol[:, 8 * j + 2] = mu[2048 + ch]
            pcol[:, 8 * j + 3] = f(rwkv_k_k)[0][ch]
            pcol[:, 8 * j + 4] = f(rwkv_k_a)[0][ch]
            pcol[:, 8 * j + 5] = f(rwkv_w0)[0][ch]
            pcol[:, 8 * j + 6] = f(rwkv_a0)[0][ch]
            rkf = f(rwkv_r_k)[0].reshape(-1)[ch]
            rkm[0:64, 2 * j] = rkf[0:64]
            rkm[64:128, 2 * j + 1] = rkf[64:128]
            pcol[:, 40 + 8 * j + 0:40 + 8 * j + 4] = f(conv_w)[0][:, ch].T
            pcol[:, 40 + 8 * j + 4] = f(conv_b)[0][ch]
            pcol[:, 40 + 8 * j + 5] = f(lru_ba)[0][ch]
            pcol[:, 40 + 8 * j + 6] = f(lru_bx)[0][ch]
            pcol[:, 40 + 8 * j + 7] = f(lru_lambda)[0][ch]
        pcol[:, 32] = mu[3072:3200]
        pcol[:, 33] = mu[3200:3328]
        pcol[0:32, 34] = mu[3328:3360]
        chs = hh * 512 + np.arange(512)
        rowv = np.stack([f(rwkv_ln_g)[0][chs], f(rwkv_ln_b)[0][chs]])
        wa2 = np.concatenate([f(rwkv_w2)[0][:, chs], f(rwkv_a2)[0][:, chs]], axis=0)
        g2 = np.ascontiguousarray(f(rwkv_g2)[0][:, chs])
        lruw = np.stack([f(lru_wa)[0][2 * hh:2 * hh + 2], f(lru_wx)[0][2 * hh:2 * hh + 2]])
        hsel = np.zeros((128, 2), np.float32)
        hsel[:, hh] = 1.0
        m = dict(x=x[b], xown=np.ascontiguousarray(x[b, hh * 1024:(hh + 1) * 1024]),
                 cb=np.ascontiguousarray(c[b].reshape(128, 16)), wada=wada, bada=bada, gvec=gvec, win=win, pcol=pcol,
                 rowv=np.ascontiguousarray(rowv), rkm=rkm, wa2=np.ascontiguousarray(wa2), g2=g2,
                 lruw=np.ascontiguousarray(lruw), wout=wout_p, wgu=wgu, wdown=wdown, ident=ident, masks=masks, hsel=hsel)
        in_maps.append(m)
    res = run_bass_kernel_spmd(nc, in_maps[:_ncores], core_ids=list(range(_ncores)))
    if _stage is not None:
        return None, res
    out = np.zeros((4, 2048, 2048), np.float32)
    for r in range(8):
        b, hh = r // 2, r % 2
        out[b, hh * 1024:(hh + 1) * 1024] = res.results[r]["out"]
    if _dbg is not None:
        return out, res
    return out
```
